# Optimizing a Trainium2 kernel written in Bass

```python
import math
import jax, jax.numpy as jnp
from jax import lax
import numpy as np

D_MODEL = 2048
BATCH = 1
SEQ = 8192
DEPTH = 2

GRID_W = 64
CTX_LEN = 256
D_MIX = D_MODEL
D_S5 = D_MIX // 2
D_CONV = D_MIX - D_S5
S5_GROUP = 16
S5_GROUPS = D_S5 // S5_GROUP
S5_STATE = 64
N_DIR = 2
CONV_WIDTH = 3
D_IN = 2 * D_S5 + 4 * D_CONV
DEEPNORM_ALPHA = (2.0 * DEPTH) ** 0.25
DEEPNORM_BETA = (8.0 * DEPTH) ** -0.25
LN_EPS = 1e-6
DT_MIN = 1e-3
DT_MAX = 1e-1

kernel_name = 'hybrid_s5_shortconv_deepnorm_dit'


def _layernorm(v):
    v32 = v.astype(jnp.float32)
    mu = jnp.mean(v32, axis=-1, keepdims=True)
    var = jnp.mean(jnp.square(v32 - mu), axis=-1, keepdims=True)
    return (v32 - mu) * lax.rsqrt(var + LN_EPS)


def _adaln(cond, w_ada, b_ada):
    mod = jax.nn.silu(cond) @ w_ada + b_ada
    return mod[..., :D_MODEL], mod[..., D_MODEL:2 * D_MODEL], mod[..., 2 * D_MODEL:]


def _modulate(v, shift, scale):
    return (_layernorm(v) * (1.0 + scale) + shift).astype(v.dtype)


def _split_proj(p):
    o1, o2, o3, o4, o5 = D_S5, 2 * D_S5, 2 * D_S5 + D_CONV, 2 * D_S5 + 2 * D_CONV, 2 * D_S5 + 3 * D_CONV
    return p[..., :o1], p[..., o1:o2], p[..., o2:o3], p[..., o3:o4], p[..., o4:o5], p[..., o5:]


def _s5_discretize(a_re, a_im, log_dt, b_re, b_im):
    lam = lax.complex(a_re.astype(jnp.float32), a_im.astype(jnp.float32))
    dt = jnp.exp(log_dt.astype(jnp.float32))[:, None]
    a_bar = jnp.exp(lam * dt)
    b = lax.complex(b_re.astype(jnp.float32), b_im.astype(jnp.float32))
    b_bar = ((a_bar - 1.0) / lam)[..., None] * b
    return a_bar, b_bar


def _scan_op(e1, e2):
    a1, b1 = e1
    a2, b2 = e2
    return a2 * a1, a2 * b1 + b2


def _s5_scan(a_bar, b_bar, u, h0, reverse):
    bu = jnp.einsum('gph,bngh->bngp', b_bar, u.astype(jnp.complex64))
    if h0 is not None:
        edge = -1 if reverse else 0
        bu = bu.at[:, edge].add(a_bar * h0)
    a = jnp.broadcast_to(a_bar, bu.shape)
    _, states = lax.associative_scan(_scan_op, (a, bu), axis=1, reverse=reverse)
    return states


def _s5_glu(y, w_glu, b_glu):
    g = jax.nn.gelu(y)
    return g * jax.nn.sigmoid(g @ w_glu.astype(jnp.float32) + b_glu.astype(jnp.float32))


def _s5_mixer(u_lat, u_ctx, a_re, a_im, log_dt, b_re, b_im, c_re, c_im, d_skip, w_glu, b_glu, with_ctx_out):
    bsz, n_lat, _ = u_lat.shape
    n_ctx = u_ctx.shape[1]
    ul = u_lat.astype(jnp.float32).reshape(bsz, n_lat, S5_GROUPS, S5_GROUP)
    uc = u_ctx.astype(jnp.float32).reshape(bsz, n_ctx, S5_GROUPS, S5_GROUP)
    d_g = d_skip.astype(jnp.float32).reshape(S5_GROUPS, S5_GROUP)
    y_lat = ul * d_g
    y_ctx = uc * d_g
    for d in range(N_DIR):
        reverse = d == 1
        a_bar, b_bar = _s5_discretize(a_re[d], a_im[d], log_dt[d], b_re[d], b_im[d])
        c_mat = lax.complex(c_re[d].astype(jnp.float32), c_im[d].astype(jnp.float32))
        h_ctx = _s5_scan(a_bar, b_bar, uc, None, reverse)
        h_last = h_ctx[:, 0] if reverse else h_ctx[:, -1]
        h_lat = _s5_scan(a_bar, b_bar, ul, h_last, reverse)
        y_lat = y_lat + jnp.einsum('ghp,bngp->bngh', c_mat, h_lat).real
        if with_ctx_out:
            y_ctx = y_ctx + jnp.einsum('ghp,bngp->bngh', c_mat, h_ctx).real
    out_lat = _s5_glu(y_lat.reshape(bsz, n_lat, D_S5), w_glu, b_glu)
    if not with_ctx_out:
        return out_lat, None
    return out_lat, _s5_glu(y_ctx.reshape(bsz, n_ctx, D_S5), w_glu, b_glu)


def _dwconv3(v, w, b):
    vp = jnp.pad(v, [(0, 0)] * (v.ndim - 2) + [(1, 1), (0, 0)])
    return vp[..., :-2, :] * w[0] + vp[..., 1:-1, :] * w[1] + vp[..., 2:, :] * w[2] + b


def _conv_branch(v, bg, cg, conv_w, conv_b, rows):
    s = cg * v
    if rows is None:
        conv = _dwconv3(s, conv_w, conv_b)
    else:
        bsz, n, ch = s.shape
        conv = _dwconv3(s.reshape(bsz, rows, GRID_W, ch), conv_w, conv_b).reshape(bsz, n, ch)
    return bg * conv


def _merge_out(y_s5, z_s5, y_cv, z_cv, w_out):
    o = jnp.concatenate([y_s5 * jax.nn.silu(z_s5), y_cv * jax.nn.silu(z_cv)], axis=-1)
    return o @ w_out


def _post_norm(res, sub, gate, ln_g, ln_b):
    v = DEEPNORM_ALPHA * res.astype(jnp.float32) + gate * sub.astype(jnp.float32)
    return (_layernorm(v) * ln_g + ln_b).astype(res.dtype)


def _layer(x, ctx, c, c_ctx, w_ada, b_ada, w_in, a_re, a_im, log_dt, b_re, b_im, c_re, c_im,
           d_skip, w_glu, b_glu, conv_w, conv_b, w_out, ln_g, ln_b, update_ctx):
    n_lat = x.shape[1]
    rows = n_lat // GRID_W
    shift, scale, gate = _adaln(c, w_ada, b_ada)
    shift_c, scale_c, gate_c = _adaln(c_ctx, w_ada, b_ada)
    h = _modulate(x, shift[:, None], scale[:, None])
    hc = _modulate(ctx, shift_c, scale_c)
    u, z_s5, v, bg, cg, z_cv = _split_proj(h @ w_in)
    if update_ctx:
        u_c, z_s5_c, v_c, bg_c, cg_c, z_cv_c = _split_proj(hc @ w_in)
    else:
        u_c = hc @ w_in[:, :D_S5]
    y_s5, y_s5_c = _s5_mixer(u, u_c, a_re, a_im, log_dt, b_re, b_im, c_re, c_im,
                             d_skip, w_glu, b_glu, update_ctx)
    y_cv = _conv_branch(v, bg, cg, conv_w, conv_b, rows)
    x_new = _post_norm(x, _merge_out(y_s5, z_s5, y_cv, z_cv, w_out), gate[:, None], ln_g, ln_b)
    if not update_ctx:
        return x_new, ctx
    y_cv_c = _conv_branch(v_c, bg_c, cg_c, conv_w, conv_b, None)
    ctx_new = _post_norm(ctx, _merge_out(y_s5_c, z_s5_c, y_cv_c, z_cv_c, w_out), gate_c, ln_g, ln_b)
    return x_new, ctx_new


def setup_inputs(seed: int = 0) -> dict:
    key = jax.random.key(seed)
    ks = jax.random.split(key, 24)
    f32 = jnp.float32

    def nrm(k, shape, s):
        return s * jax.random.normal(k, shape, f32)

    s5_a = (DEPTH, N_DIR, S5_GROUPS, S5_STATE)
    s5_b = (DEPTH, N_DIR, S5_GROUPS, S5_STATE, S5_GROUP)
    s5_c = (DEPTH, N_DIR, S5_GROUPS, S5_GROUP, S5_STATE)
    n_idx = jnp.arange(S5_STATE, dtype=f32)
    return {
        'x': nrm(ks[0], (BATCH, SEQ, D_MODEL), 1.0),
        'c': nrm(ks[1], (BATCH, D_MODEL), 1.0),
        'ctx': nrm(ks[2], (BATCH, CTX_LEN, D_MODEL), 1.0),
        'c_ctx': nrm(ks[3], (D_MODEL,), 1.0),
        'w_ada': nrm(ks[4], (DEPTH, D_MODEL, 3 * D_MODEL), 0.5 * D_MODEL ** -0.5),
        'b_ada': nrm(ks[5], (DEPTH, 3 * D_MODEL), 0.02),
        'w_in': nrm(ks[6], (DEPTH, D_MODEL, D_IN), D_MODEL ** -0.5),
        's5_a_re': -0.5 + nrm(ks[7], s5_a, 0.01),
        's5_a_im': math.pi * n_idx + nrm(ks[8], s5_a, 0.01),
        's5_log_dt': jax.random.uniform(ks[9], (DEPTH, N_DIR, S5_GROUPS), f32,
                                        math.log(DT_MIN), math.log(DT_MAX)),
        's5_b_re': nrm(ks[10], s5_b, (2 * S5_GROUP) ** -0.5),
        's5_b_im': nrm(ks[11], s5_b, (2 * S5_GROUP) ** -0.5),
        's5_c_re': nrm(ks[12], s5_c, S5_STATE ** -0.5),
        's5_c_im': nrm(ks[13], s5_c, S5_STATE ** -0.5),
        's5_d': nrm(ks[14], (DEPTH, D_S5), 1.0),
        'w_glu': nrm(ks[15], (DEPTH, D_S5, D_S5), D_S5 ** -0.5),
        'b_glu': nrm(ks[16], (DEPTH, D_S5), 0.02),
        'conv_w': nrm(ks[17], (DEPTH, CONV_WIDTH, D_CONV), CONV_WIDTH ** -0.5),
        'conv_b': nrm(ks[18], (DEPTH, D_CONV), 0.02),
        'w_out': nrm(ks[19], (DEPTH, D_MIX, D_MODEL), DEEPNORM_BETA * D_MIX ** -0.5),
        'ln_g': 1.0 + nrm(ks[20], (DEPTH, D_MODEL), 0.02),
        'ln_b': nrm(ks[21], (DEPTH, D_MODEL), 0.02),
    }


def reference(x, c, ctx, c_ctx, w_ada, b_ada, w_in, s5_a_re, s5_a_im, s5_log_dt, s5_b_re, s5_b_im,
              s5_c_re, s5_c_im, s5_d, w_glu, b_glu, conv_w, conv_b, w_out, ln_g, ln_b):
    for i in range(DEPTH):
        x, ctx = _layer(x, ctx, c, c_ctx, w_ada[i], b_ada[i], w_in[i], s5_a_re[i], s5_a_im[i],
                        s5_log_dt[i], s5_b_re[i], s5_b_im[i], s5_c_re[i], s5_c_im[i], s5_d[i],
                        w_glu[i], b_glu[i], conv_w[i], conv_b[i], w_out[i], ln_g[i], ln_b[i],
                        i < DEPTH - 1)
    return x
```

```python
import contextlib
import numpy as np
import ml_dtypes
import concourse.bass as bass
import concourse.mybir as mybir
from concourse.bass_utils import run_bass_kernel_spmd

F32 = mybir.dt.float32
BF16 = mybir.dt.bfloat16
I32 = mybir.dt.int32
ALU = mybir.AluOpType
AF = mybir.ActivationFunctionType
AX = mybir.AxisListType

D = 2048
SEQ = 8192
CTX = 256
NC = 8
TL = SEQ // NC
TT = TL + CTX
DIN = 6144
DS5 = 1024
NG = 64
GP = 8
NCH = (SEQ + CTX) // 8
NB = NCH // 32
ALPHA = (2.0 * 2) ** 0.25
EPS = 1e-6
TWO_PI = 2.0 * np.pi
DEBUG_O = False


class Prog:
    ENG = ("pe", "act", "dve", "pool", "sp")

    def __init__(self, nc, stack):
        self.nc = nc
        self.stack = stack
        self.sem_stack = stack
        self.q = {e: [] for e in self.ENG}
        self.sem = {}
        self.cnt = {}
        self.seen = {e: {} for e in self.ENG}
        self.lastw = {}
        self.readers = {}
        for e in self.ENG:
            self._mksem(e)

    def _mksem(self, name):
        if name not in self.sem:
            self.sem[name] = self.sem_stack.enter_context(self.nc.semaphore("s_" + name))
            self.cnt[name] = 0

    pfx = ""

    def sb(self, name, shape, dtype):
        return self.stack.enter_context(self.nc.sbuf_tensor(self.pfx + name, list(shape), dtype))

    def ps(self, name, shape, dtype=F32):
        return self.stack.enter_context(self.nc.psum_tensor(name, list(shape), dtype))

    def _waits(self, eng, r, w):
        need = {}

        def add(tok, same_ok):
            if tok is None:
                return
            for s, v in tok.items():
                if s == eng and not same_ok:
                    continue
                if need.get(s, 0) < v:
                    need[s] = v

        for k in r:
            add(self.lastw.get(k), True)
        so = eng != "pe"
        for k in w:
            add(self.lastw.get(k), so)
            add(self.readers.get(k), so)
        out = []
        for s, v in need.items():
            if self.seen[eng].get(s, 0) < v:
                self.seen[eng][s] = v
                out.append((s, v))
        return out

    def _commit(self, tok, r, w):
        for k in w:
            self.lastw[k] = dict(tok)
            self.readers[k] = {}
        for k in r:
            d = self.readers.setdefault(k, {})
            for s, v in tok.items():
                if d.get(s, 0) < v:
                    d[s] = v

    def op(self, eng, fn, r=(), w=()):
        waits = self._waits(eng, r, w)
        self.cnt[eng] += 1
        tok = {eng: self.cnt[eng]}
        self.q[eng].append((waits, fn, eng, 1))
        self._commit(tok, r, w)

    def dma(self, q, out, in_, r=(), w=(), sem=None):
        if sem is None:
            self._nu = getattr(self, "_nu", 0) + 1
            sem = f"du{self._nu}"
        self._mksem(sem)
        waits = self._waits(q, r, w)
        self.cnt[sem] += 16
        tok = {sem: self.cnt[sem]}
        self.q[q].append((waits, lambda e, o=out, i=in_: e.dma_start(out=o, in_=i), sem, 16))
        self._commit(tok, r, w)

    def custom(self, q, fn, sem, inc, r=(), w=()):
        self._mksem(sem)
        waits = self._waits(q, r, w)
        self.cnt[sem] += inc
        self.q[q].append((waits, fn, sem, inc))
        self._commit({sem: self.cnt[sem]}, r, w)

    def wait_all(self, eng, keys):
        waits = self._waits(eng, keys, ())
        if waits:
            self.q[eng].append((waits, None, None, 0))

    def emit(self):
        nc = self.nc
        sem = self.sem
        q = self.q
        with nc.Block() as block:
            def run(e, lst):
                for waits, fn, s, n in lst:
                    for ws, wv in waits:
                        e.wait_ge(sem[ws], wv)
                    if fn is not None:
                        fn(e).then_inc(sem[s], n)

            @block.tensor
            def _(e):
                run(e, q["pe"])

            @block.scalar
            def _(e):
                run(e, q["act"])

            @block.vector
            def _(e):
                run(e, q["dve"])

            @block.gpsimd
            def _(e):
                run(e, q["pool"])

            @block.sync
            def _(e):
                run(e, q["sp"])


def _new_nc():
    return bass.Bass("TRN2", target_bir_lowering=False)


def _run(nc, in_maps):
    res = run_bass_kernel_spmd(nc, in_maps, core_ids=list(range(NC)))
    return res.results


ADA_N = 3 * D // NC


def build_ada():
    nc = _new_nc()
    cc = nc.dram_tensor("cc", [128, 16, 2], F32, kind="ExternalInput").ap()
    wa = nc.dram_tensor("wa", [2, 128, 16, ADA_N], F32, kind="ExternalInput").ap()
    ba = nc.dram_tensor("ba", [2, 2, ADA_N], F32, kind="ExternalInput").ap()
    mod = nc.dram_tensor("mod", [2, 2, ADA_N], F32, kind="ExternalOutput").ap()
    with contextlib.ExitStack() as st:
        p = Prog(nc, st)
        cct = p.sb("cct", [128, 16, 2], F32)
        sc = p.sb("sc", [128, 16, 2], BF16)
        sg = p.sb("sg", [128, 16, 2], F32)
        wt = [p.sb(f"wt{l}", [128, 16, ADA_N], BF16) for l in range(2)]
        bt = p.sb("bt", [2, 2, ADA_N], F32)
        ot = p.sb("ot", [2, 2, ADA_N], F32)
        pp = [p.ps(f"pp{i}", [2, 384]) for i in range(4)]
        p.dma("sp", cct[:], cc[:, :, :], w=["cct"])
        p.dma("sp", bt[:], ba.rearrange("l t n -> t l n"), w=["bt"])
        for l in range(2):
            for h in range(2):
                p.dma("pool", wt[l][:, 8 * h:8 * h + 8, :], wa[l, :, 8 * h:8 * h + 8, :], w=[f"wt{l}{h}"], sem=f"dw{l}{h}")
        p.op("act", lambda e: e.activation(out=sg[:], in_=cct[:], func=AF.Sigmoid), r=["cct"], w=["sg"])
        p.op("dve", lambda e: e.tensor_tensor(out=sc[:], in0=cct[:], in1=sg[:], op=ALU.mult), r=["cct", "sg"], w=["sc"])
        for l in range(2):
            for nb in range(2):
                ps = pp[l * 2 + nb]
                for kf in range(16):
                    p.op("pe", lambda e, ps=ps, kf=kf, l=l, nb=nb: e.matmul(
                        ps[:, :], lhsT=sc[:, kf, :], rhs=wt[l][:, kf, nb * 384:(nb + 1) * 384],
                        start=(kf == 0), stop=(kf == 15)),
                        r=["sc", f"wt{l}{kf // 8}"], w=[f"pp{l}{nb}"])
                p.op("dve", lambda e, ps=ps, l=l, nb=nb: e.tensor_tensor(
                    out=ot[:, l, nb * 384:(nb + 1) * 384], in0=ps[:, :], in1=bt[:, l, nb * 384:(nb + 1) * 384], op=ALU.add),
                    r=[f"pp{l}{nb}", "bt"], w=["ot"])
        p.dma("sp", mod.rearrange("l t n -> t l n"), ot[:], r=["ot"], w=["mod"], sem="dout")
        p.wait_all("sp", ["mod"])
        p.emit()
    return nc


def run_ada(c, c_ctx, w_ada, b_ada):
    nc = build_ada()
    cc = np.stack([c.reshape(D), c_ctx.reshape(D)], axis=-1)
    cc = np.ascontiguousarray(cc.reshape(16, 128, 2).transpose(1, 0, 2))
    wr = w_ada.reshape(2, 16, 128, NC, ADA_N)
    in_maps = []
    for k in range(NC):
        wa = np.ascontiguousarray(wr[:, :, :, k, :].transpose(0, 2, 1, 3))
        ba = np.ascontiguousarray(np.broadcast_to(b_ada.reshape(2, 1, NC, ADA_N)[:, :, k, :], (2, 2, ADA_N)))
        in_maps.append({"cc": cc, "wa": wa, "ba": ba})
    res = _run(nc, in_maps)
    mod = np.concatenate([r["mod"] for r in res], axis=-1)
    return mod


def _maybe_stack(cond):
    if cond:
        with contextlib.ExitStack() as s_:
            yield s_


def _barrier(p):
    snap = dict(p.cnt)
    for e in p.ENG:
        waits = []
        for s, v in snap.items():
            if s == "cc":
                continue
            if v > 0 and p.seen[e].get(s, 0) < v and s != e:
                p.seen[e][s] = v
                waits.append((s, v))
        if waits:
            p.q[e].append((waits, None, None, 0))


def build_tok(mode, with_ctx):
    ntok = TT if with_ctx else TL
    nt_all = TT // 128
    nc = _new_nc()
    xin = nc.dram_tensor("xin", [TT, D], F32, kind="ExternalInput").ap()
    ms = nc.dram_tensor("ms", [128, 16, 4], F32, kind="ExternalInput").ap()
    win = nc.dram_tensor("win", [16, 128, DIN], F32, kind="ExternalInput").ap()
    ident = nc.dram_tensor("ident", [128, 128], F32, kind="ExternalInput").ap()
    if mode == "A":
        uo = nc.dram_tensor("uo", [8, 128, TT], BF16, kind="ExternalOutput").ap()
    else:
        yin = nc.dram_tensor("yin", [8, 128, TT], BF16, kind="ExternalInput").ap()
        gbc = nc.dram_tensor("gbc", [128, 2, D], F32, kind="ExternalInput").ap()
        lnb = nc.dram_tensor("lnb", [128, 2, D], F32, kind="ExternalInput").ap()
        pv = nc.dram_tensor("pv", [128, 8, 5], F32, kind="ExternalInput").ap()
        wglu = nc.dram_tensor("wglu", [8, 128, DS5], F32, kind="ExternalInput").ap()
        wout = nc.dram_tensor("wout", [16, 128, D], F32, kind="ExternalInput").ap()
        xo = nc.dram_tensor("xo", [ntok, D], F32, kind="ExternalOutput").ap()
        if DEBUG_O:
            odbg = nc.dram_tensor("odbg", [16, 128, TT], BF16, kind="ExternalOutput").ap()

    with contextlib.ExitStack() as st:
        p = Prog(nc, st)
        idf = p.sb("idf", [128, 128], F32)
        idb = p.sb("idb", [128, 128], BF16)
        mst = p.sb("mst", [128, 16, 4], F32)
        hT = p.sb("hT", [128, 16, TT], BF16)
        acc = [p.ps(f"acc{i}", [128, 512]) for i in range(4)]
        tps = [p.ps(f"tp{i}", [128, 512], BF16) for i in range(2)]
        p.dma("sp", idf[:], ident[:, :], w=["idf"])
        p.dma("sp", mst[:], ms[:, :, :], w=["mst"])
        p.op("dve", lambda e: e.tensor_copy(out=idb[:], in_=idf[:]), r=["idf"], w=["idb"])
        for j in (1, 3):
            p.op("dve", lambda e, j=j: e.tensor_scalar(out=mst[:, :, j], in0=mst[:, :, j], scalar1=1.0, scalar2=None, op0=ALU.add),
                 r=["mst"], w=["mst"])

        F = dict(p=p, st=st, xin=xin, win=win, idb=idb, mst=mst, hT=hT, acc=acc, tps=tps, do_ln=True, standalone=True)
        if mode == "A":
            F.update(uo=uo)
        else:
            F.update(yin=yin, gbc=gbc, lnb=lnb, pv=pv, wglu=wglu, wout=wout, xo=xo, odbg=(odbg if DEBUG_O else None))
        _tok_body(mode, with_ctx, F)
        p.emit()
    return nc


def _tok_body(mode, with_ctx, F):
    ntok = TT if with_ctx else TL
    nt_all = TT // 128
    p = F["p"]; st = F["st"]; xin = F["xin"]; win = F["win"]; idb = F["idb"]; mst = F["mst"]; hT = F["hT"]
    acc = F["acc"]; tps = F["tps"]; standalone = F["standalone"]
    sfx = F.get("sfx", "")
    if True:
        with contextlib.ExitStack() as st1:
            p.stack = st1
            NBUF = 3
            xt = [p.sb(f"xt{i}", [128, D], F32) for i in range(NBUF)]
            nt_ln = nt_all if F["do_ln"] else 0
            xn = [p.sb(f"xn{i}", [128, D], BF16) for i in range(NBUF)]
            stt = [p.sb(f"stt{i}", [128, 4, 6], F32) for i in range(NBUF)]
            mv = [p.sb(f"mv{i}", [128, 2], F32) for i in range(NBUF)]
            rs = [p.sb(f"rs{i}", [128, 1], F32) for i in range(NBUF)]
            def ln_s1(t):
                b = t % NBUF
                p.dma("sp", xt[b][:], xin[t * 128:(t + 1) * 128, :], w=[f"xt{b}"], sem=f"dx{b}")
                for c4 in range(4):
                    p.op("dve", lambda e, b=b, c4=c4: e.bn_stats(out=stt[b][:, c4, :], in_=xt[b][:, c4 * 512:(c4 + 1) * 512]),
                         r=[f"xt{b}"], w=[f"stt{b}"])
                p.op("dve", lambda e, b=b: e.bn_aggr(out=mv[b][:], in_=stt[b][:].rearrange("p a s -> p (a s)")),
                     r=[f"stt{b}"], w=[f"mv{b}"])
                p.op("act", lambda e, b=b: e.activation(out=rs[b][:], in_=mv[b][:, 1:2], func=AF.Sqrt, bias=EPS, scale=1.0),
                     r=[f"mv{b}"], w=[f"rs{b}"])

            def ln_s2(t):
                b = t % NBUF
                p.op("dve", lambda e, b=b: e.reciprocal(out=rs[b][:], in_=rs[b][:]), r=[f"rs{b}"], w=[f"rs{b}"])
                p.op("dve", lambda e, b=b: e.tensor_scalar(out=xn[b][:], in0=xt[b][:], scalar1=mv[b][:, 0:1], scalar2=rs[b][:, 0:1],
                                                           op0=ALU.subtract, op1=ALU.mult),
                     r=[f"xt{b}", f"mv{b}", f"rs{b}"], w=[f"xn{b}"])

            def ln_s3(t):
                b = t % NBUF
                which = 0 if t < TL // 128 else 1
                for k4 in range(4):
                    tp = tps[k4 % 2]
                    for j in range(4):
                        kf = k4 * 4 + j
                        p.op("pe", lambda e, tp=tp, j=j, kf=kf, b=b: e.transpose(
                            out=tp[:, j * 128:(j + 1) * 128], in_=xn[b][:, kf * 128:(kf + 1) * 128], identity=idb[:]),
                            r=[f"xn{b}", "idb"], w=[f"tp{k4 % 2}"])
                    for j in range(4):
                        kf = k4 * 4 + j
                        if j % 2 == 0:
                            p.op("act", lambda e, tp=tp, j=j, kf=kf, t=t, which=which: e.activation(
                                out=hT[:, kf, t * 128:(t + 1) * 128], in_=tp[:, j * 128:(j + 1) * 128], func=AF.Identity,
                                scale=mst[:, kf, 2 * which + 1:2 * which + 2], bias=mst[:, kf, 2 * which:2 * which + 1]),
                                r=[f"tp{k4 % 2}", "mst"], w=[f"hT{t}"])
                        else:
                            p.op("dve", lambda e, tp=tp, j=j, kf=kf, t=t, which=which: e.tensor_scalar(
                                out=hT[:, kf, t * 128:(t + 1) * 128], in0=tp[:, j * 128:(j + 1) * 128],
                                scalar1=mst[:, kf, 2 * which + 1:2 * which + 2], scalar2=mst[:, kf, 2 * which:2 * which + 1],
                                op0=ALU.mult, op1=ALU.add),
                                r=[f"tp{k4 % 2}", "mst"], w=[f"hT{t}"])

            for step in range(nt_ln + 2 if nt_ln else 0):
                if step < nt_ln:
                    ln_s1(step)
                if 0 <= step - 1 < nt_ln:
                    ln_s2(step - 1)
                if 0 <= step - 2 < nt_ln:
                    ln_s3(step - 2)
            _barrier(p)
        p.stack = st

        blocks = [(0, 512), (512, 512)] + ([(1024, 256)] if True else [])
        hkeys = lambda t0, n: [f"hT{t}" for t in range(t0 // 128, (t0 + n) // 128)]

        wq = {"n": 0}

        def load_w(wb, col0, ncol=256):
            i = wq["n"] % len(wb)
            if F.get("hwcast"):
                j = wq["n"] % len(F["wst"])
                wq["n"] += 1
                wst = F["wst"]
                p.dma("sp", wst[j][:, :, 0:ncol], win[:, :, col0:col0 + ncol].rearrange("k p n -> p k n"), w=[f"wst{j}"], sem=f"dws{j}")
                p.op("act", lambda e, i=i, j=j: e.activation(out=wb[i][:, :, 0:ncol], in_=wst[j][:, :, 0:ncol], func=AF.Identity),
                     r=[f"wst{j}"], w=[f"wb{i}"])
                return i
            wq["n"] += 1
            p.dma("pool", wb[i][:, :, 0:ncol], win[:, :, col0:col0 + ncol].rearrange("k p n -> p k n"),
                  w=[f"wb{i}"], sem=f"dwb{i}")
            return i

        def proj(wbt, wkey, cl, t0, n, ps, pkey):
            for kf in range(16):
                p.op("pe", lambda e, kf=kf: e.matmul(ps[:, 0:n], lhsT=wbt[:, kf, cl * 128:(cl + 1) * 128],
                                                      rhs=hT[:, kf, t0:t0 + n], start=(kf == 0), stop=(kf == 15)),
                     r=[wkey] + hkeys(t0, n), w=[pkey])

        pa = {"n": 0}

        def next_acc():
            i = pa["n"] % 4
            pa["n"] += 1
            return acc[i], f"acc{i}"

        if mode == "A":
            with contextlib.ExitStack() as st2:
                p.stack = st2
                wb = [p.sb(f"wb{i}", [128, 16, 256], BF16) for i in range(2 if standalone else 3)]
                ut = p.sb("ut", [128, 8, TT], BF16)
                if not standalone:
                    F["utp"] = p.sb("utp", [128, 8, TT], BF16)
                pre = None
                for pair in range(4):
                    i = pre[pair] if pre is not None else load_w(wb, pair * 256)
                    for cl in range(2):
                        ch = pair * 2 + cl
                        for bi, (t0, n) in enumerate(blocks):
                            ps, pk = next_acc()
                            proj(wb[i], f"wb{i}", cl, t0, n, ps, pk)
                            eng = "act" if (bi % 2 == 0) else "dve"
                            if eng == "act":
                                p.op("act", lambda e, ps=ps, ch=ch, t0=t0, n=n: e.activation(out=ut[:, ch, t0:t0 + n], in_=ps[:, 0:n], func=AF.Identity),
                                     r=[pk], w=[f"ut{ch}"])
                            else:
                                p.op("dve", lambda e, ps=ps, ch=ch, t0=t0, n=n: e.tensor_copy(out=ut[:, ch, t0:t0 + n], in_=ps[:, 0:n]),
                                     r=[pk], w=[f"ut{ch}"])
                        if standalone:
                            p.dma("sp", F["uo"][ch, :, :], ut[:, ch, :], r=[f"ut{ch}"], w=[f"uo{ch}"], sem="dout")
                        else:
                            utp = F["utp"]
                            p.op("act", lambda e, ch=ch: e.activation(out=utp[:, ch, 0:TL].rearrange("p (j c) -> p c j", j=8),
                                                                      in_=ut[:, ch, 0:TL].rearrange("p (c j) -> p c j", j=8), func=AF.Identity),
                                 r=[f"ut{ch}"], w=[f"utp{ch}"])
                            p.op("dve", lambda e, ch=ch: e.tensor_copy(out=utp[:, ch, TL:TT].rearrange("p (j c) -> p c j", j=8),
                                                                       in_=ut[:, ch, TL:TT].rearrange("p (c j) -> p c j", j=8)),
                                 r=[f"ut{ch}"], w=[f"utp{ch}"])
                            p.dma("sp", F["usend"][ch * 128:(ch + 1) * 128, :], utp[:, ch, 0:TL], r=[f"utp{ch}"], w=[f"usend{ch}"])
                            p.dma("act", F["uctx"][ch * 128:(ch + 1) * 128, :], utp[:, ch, TL:TT], r=[f"utp{ch}"], w=["uctx"])
                            if ch in (2, 5, 7):
                                F["cc_piece"]({2: 0, 5: 1, 7: 2}[ch])
                if standalone:
                    p.wait_all("sp", [f"uo{ch}" for ch in range(8)])
                _barrier(p)
            p.stack = st
            return

        part = F.get("part", 0)
        if "o" in F:
            o = F["o"]
            pvt = F["pvt"]
        else:
            o = p.sb("o", [128, 16, TT], BF16)
            pvt = p.sb("pvt", [128, 8, 5], F32)
            p.dma("sp", pvt[:], F["pv"][:, :, :], w=["pvt"])
        n1 = 0 if part == 2 else 1
        n2 = 0 if part == 1 else 1
        blk = blocks if with_ctx else blocks[:2]
        wo_early = None
        if part == 2:
            wo_early = p.sb("wo", [128, 16, D], BF16)
        with contextlib.ExitStack() as st2:
            p.stack = st2
            wb = [p.sb(f"wb{i}", [128, 16, 256], BF16) for i in range(F.get("nwb", 3) * n1)]
            if F.get("hwcast"):
                F["wst"] = [p.sb(f"wst{i}", [128, 16, 256], F32) for i in range(F.get("nwst", 2))]
            gel = p.sb("gel", [128, 8, TT], BF16) if n2 else None
            wg = p.sb("wg", [128, 8, DS5], BF16) if n2 else None
            vt = [p.sb(f"vt{i}", [128, 512], BF16) for i in range(2 * n1)]
            s_t = [p.sb(f"s_t{i}", [128, 512], F32) for i in range(2 * n1)]
            a_t = [p.sb(f"a_t{i}", [128, 512], F32) for i in range(2 * n1)]
            sz = [p.sb(f"sz{i}", [128, 512], BF16) for i in range(2 * n1)]
            if standalone:
                for ch in range(8):
                    p.dma("sp", gel[:, ch, :], F["yin"][ch, :, :], w=[f"gel{ch}"])
            elif n2:
                F["load_gel"](gel)
                if wo_early is not None:
                    for h in range(4):
                        p.dma("pool", wo_early[:, 4 * h:4 * h + 4, :], F["wout"][4 * h:4 * h + 4, :, :].rearrange("k p n -> p k n"), w=[f"wo{h}"], sem=f"dwo{h}")
            ytmp = [p.sb(f"ytmp{i}", [128, 512], F32) for i in range(2 * n2)]
            ysq = [p.sb(f"ysq{i}", [128, 512], F32) for i in range(2 * n2)]
            sgt = [p.sb(f"sgt{i}", [128, 512], BF16) for i in range(2 * n2)]
            for h in range(2 * n2):
                p.dma("pool", wg[:, 4 * h:4 * h + 4, :], F["wglu"][4 * h:4 * h + 4, :, :].rearrange("k p n -> p k n"), w=[f"wg{h}"], sem=f"dwg{h}")
            for pair in range(4 * n1):
                i = load_w(wb, DS5 + pair * 256)
                for cl in range(2):
                    ch = pair * 2 + cl
                    for (t0, n) in blk:
                        ps, pk = next_acc()
                        proj(wb[i], f"wb{i}", cl, t0, n, ps, pk)
                        b2 = pa["n"] % 2
                        p.op("act", lambda e, ps=ps, b2=b2, n=n: e.activation(out=sz[b2][:, 0:n], in_=ps[:, 0:n], func=AF.Sigmoid),
                             r=[pk], w=[f"sz{b2}"])
                        p.op("dve", lambda e, ps=ps, b2=b2, ch=ch, t0=t0, n=n: e.tensor_tensor(out=o[:, ch, t0:t0 + n], in0=ps[:, 0:n], in1=sz[b2][:, 0:n], op=ALU.mult),
                             r=[pk, f"sz{b2}"], w=[f"o{ch}"])
            if DEBUG_O == 2:
                for k in range(8):
                    p.dma("sp", F["odbg"][k, :, :], o[:, k, :], r=[f"o{k}"], w=[f"odbg{k}"])
            if n1 and "mid_hook" in F:
                F["mid_hook"]()
            it = 0
            for pair in range(4 * n1):
                iv = load_w(wb, 2 * DS5 + pair * 256)
                ic = load_w(wb, 2 * DS5 + 2 * 1024 + pair * 256)
                for cl in range(2):
                    k = pair * 2 + cl
                    if cl == 0:
                        pass
                    for (t0, n) in blk:
                        b2 = it % 2
                        it += 1
                        rl = 64 if t0 < TL else 256
                        ps, pk = next_acc()
                        proj(wb[iv], f"wb{iv}", cl, t0, n, ps, pk)
                        p.op("act", lambda e, ps=ps, b2=b2, n=n: e.activation(out=vt[b2][:, 0:n], in_=ps[:, 0:n], func=AF.Identity),
                             r=[pk], w=[f"vt{b2}"])
                        ps, pk = next_acc()
                        proj(wb[ic], f"wb{ic}", cl, t0, n, ps, pk)
                        p.op("dve", lambda e, ps=ps, b2=b2, n=n: e.tensor_tensor(out=s_t[b2][:, 0:n], in0=ps[:, 0:n], in1=vt[b2][:, 0:n], op=ALU.mult),
                             r=[pk, f"vt{b2}"], w=[f"s_t{b2}"])
                        p.op("dve", lambda e, b2=b2, n=n, k=k: e.tensor_scalar(out=a_t[b2][:, 0:n], in0=s_t[b2][:, 0:n], scalar1=pvt[:, k, 2:3], scalar2=pvt[:, k, 4:5],
                                                                              op0=ALU.mult, op1=ALU.add),
                             r=[f"s_t{b2}", "pvt"], w=[f"a_t{b2}"])
                        sv = s_t[b2][:, 0:n].rearrange("p (r c) -> p r c", c=rl)
                        av = a_t[b2][:, 0:n].rearrange("p (r c) -> p r c", c=rl)
                        p.op("dve", lambda e, sv=sv, av=av, k=k, rl=rl: e.scalar_tensor_tensor(
                            out=av[:, :, 1:rl], in0=sv[:, :, 0:rl - 1], scalar=pvt[:, k, 1:2], in1=av[:, :, 1:rl], op0=ALU.mult, op1=ALU.add),
                            r=[f"s_t{b2}", f"a_t{b2}", "pvt"], w=[f"a_t{b2}"])
                        p.op("dve", lambda e, sv=sv, av=av, k=k, rl=rl: e.scalar_tensor_tensor(
                            out=av[:, :, 0:rl - 1], in0=sv[:, :, 1:rl], scalar=pvt[:, k, 3:4], in1=av[:, :, 0:rl - 1], op0=ALU.mult, op1=ALU.add),
                            r=[f"s_t{b2}", f"a_t{b2}", "pvt"], w=[f"a_t{b2}"])
                        p.op("dve", lambda e, b2=b2, n=n, k=k, t0=t0: e.tensor_copy(out=o[:, 8 + k, t0:t0 + n], in_=a_t[b2][:, 0:n]),
                             r=[f"a_t{b2}"], w=[f"o{8 + k}"])
            it = 0
            for pair in range(4 * n1):
                ib = load_w(wb, 2 * DS5 + 1024 + pair * 256)
                iz = load_w(wb, 2 * DS5 + 3 * 1024 + pair * 256)
                for cl in range(2):
                    k = pair * 2 + cl
                    for (t0, n) in blk:
                        b2 = it % 2
                        it += 1
                        ps, pk = next_acc()
                        proj(wb[iz], f"wb{iz}", cl, t0, n, ps, pk)
                        p.op("act", lambda e, ps=ps, b2=b2, n=n: e.activation(out=sz[b2][:, 0:n], in_=ps[:, 0:n], func=AF.Sigmoid),
                             r=[pk], w=[f"sz{b2}"])
                        p.op("dve", lambda e, ps=ps, b2=b2, n=n: e.tensor_tensor(out=sz[b2][:, 0:n], in0=ps[:, 0:n], in1=sz[b2][:, 0:n], op=ALU.mult),
                             r=[pk, f"sz{b2}"], w=[f"sz{b2}"])
                        ps, pk = next_acc()
                        proj(wb[ib], f"wb{ib}", cl, t0, n, ps, pk)
                        p.op("dve", lambda e, ps=ps, b2=b2, n=n, k=k, t0=t0: e.tensor_tensor(out=vt[b2][:, 0:n], in0=ps[:, 0:n], in1=o[:, 8 + k, t0:t0 + n], op=ALU.mult),
                             r=[pk, f"o{8 + k}"], w=[f"vt{b2}"])
                        p.op("dve", lambda e, b2=b2, n=n, k=k, t0=t0: e.tensor_tensor(out=o[:, 8 + k, t0:t0 + n], in0=vt[b2][:, 0:n], in1=sz[b2][:, 0:n], op=ALU.mult),
                             r=[f"vt{b2}", f"sz{b2}"], w=[f"o{8 + k}"])
            it = 0
            for ch in range(8 * n2):
                for (t0, n) in blk:
                    b2 = it % 2
                    it += 1
                    g = gel[:, ch, t0:t0 + n]
                    p.op("act", lambda e, g=g, b2=b2, n=n: e.activation(out=ysq[b2][:, 0:n], in_=g, func=AF.Square),
                         r=[f"gel{ch}"], w=[f"ysq{b2}"])
                    p.op("dve", lambda e, b2=b2, n=n: e.tensor_scalar(out=ysq[b2][:, 0:n], in0=ysq[b2][:, 0:n], scalar1=0.044715, scalar2=1.0, op0=ALU.mult, op1=ALU.add),
                         r=[f"ysq{b2}"], w=[f"ysq{b2}"])
                    p.op("dve", lambda e, g=g, b2=b2, n=n: e.tensor_tensor(out=ytmp[b2][:, 0:n], in0=ysq[b2][:, 0:n], in1=g, op=ALU.mult),
                         r=[f"ysq{b2}", f"gel{ch}"], w=[f"ytmp{b2}"])
                    p.op("act", lambda e, b2=b2, n=n: e.activation(out=ytmp[b2][:, 0:n], in_=ytmp[b2][:, 0:n], func=AF.Sigmoid, scale=1.5957691216057308),
                         r=[f"ytmp{b2}"], w=[f"ytmp{b2}"])
                    p.op("dve", lambda e, g=g, b2=b2, n=n: e.tensor_tensor(out=g, in0=ytmp[b2][:, 0:n], in1=g, op=ALU.mult),
                         r=[f"ytmp{b2}", f"gel{ch}"], w=[f"gel{ch}"])
            it = 0
            for nch in range(8 * n2):
                for (t0, n) in blk:
                    b2 = it % 2
                    it += 1
                    ps, pk = next_acc()
                    for kf in range(8):
                        p.op("pe", lambda e, ps=ps, kf=kf, nch=nch, t0=t0, n=n: e.matmul(
                            ps[:, 0:n], lhsT=wg[:, kf, nch * 128:(nch + 1) * 128], rhs=gel[:, kf, t0:t0 + n], start=(kf == 0), stop=(kf == 7)),
                            r=[f"wg{kf // 4}", f"gel{kf}"], w=[pk])
                    p.op("act", lambda e, ps=ps, b2=b2, n=n, nch=nch: e.activation(out=sgt[b2][:, 0:n], in_=ps[:, 0:n], func=AF.Sigmoid, bias=pvt[:, nch, 0:1], scale=1.0),
                         r=[pk, "pvt"], w=[f"sgt{b2}"])
                    p.op("dve", lambda e, b2=b2, n=n, nch=nch, t0=t0: e.tensor_tensor(out=sgt[b2][:, 0:n], in0=sgt[b2][:, 0:n], in1=gel[:, nch, t0:t0 + n], op=ALU.mult),
                         r=[f"sgt{b2}", f"gel{nch}"], w=[f"sgt{b2}"])
                    p.op("dve", lambda e, b2=b2, n=n, nch=nch, t0=t0: e.tensor_tensor(out=o[:, nch, t0:t0 + n], in0=sgt[b2][:, 0:n], in1=o[:, nch, t0:t0 + n], op=ALU.mult),
                         r=[f"sgt{b2}", f"o{nch}"], w=[f"o{nch}"])
            if DEBUG_O:
                for k in range(16 if DEBUG_O == 1 else 0):
                    p.dma("sp", F["odbg"][k, :, :], o[:, k, :], r=[f"o{k}"], w=[f"odbg{k}"])
                p.wait_all("sp", [f"odbg{k}" for k in range(16 if DEBUG_O == 1 else 8)])
            if part == 1 and "tail_hook" in F:
                F["tail_hook"]()
            _barrier(p)
        p.stack = st
        if part == 1:
            return
        with contextlib.ExitStack() as st3:
            p.stack = st3
            wo = wo_early if wo_early is not None else p.sb("wo", [128, 16, D], BF16)
            gb = p.sb("gb", [128, 2, D], F32)
            lb = p.sb("lb", [128, 2, D], F32)
            xt2 = [p.sb("x2t0", [128, D], F32)] * 2
            vv = [p.sb(f"vv{i}", [128, D], F32) for i in range(2)]
            stt2 = [p.sb(f"st2{i}", [128, 4, 6], F32) for i in range(2)]
            mv2 = [p.sb(f"mv2{i}", [128, 2], F32) for i in range(2)]
            rs2 = [p.sb(f"rs2{i}", [128, 1], F32) for i in range(2)]
            for h in range(4 if wo_early is None else 0):
                p.dma("pool", wo[:, 4 * h:4 * h + 4, :], F["wout"][4 * h:4 * h + 4, :, :].rearrange("k p n -> p k n"), w=[f"wo{h}"], sem=f"dwo{h}")
            if standalone:
                p.dma("sp", gb[:], F["gbc"][:, :, :], w=["gb"])
            else:
                F["make_gb"](gb)
            p.dma("sp", lb[:], F["lnb"][:, :, :], w=["lb"])
            okeys = [f"o{k}" for k in range(16)]
            for t in range(ntok // 128):
                b = t % 2
                which = 0 if t < TL // 128 else 1
                p.dma("sp", xt2[b][:], xin[t * 128:(t + 1) * 128, :], w=["x2t"], sem="dx2")
                for nb in range(4):
                    ps, pk = next_acc()
                    for kf in range(16):
                        p.op("pe", lambda e, ps=ps, kf=kf, t=t, nb=nb: e.matmul(
                            ps[:, :], lhsT=o[:, kf, t * 128:(t + 1) * 128], rhs=wo[:, kf, nb * 512:(nb + 1) * 512], start=(kf == 0), stop=(kf == 15)),
                            r=[f"o{kf}", f"wo{kf // 4}"], w=[pk])
                    p.op("dve", lambda e, ps=ps, b=b, nb=nb, which=which: e.tensor_tensor(
                        out=vv[b][:, nb * 512:(nb + 1) * 512], in0=ps[:, :], in1=gb[:, which, nb * 512:(nb + 1) * 512], op=ALU.mult),
                        r=[pk, "gb"], w=[f"vv{b}"])
                    p.op("dve", lambda e, b=b, nb=nb: e.scalar_tensor_tensor(
                        out=vv[b][:, nb * 512:(nb + 1) * 512], in0=xt2[b][:, nb * 512:(nb + 1) * 512], scalar=ALPHA, in1=vv[b][:, nb * 512:(nb + 1) * 512],
                        op0=ALU.mult, op1=ALU.add),
                        r=["x2t", f"vv{b}"], w=[f"vv{b}"])
                    p.op("dve", lambda e, b=b, nb=nb: e.bn_stats(out=stt2[b][:, nb, :], in_=vv[b][:, nb * 512:(nb + 1) * 512]),
                         r=[f"vv{b}"], w=[f"st2{b}"])
                p.op("dve", lambda e, b=b: e.bn_aggr(out=mv2[b][:], in_=stt2[b][:].rearrange("p a s -> p (a s)")),
                     r=[f"st2{b}"], w=[f"mv2{b}"])
                p.op("act", lambda e, b=b: e.activation(out=rs2[b][:], in_=mv2[b][:, 1:2], func=AF.Sqrt, bias=EPS, scale=1.0),
                     r=[f"mv2{b}"], w=[f"rs2{b}"])
                p.op("dve", lambda e, b=b: e.reciprocal(out=rs2[b][:], in_=rs2[b][:]), r=[f"rs2{b}"], w=[f"rs2{b}"])
                p.op("dve", lambda e, b=b: e.tensor_scalar(out=vv[b][:], in0=vv[b][:], scalar1=mv2[b][:, 0:1], scalar2=rs2[b][:, 0:1],
                                                           op0=ALU.subtract, op1=ALU.mult),
                     r=[f"vv{b}", f"mv2{b}", f"rs2{b}"], w=[f"vv{b}"])
                p.op("dve", lambda e, b=b: e.tensor_tensor(out=vv[b][:], in0=vv[b][:], in1=lb[:, 0, :], op=ALU.mult),
                     r=[f"vv{b}", "lb"], w=[f"vv{b}"])
                p.op("dve", lambda e, b=b: e.tensor_tensor(out=vv[b][:], in0=vv[b][:], in1=lb[:, 1, :], op=ALU.add),
                     r=[f"vv{b}", "lb"], w=[f"vv{b}"])
                p.dma("sp", (F["xo"] if t < TL // 128 else F.get("xo_ctx", F["xo"]))[t * 128:(t + 1) * 128, :], vv[b][:], r=[f"vv{b}"], w=[f"xo{t}"], sem=f"do{b}")
            p.wait_all("sp", [f"xo{t}" for t in range(ntok // 128)])
            _barrier(p)
        p.stack = st


_IDENT = np.eye(128, dtype=np.float32)


def _ms_layout(mod_l):
    out = np.empty((128, 16, 4), np.float32)
    for w in range(2):
        out[:, :, 2 * w] = mod_l[w, 0:D].reshape(16, 128).T
        out[:, :, 2 * w + 1] = mod_l[w, D:2 * D].reshape(16, 128).T
    return out


def run_tok_A(xs, mod_l, w_in_l, with_ctx=True):
    nc = build_tok("A", True)
    ms = _ms_layout(mod_l)
    win = w_in_l.reshape(16, 128, DIN)
    in_maps = [{"xin": xs[k], "ms": ms, "win": win, "ident": _IDENT} for k in range(NC)]
    res = _run(nc, in_maps)
    return [r["uo"] for r in res]


def run_tok_C(xs, ys, mod_l, w_in_l, w_glu_l, b_glu_l, conv_w_l, conv_b_l, w_out_l, ln_g_l, ln_b_l, with_ctx):
    nc = build_tok("C", with_ctx)
    ms = _ms_layout(mod_l)
    win = w_in_l.reshape(16, 128, DIN)
    gbc = np.ascontiguousarray(np.broadcast_to(mod_l[:, 2 * D:3 * D][None, :, :], (128, 2, D)))
    lnb = np.ascontiguousarray(np.broadcast_to(np.stack([ln_g_l, ln_b_l])[None, :, :], (128, 2, D)))
    pv = np.empty((128, 8, 5), np.float32)
    pv[:, :, 0] = b_glu_l.reshape(8, 128).T
    for j in range(3):
        pv[:, :, 1 + j] = conv_w_l[j].reshape(8, 128).T
    pv[:, :, 4] = conv_b_l.reshape(8, 128).T
    wglu = w_glu_l.reshape(8, 128, DS5)
    wout = w_out_l.reshape(16, 128, D)
    in_maps = [{"xin": xs[k], "ms": ms, "win": win, "ident": _IDENT, "yin": ys[k], "gbc": gbc, "lnb": lnb,
                "pv": pv, "wglu": wglu, "wout": wout} for k in range(NC)]
    res = _run(nc, in_maps)
    if DEBUG_O:
        return [r["xo"] for r in res], [r["odbg"] for r in res]
    return [r["xo"] for r in res]


SEGS = [(0, 32), (32, 512), (544, 512)]


def build_s5(stage=9):
    nc = _new_nc()
    u2 = nc.dram_tensor("u2", [GP, 128, NCH], BF16, kind="ExternalInput").ap()
    par = nc.dram_tensor("par", [128, GP, 3], F32, kind="ExternalInput").ap()
    bri = nc.dram_tensor("bri", [128, GP, 16, 2], F32, kind="ExternalInput").ap()
    cri = nc.dram_tensor("cri", [128, GP, 16, 2], F32, kind="ExternalInput").ap()
    dcol = nc.dram_tensor("dcol", [128, GP], F32, kind="ExternalInput").ap()
    cst = nc.dram_tensor("cst", [128, 3, 128], F32, kind="ExternalInput").ap()
    y2 = nc.dram_tensor("y2", [GP, 128, NCH], BF16, kind="ExternalOutput").ap()

    with contextlib.ExitStack() as st:
        p = Prog(nc, st)
        pt = [p.ps(f"pt{i}", [128, 512]) for i in range(2)]
        psS = [p.ps(f"psS{i}", [128, 512]) for i in range(2)]
        psY = [p.ps(f"psY{i}", [128, 512]) for i in range(2)]
        F = dict(p=p, st=st, standalone=True, stage=stage, u2=u2, par=par, bri=bri, cri=cri, dcol=dcol, cst=cst, y2=y2,
                 pt=pt, psS=psS, psY=psY, ptk=["pt0", "pt1"], psSk=["psS0", "psS1"], psYk=["psY0", "psY1"])
        _s5_body(F)
        p.emit()
    return nc


def _s5_body(F):
    p = F["p"]; st = F["st"]; stage = F["stage"]; standalone = F["standalone"]
    par = F["par"]; bri = F["bri"]; cri = F["cri"]; dcol = F["dcol"]; cst = F["cst"]
    phase = F.get("phase", "all")
    if True:
        if phase != "main" and "wts" not in F:
            p.stack = F.get("wstack", st)
            BS = p.sb("BS", [128, GP, 2, 128], BF16)
            CAf = p.sb("CAf", [128, GP, 2, 128], BF16)
            CAb = p.sb("CAb", [128, GP, 2, 128], BF16)
            Tbf = p.sb("Tbf", [128, GP, 128], BF16)
            A1 = p.sb("A1", [128, GP, 2], F32)
            A2 = p.sb("A2", [128, GP, 2], F32)
            B1 = p.sb("B1", [128, GP, 2], F32)
            B2 = p.sb("B2", [128, GP, 2], F32)
            TAB1 = p.sb("TAB1", [128, GP, 32, 2], F32)
            TAB2 = p.sb("TAB2", [128, GP, 32, 2], F32)
            cs = p.sb("cs", [128, 3, 128], F32)
            dc = p.sb("dc", [128, GP], F32)
            F["wts"] = (BS, CAf, CAb, Tbf, A1, A2, B1, B2, TAB1, TAB2, cs, dc)
            p.stack = st
        else:
            (BS, CAf, CAb, Tbf, A1, A2, B1, B2, TAB1, TAB2, cs, dc) = F["wts"]
        if phase == "alloc":
            return
        if phase not in ("derive",):
            U = F["U"] if "U" in F else p.sb("U", [128, GP, NCH], BF16)
            G = p.sb("G", [128, GP, NB, 2], F32)
        pt = F["pt"]
        psS = F["psS"]
        psY = F["psY"]
        ptk = F["ptk"]; psSk = F["psSk"]; psYk = F["psYk"]
        if standalone:
            for g in range(GP):
                p.dma("sp", U[:, g, :], F["u2"][g, :, :], w=[f"U{g}"])
        if phase in ("all", "derive"):
            p.dma("sp", cs[:], cst[:, :, :], w=["cs"])
            p.dma("sp", dc[:], dcol[:, :], w=["dc"])

        def tt(eng, out, a, b, op, r, w):
            p.op(eng, lambda e: e.tensor_tensor(out=out, in0=a, in1=b, op=op), r=r, w=w)

        def ts(eng, out, a, s1, s2, op0, op1, r, w):
            if s2 is None:
                p.op(eng, lambda e: e.tensor_scalar(out=out, in0=a, scalar1=s1, scalar2=None, op0=op0), r=r, w=w)
            else:
                p.op(eng, lambda e: e.tensor_scalar(out=out, in0=a, scalar1=s1, scalar2=s2, op0=op0, op1=op1), r=r, w=w)

        def cp(eng, out, a, r, w):
            if eng == "act":
                p.op(eng, lambda e: e.activation(out=out, in_=a, func=AF.Identity), r=r, w=w)
            else:
                p.op(eng, lambda e: e.tensor_copy(out=out, in_=a), r=r, w=w)

        if phase == "main" and "U" not in F:
            with contextlib.ExitStack() as stu:
                p.stack = stu
                F["load_U"](U)
                _barrier(p)
            p.stack = st
        for st1 in _maybe_stack(phase != "main"):
            p.stack = st1
            if phase == "all" and not standalone:
                F["load_U"](U)
            pr = p.sb("pr", [128, GP, 3], F32)
            Bt = p.sb("Bt", [128, GP, 16, 2], F32)
            Ct = p.sb("Ct", [128, GP, 16, 2], F32)
            p.dma("sp", pr[:], par[:, :, :], w=["pr"])
            p.dma("sp", Bt[:], bri[:, :, :, :], w=["Bt"])
            p.dma("sp", Ct[:], cri[:, :, :, :], w=["Ct"])
            sm = {}
            for nm in ("dt", "xr", "th", "mag", "t", "tf", "fr", "m", "sn", "cn", "are", "aim", "n2", "rn", "ire", "iim",
                       "nre", "clre", "clim", "wre", "wim"):
                sm[nm] = p.sb("s_" + nm, [128, GP], F32)
            ti = p.sb("s_ti", [128, GP], I32)
            Pre = p.sb("Pre", [128, GP, 8], F32)
            Pim = p.sb("Pim", [128, GP, 8], F32)
            Tre = p.sb("Tre", [128, GP, 32], F32)
            Tim = p.sb("Tim", [128, GP, 32], F32)
            PWre = p.sb("PWre", [128, GP, 8], F32)
            PWim = p.sb("PWim", [128, GP, 8], F32)
            EXre = p.sb("EXre", [128, GP, 8], F32)
            EXim = p.sb("EXim", [128, GP, 8], F32)
            Bbre = p.sb("Bbre", [128, GP, 16], F32)
            Bbim = p.sb("Bbim", [128, GP, 16], F32)
            Xre = p.sb("Xre", [128, GP, 8, 16], F32)
            Xim = p.sb("Xim", [128, GP, 8, 16], F32)
            Cre = p.sb("CAre", [128, GP, 8, 16], F32)
            Cim = p.sb("CAim", [128, GP, 8, 16], F32)
            Rre = p.sb("Rre", [128, GP, 8, 16], F32)
            Rim = p.sb("Rim", [128, GP, 8, 16], F32)
            R2re = p.sb("R2re", [128, GP, 8, 16], F32)
            R2im = p.sb("R2im", [128, GP, 8, 16], F32)
            tq = [p.sb(f"tq{i}", [128, GP * 128], F32) for i in range(4)]

            def cmul(o_re, o_im, okeys, a_re, a_im, akeys, b_re, b_im, bkeys, shape, eng="dve"):
                n = int(np.prod(shape))
                pat = {1: None, 2: "p (a b) -> p a b", 3: "p (a b c) -> p a b c"}[len(shape)]
                kw = dict(zip("abc", shape))
                tv = [t[:, 0:n] if len(shape) == 1 else t[:, 0:n].rearrange(pat, **kw) for t in tq]
                tk = [f"tq{i}" for i in range(4)]
                tt(eng, tv[0], a_re, b_re, ALU.mult, akeys + bkeys, [tk[0]])
                tt(eng, tv[1], a_im, b_im, ALU.mult, akeys + bkeys, [tk[1]])
                tt(eng, tv[2], a_re, b_im, ALU.mult, akeys + bkeys, [tk[2]])
                tt(eng, tv[3], a_im, b_re, ALU.mult, akeys + bkeys, [tk[3]])
                tt(eng, o_re, tv[0], tv[1], ALU.subtract, [tk[0], tk[1]], okeys)
                tt(eng, o_im, tv[2], tv[3], ALU.add, [tk[2], tk[3]], okeys)

            s = sm
            p.op("act", lambda e: e.activation(out=s["dt"][:], in_=pr[:, :, 2], func=AF.Exp), r=["pr"], w=["dt"])
            tt("dve", s["xr"][:], s["dt"][:], pr[:, :, 0], ALU.mult, ["dt", "pr"], ["xr"])
            tt("dve", s["th"][:], s["dt"][:], pr[:, :, 1], ALU.mult, ["dt", "pr"], ["th"])
            p.op("act", lambda e: e.activation(out=s["mag"][:], in_=s["xr"][:], func=AF.Exp), r=["xr"], w=["mag"])

            def sin_of(off, out_nm):
                ts("dve", s["t"][:], s["th"][:], 1.0 / TWO_PI, off, ALU.mult, ALU.add, ["th"], ["t"])
                cp("dve", ti[:], s["t"][:], ["t"], ["ti"])
                cp("dve", s["tf"][:], ti[:], ["ti"], ["tf"])
                tt("dve", s["fr"][:], s["t"][:], s["tf"][:], ALU.subtract, ["t", "tf"], ["fr"])
                ts("dve", s["m"][:], s["fr"][:], 0.0, None, ALU.is_lt, None, ["fr"], ["m"])
                tt("dve", s["fr"][:], s["fr"][:], s["m"][:], ALU.add, ["fr", "m"], ["fr"])
                ts("dve", s["m"][:], s["fr"][:], 1.0, None, ALU.is_ge, None, ["fr"], ["m"])
                tt("dve", s["fr"][:], s["fr"][:], s["m"][:], ALU.subtract, ["fr", "m"], ["fr"])
                ts("dve", s["fr"][:], s["fr"][:], TWO_PI, -np.pi, ALU.mult, ALU.add, ["fr"], ["fr"])
                ts("dve", s["fr"][:], s["fr"][:], 3.1415925, -3.1415925, ALU.min, ALU.max, ["fr"], ["fr"])
                p.op("act", lambda e: e.activation(out=s[out_nm][:], in_=s["fr"][:], func=AF.Sin), r=["fr"], w=[out_nm])

            sin_of(0.5, "sn")
            sin_of(0.75, "cn")
            tt("dve", s["are"][:], s["mag"][:], s["cn"][:], ALU.mult, ["mag", "cn"], ["are"])
            tt("dve", s["aim"][:], s["mag"][:], s["sn"][:], ALU.mult, ["mag", "sn"], ["aim"])
            cp("dve", Pre[:, :, 0], s["are"][:], ["are"], ["P"])
            cp("dve", Pim[:, :, 0], s["aim"][:], ["aim"], ["P"])
            cmul(Pre[:, :, 1], Pim[:, :, 1], ["P"], s["are"][:], s["aim"][:], ["are", "aim"], s["are"][:], s["aim"][:], [], [GP])
            for w_ in (2, 4):
                bsh = [128, GP, w_]
                cmul(Pre[:, :, w_:2 * w_], Pim[:, :, w_:2 * w_], ["P"], Pre[:, :, 0:w_], Pim[:, :, 0:w_], ["P"],
                     Pre[:, :, w_ - 1:w_].broadcast_to(bsh), Pim[:, :, w_ - 1:w_].broadcast_to(bsh), [], [GP, w_])
            cp("dve", Tre[:, :, 0], Pre[:, :, 7], ["P"], ["T"])
            cp("dve", Tim[:, :, 0], Pim[:, :, 7], ["P"], ["T"])
            cmul(Tre[:, :, 1], Tim[:, :, 1], ["T"], Pre[:, :, 7], Pim[:, :, 7], ["P"], Pre[:, :, 7], Pim[:, :, 7], [], [GP])
            for w_ in (2, 4, 8, 16):
                bsh = [128, GP, w_]
                cmul(Tre[:, :, w_:2 * w_], Tim[:, :, w_:2 * w_], ["T"], Tre[:, :, 0:w_], Tim[:, :, 0:w_], ["T"],
                     Tre[:, :, w_ - 1:w_].broadcast_to(bsh), Tim[:, :, w_ - 1:w_].broadcast_to(bsh), [], [GP, w_])
            for (t1, t2, sre, sim, key) in ((A1, A2, Pre[:, :, 7], Pim[:, :, 7], "P"), (B1, B2, Tre[:, :, 31], Tim[:, :, 31], "T")):
                cp("dve", t1[:, :, 0], sre, [key], ["AB"])
                cp("dve", t1[:, :, 1], sre, [key], ["AB"])
                ts("dve", t2[:, :, 0], sim, -1.0, None, ALU.mult, None, [key], ["AB"])
                cp("dve", t2[:, :, 1], sim, [key], ["AB"])
            cp("dve", TAB1[:, :, :, 0], Tre[:], ["T"], ["TAB"])
            cp("dve", TAB1[:, :, :, 1], Tre[:], ["T"], ["TAB"])
            ts("dve", TAB2[:, :, :, 0], Tim[:], -1.0, None, ALU.mult, None, ["T"], ["TAB"])
            cp("dve", TAB2[:, :, :, 1], Tim[:], ["T"], ["TAB"])
            tt("dve", s["n2"][:], Pre[:, :, 7], Pre[:, :, 7], ALU.mult, ["P"], ["n2"])
            tt("dve", s["rn"][:], Pim[:, :, 7], Pim[:, :, 7], ALU.mult, ["P"], ["rn"])
            tt("dve", s["n2"][:], s["n2"][:], s["rn"][:], ALU.add, ["n2", "rn"], ["n2"])
            p.op("dve", lambda e: e.reciprocal(out=s["rn"][:], in_=s["n2"][:]), r=["n2"], w=["rn"])
            tt("dve", s["ire"][:], Pre[:, :, 7], s["rn"][:], ALU.mult, ["P", "rn"], ["ire"])
            tt("dve", s["iim"][:], Pim[:, :, 7], s["rn"][:], ALU.mult, ["P", "rn"], ["iim"])
            ts("dve", s["iim"][:], s["iim"][:], -1.0, None, ALU.mult, None, ["iim"], ["iim"])
            tt("dve", s["n2"][:], pr[:, :, 0], pr[:, :, 0], ALU.mult, ["pr", "ire", "iim"], ["n2"])
            tt("dve", s["rn"][:], pr[:, :, 1], pr[:, :, 1], ALU.mult, ["pr"], ["rn"])
            tt("dve", s["n2"][:], s["n2"][:], s["rn"][:], ALU.add, ["n2", "rn"], ["n2"])
            p.op("dve", lambda e: e.reciprocal(out=s["rn"][:], in_=s["n2"][:]), r=["n2"], w=["rn"])
            tt("dve", s["clre"][:], pr[:, :, 0], s["rn"][:], ALU.mult, ["pr", "rn"], ["clre"])
            tt("dve", s["clim"][:], pr[:, :, 1], s["rn"][:], ALU.mult, ["pr", "rn"], ["clim"])
            ts("dve", s["clim"][:], s["clim"][:], -1.0, None, ALU.mult, None, ["clim"], ["clim"])
            ts("dve", s["nre"][:], s["are"][:], -1.0, None, ALU.add, None, ["are"], ["nre"])
            cmul(s["wre"][:], s["wim"][:], ["w"], s["nre"][:], s["aim"][:], ["nre", "aim"], s["clre"][:], s["clim"][:], ["clre", "clim"], [GP])
            bsh = [128, GP, 16]
            cmul(Bbre[:], Bbim[:], ["Bb"], Bt[:, :, :, 0], Bt[:, :, :, 1], ["Bt"],
                 s["wre"][:].unsqueeze(2).broadcast_to(bsh), s["wim"][:].unsqueeze(2).broadcast_to(bsh), ["w"], [GP, 16])
            p.op("dve", lambda e: e.memset(PWre[:], 1.0), w=["PW"])
            p.op("dve", lambda e: e.memset(PWim[:], 0.0), w=["PW"])
            for (dst, src) in ((PWre, Pre), (PWim, Pim)):
                for j in range(7):
                    cp("dve", dst[0:64, :, j], src[0:64, :, 6 - j], ["P"], ["PW"])
                cp("dve", dst[64:128, :, 1:8], src[64:128, :, 0:7], ["P"], ["PW"])
            for (dst, src) in ((EXre, Pre), (EXim, Pim)):
                cp("dve", dst[0:64, :, :], src[0:64, :, :], ["P"], ["EX"])
                for i in range(8):
                    cp("dve", dst[64:128, :, i], src[64:128, :, 7 - i], ["P"], ["EX"])
            xsh = [128, GP, 8, 16]
            cmul(Xre[:], Xim[:], ["X"], PWre[:].unsqueeze(3).broadcast_to(xsh), PWim[:].unsqueeze(3).broadcast_to(xsh), ["PW"],
                 Bbre[:].unsqueeze(2).broadcast_to(xsh), Bbim[:].unsqueeze(2).broadcast_to(xsh), ["Bb"], [GP, 8, 16])
            cmul(Cre[:], Cim[:], ["CA"], EXre[:].unsqueeze(3).broadcast_to(xsh), EXim[:].unsqueeze(3).broadcast_to(xsh), ["EX"],
                 Ct[:, :, :, 0].unsqueeze(2).broadcast_to(xsh), Ct[:, :, :, 1].unsqueeze(2).broadcast_to(xsh), ["Ct"], [GP, 8, 16])
            g3 = [128, GP, 128]
            cmul(Rre[:].rearrange("p g i h -> p g (i h)"), Rim[:].rearrange("p g i h -> p g (i h)"), ["R"],
                 Cre[:].rearrange("p g i h -> p g (i h)"), Cim[:].rearrange("p g i h -> p g (i h)"), ["CA"],
                 s["ire"][:].unsqueeze(2).broadcast_to(g3), s["iim"][:].unsqueeze(2).broadcast_to(g3), ["ire", "iim"], [GP, 128])
            ts("dve", Rim[:], Rim[:], -1.0, None, ALU.mult, None, ["R"], ["R"])
            p.op("dve", lambda e: e.memset(CAf[:], 0.0), w=["CAs"])
            p.op("dve", lambda e: e.memset(CAb[:], 0.0), w=["CAs"])
            for (dstT, sl) in ((CAf, slice(0, 64)), (CAb, slice(64, 128))):
                cp("dve", dstT[sl, :, 0, :], Cre[sl].rearrange("p g i h -> p g (i h)"), ["CA"], ["CAs"])
                ts("dve", dstT[sl, :, 1, :], Cim[sl].rearrange("p g i h -> p g (i h)"), -1.0, None, ALU.mult, None, ["CA"], ["CAs"])
            p.op("dve", lambda e: e.memset(R2re[:], 0.0), w=["R2"])
            p.op("dve", lambda e: e.memset(R2im[:], 0.0), w=["R2"])
            cp("dve", R2re[64:128], Rre[64:128], ["R"], ["R2"])
            cp("dve", R2im[64:128], Rim[64:128], ["R"], ["R2"])
            p.op("dve", lambda e: e.memset(Rre[64:128], 0.0), r=["R2"], w=["R"])
            p.op("dve", lambda e: e.memset(Rim[64:128], 0.0), r=["R2"], w=["R"])
            for g in range(GP):
                for ri, Xp in enumerate((Xre, Xim)):
                    ptile = pt[(2 * g + ri) % 2]
                    pk = ptk[(2 * g + ri) % 2]
                    p.op("pe", lambda e, ptile=ptile, Xp=Xp, g=g: e.transpose(out=ptile[:, 0:128], in_=Xp[:, g].rearrange("p j h -> p (j h)"), identity=cs[:, 2, :]),
                         r=["X", "cs"], w=[pk])
                    cp("act" if ri == 0 else "dve", BS[:, g, ri, :], ptile[:, 0:128], [pk], [f"BS{g}"])
            for g in range(GP):
                for d in range(2):
                    ptile = pt[d]
                    pk = ptk[d]
                    Ra, Rb_ = (Rre, Rim) if d == 0 else (R2re, R2im)
                    p.op("pe", lambda e, ptile=ptile, Ra=Ra, g=g: e.matmul(ptile[:, 0:128], lhsT=Xre[:, g].rearrange("p j h -> p (j h)"),
                                                                           rhs=Ra[:, g].rearrange("p i h -> p (i h)"), start=True, stop=False),
                         r=["X", "R", "R2"], w=[pk])
                    p.op("pe", lambda e, ptile=ptile, Rb_=Rb_, g=g: e.matmul(ptile[:, 0:128], lhsT=Xim[:, g].rearrange("p j h -> p (j h)"),
                                                                             rhs=Rb_[:, g].rearrange("p i h -> p (i h)"), start=False, stop=True),
                         r=["X", "R", "R2"], w=[pk])
                tA = tq[0][:, 0:128]
                tB = tq[1][:, 0:128]
                tt("dve", tA, pt[0][:, 0:128], cs[:, 0, :], ALU.mult, [ptk[0], "cs"], ["tq0"])
                tt("dve", tB, pt[1][:, 0:128], cs[:, 1, :], ALU.mult, [ptk[1], "cs"], ["tq1"])
                tt("dve", tA, tA, tB, ALU.add, ["tq0", "tq1"], ["tq0"])
                p.op("dve", lambda e, g=g, tA=tA: e.scalar_tensor_tensor(out=Tbf[:, g, :], in0=cs[:, 2, :], scalar=dc[:, g:g + 1], in1=tA,
                                                                      op0=ALU.mult, op1=ALU.add),
                     r=["cs", "dc", "tq0"], w=[f"Tbf{g}"])
            if not F.get("no_barrier"):
                _barrier(p)
        p.stack = st

        if phase == "derive":
            return
        W = p.sb("W", [128, GP, NCH, 2], F32)
        ne = 0
        for g in range(GP if stage >= 2 else 0):
            for ri in range(2):
                for (c0, n) in SEGS:
                    ps = psS[ne % len(psS)]
                    pk = psSk[ne % len(psS)]
                    p.op("pe", lambda e, ps=ps, g=g, ri=ri, c0=c0, n=n: e.matmul(ps[:, 0:n], lhsT=BS[:, g, ri, :], rhs=U[:, g, c0:c0 + n], start=True, stop=True),
                         r=[f"BS{g}", f"U{g}"], w=[pk])
                    if c0 == 0:
                        qs = slice(31, None, -1)
                    else:
                        qs = slice(1087 - c0, 1087 - c0 - n, -1)
                    e1, e2 = ("act", "dve")
                    if e1 == "act":
                        p.op("act", lambda e, ps=ps, g=g, ri=ri, c0=c0, n=n: e.activation(out=W[0:64, g, c0:c0 + n, ri], in_=ps[0:64, 0:n], func=AF.Identity),
                             r=[pk], w=[f"W{g}"])
                        cp("dve", W[64:128, g, qs, ri], ps[64:128, 0:n], [pk], [f"W{g}"])
                    else:
                        cp("dve", W[0:64, g, c0:c0 + n, ri], ps[0:64, 0:n], [pk], [f"W{g}"])
                        p.op("act", lambda e, ps=ps, g=g, ri=ri, qs=qs, n=n: e.activation(out=W[64:128, g, qs, ri], in_=ps[64:128, 0:n], func=AF.Identity),
                             r=[pk], w=[f"W{g}"])
                    ne += 1

        with contextlib.ExitStack() as st2:
            p.stack = st2
            Hn = p.sb("Hn", [128, GP, NCH, 2], BF16)
            tmpd = p.sb("tmpd", [128, 9 * 32 * 2], F32)
            tmpp = p.sb("tmpp", [128, 9 * 32 * 2], F32)
            yb = [p.sb(f"yb{i}", [128, 512], BF16) for i in range(2)]
            Wv = W[:].rearrange("p g (b s) r -> p g b s r", s=32)
            Hv = Hn[:].rearrange("p g (b s) r -> p g b s r", s=32)
            splits = (("dve", 0, 5, tmpd, "tmpd"), ("pool", 5, 8, tmpp, "tmpp"))
            for (eng, g0, g1, tmp, tk) in (splits if stage >= 3 else ()):
                ng = g1 - g0
                wk = [f"W{g}" for g in range(g0, g1)]
                tv = tmp[:, 0:ng * NB * 2].rearrange("p (g b r) -> p g b r", g=ng, b=NB)
                sh = [128, ng, NB, 2]
                a1 = A1[:, g0:g1, :].unsqueeze(2).broadcast_to(sh)
                a2 = A2[:, g0:g1, :].unsqueeze(2).broadcast_to(sh)
                for s_ in range(1, 32):
                    prev = Wv[:, g0:g1, :, s_ - 1, :]
                    prevs = Wv[:, g0:g1, :, s_ - 1, ::-1]
                    cur = Wv[:, g0:g1, :, s_, :]
                    tt(eng, tv, prev, a1, ALU.mult, wk + ["AB"], [tk])
                    tt(eng, cur, cur, tv, ALU.add, wk + [tk], wk)
                    tt(eng, tv, prevs, a2, ALU.mult, wk + ["AB"], [tk])
                    tt(eng, cur, cur, tv, ALU.add, wk + [tk], wk)
            p.op("dve", lambda e: e.memset(G[:, :, 0, :], 0.0), w=["G"])
            allw = [f"W{g}" for g in range(GP)]
            tg = tmpd[:, 0:GP * 2].rearrange("p (g r) -> p g r", r=2)
            cp("dve", G[:, :, 1, :], Wv[:, :, 0, 31, :], allw, ["G"])
            for b in range(1, NB - 1 if stage >= 4 else 0):
                tt("dve", tg, G[:, :, b, :], B1[:], ALU.mult, ["G", "AB"], ["tmpd"])
                tt("dve", G[:, :, b + 1, :], tg, Wv[:, :, b, 31, :], ALU.add, ["tmpd"] + allw, ["G"])
                tt("dve", tg, G[:, :, b, ::-1], B2[:], ALU.mult, ["G", "AB"], ["tmpd"])
                tt("dve", G[:, :, b + 1, :], G[:, :, b + 1, :], tg, ALU.add, ["G", "tmpd"], ["G"])
            nyd = {"n": 0}

            def emit_out(g):
                for (c0, n) in SEGS:
                    ny = nyd["n"]
                    ps = psY[ny % 2]
                    pk = psYk[ny % 2]
                    ybt = yb[ny % 2]
                    yk = f"yb{ny % 2}"
                    ny += 1
                    nyd["n"] = ny
                    rk = [f"U{g}", f"Tbf{g}", f"Hn{g}", "CAs"]
                    p.op("pe", lambda e, ps=ps, g=g, c0=c0, n=n: e.matmul(ps[:, 0:n], lhsT=Tbf[:, g, :], rhs=U[:, g, c0:c0 + n], start=True, stop=False),
                         r=rk, w=[pk])
                    for ri in range(2):
                        if c0 == 0:
                            p.op("pe", lambda e, ps=ps, g=g, ri=ri, n=n: e.matmul(ps[:, 1:n], lhsT=CAf[:, g, ri, :], rhs=Hn[:, g, 0:n - 1, ri], start=False, stop=False),
                                 r=rk, w=[pk])
                        else:
                            p.op("pe", lambda e, ps=ps, g=g, ri=ri, c0=c0, n=n: e.matmul(ps[:, 0:n], lhsT=CAf[:, g, ri, :], rhs=Hn[:, g, c0 - 1:c0 + n - 1, ri], start=False, stop=False),
                                 r=rk, w=[pk])
                    for ri in range(2):
                        last = (ri == 1)
                        if c0 == 0:
                            p.op("pe", lambda e, ps=ps, g=g, ri=ri, n=n, last=last: e.matmul(ps[:, 0:n - 1], lhsT=CAb[:, g, ri, :], rhs=Hn[:, g, 30::-1, ri], start=False, stop=last),
                                 r=rk, w=[pk])
                        else:
                            hi = 1086 - c0
                            p.op("pe", lambda e, ps=ps, g=g, ri=ri, hi=hi, n=n, last=last: e.matmul(ps[:, 0:n], lhsT=CAb[:, g, ri, :], rhs=Hn[:, g, hi:hi - n:-1, ri], start=False, stop=last),
                                 r=rk, w=[pk])
                    p.op("act", lambda e, ps=ps, ybt=ybt, n=n: e.activation(out=ybt[:, 0:n], in_=ps[:, 0:n], func=AF.Identity), r=[pk], w=[yk])
                    if standalone:
                        p.dma("sp", F["y2"][g, :, c0:c0 + n], ybt[:, 0:n], r=[yk], w=[f"y2_{g}_{c0}"], sem=f"dy{ny % 2}")
                    else:
                        F["store_y"](g, c0, n, ybt, yk)
            for g in range(GP if stage >= 5 else 0):
                eng, tmp, tk = ("dve", tmpd, "tmpd") if g < 5 else ("pool", tmpp, "tmpp")
                for (b0, b1) in ((0, 9), (9, 17), (17, 25), (25, NB)):
                    nb_ = b1 - b0
                    tv = tmp[:, 0:nb_ * 32 * 2].rearrange("p (b s r) -> p b s r", b=nb_, s=32)
                    sh = [128, nb_, 32, 2]
                    t1 = TAB1[:, g, :, :].unsqueeze(1).broadcast_to(sh)
                    t2 = TAB2[:, g, :, :].unsqueeze(1).broadcast_to(sh)
                    gg = G[:, g, b0:b1, :].unsqueeze(2).broadcast_to(sh)
                    ggs = G[:, g, b0:b1, ::-1].unsqueeze(2).broadcast_to(sh)
                    tt(eng, tv, t1, gg, ALU.mult, ["TAB", "G"], [tk])
                    tt(eng, Wv[:, g, b0:b1], Wv[:, g, b0:b1], tv, ALU.add, [f"W{g}", tk], [f"W{g}"])
                    tt(eng, tv, t2, ggs, ALU.mult, ["TAB", "G"], [tk])
                    tt(eng, Hv[:, g, b0:b1], Wv[:, g, b0:b1], tv, ALU.add, [f"W{g}", tk], [f"Hn{g}"])
                if stage >= 6:
                    emit_out(g)
            if standalone:
                if stage < 6:
                    p.dma("sp", F["y2"][0, :, 0:512], yb[0][:, :], w=["y2_dummy"], sem="dy0")
                    p.wait_all("sp", ["y2_dummy"])
                p.wait_all("sp", [f"y2_{g}_{c0}" for g in range(GP) for (c0, n) in SEGS])
            _barrier(p)
        p.stack = st


def _s5_consts():
    jj = np.arange(128) // 16
    mf = (jj[None, :] >= jj[:, None]).astype(np.float32)
    mb = (jj[:, None] >= jj[None, :]).astype(np.float32)
    return np.ascontiguousarray(np.stack([mf, mb, np.eye(128, dtype=np.float32)], axis=1))


def run_s5(u_full, a_re, a_im, log_dt, b_re, b_im, c_re, c_im, d_skip, stage=9):
    nc = build_s5(stage)
    cst = _s5_consts()
    ug = u_full.reshape(NCH, 8, NG, 16)
    in_maps = []
    for k in range(NC):
        gs = slice(GP * k, GP * k + GP)
        u2 = np.ascontiguousarray(ug[:, :, gs, :].transpose(2, 1, 3, 0).reshape(GP, 128, NCH))
        par = np.empty((128, GP, 3), np.float32)
        par[:, :, 0] = a_re[:, gs, :].transpose(0, 2, 1).reshape(128, GP)
        par[:, :, 1] = a_im[:, gs, :].transpose(0, 2, 1).reshape(128, GP)
        par[:, :, 2] = np.broadcast_to(log_dt[:, None, gs], (2, 64, GP)).reshape(128, GP)
        bri = np.stack([b_re[:, gs], b_im[:, gs]], axis=-1)
        bri = np.ascontiguousarray(bri.transpose(0, 2, 1, 3, 4).reshape(128, GP, 16, 2))
        cri = np.stack([c_re[:, gs], c_im[:, gs]], axis=-1)
        cri = np.ascontiguousarray(cri.transpose(0, 3, 1, 2, 4).reshape(128, GP, 16, 2))
        dcol = np.ascontiguousarray(np.broadcast_to(d_skip.reshape(NG, 16)[gs].T[None, :, :], (8, 16, GP)).reshape(128, GP))
        in_maps.append({"u2": u2, "par": par, "bri": bri, "cri": cri, "dcol": dcol, "cst": cst})
    res = _run(nc, in_maps)
    yg = np.stack([r["y2"] for r in res])
    y = yg.reshape(NC, GP, 8, 16, NCH).transpose(4, 2, 0, 1, 3).reshape(NCH * 8, DS5)
    return y


RA = [(0, 384), (384, 768), (768, 1024)]
G4 = [[0, 1, 2, 3], [4, 5, 6, 7]]
G2P = [[0, 4], [1, 5], [2, 6], [3, 7]]


def _g2_row(r, rho):
    hi, lo = divmod(r, 4)
    for (a0, a1) in RA:
        if a0 <= rho < a1:
            la = a1 - a0
            b, lo2 = divmod(lo, 2)
            return 8 * a0 + b * 4 * la + hi * 2 * la + lo2 * la + (rho - a0)
    raise ValueError


def _cc(p, groups, src, dst, rkeys, wkeys):
    p.custom("pool", lambda e: e.collective_compute("AllGather", ALU.bypass, replica_groups=groups, ins=[src], outs=[dst]),
             "cc", 1, r=rkeys, w=wkeys)


def _exchange(p, send, G1, G2, skey, gkey, stage=0):
    for (a0, a1) in (RA if stage in (0, 1) else ()):
        _cc(p, G4, send[a0:a1, :], G1[4 * a0:4 * a1, :], [skey], [gkey + "1"])
    for (a0, a1) in (RA if stage in (0, 2) else ()):
        la = a1 - a0
        for b in range(2):
            base = 8 * a0 + b * 4 * la
            _cc(p, G2P, G1[4 * a0 + 2 * b * la:4 * a0 + 2 * (b + 1) * la, :], G2[base:base + 4 * la, :], [gkey + "1"], [gkey])


def _g1_row(lo, rho):
    for (a0, a1) in RA:
        if a0 <= rho < a1:
            return 4 * a0 + lo * (a1 - a0) + (rho - a0)
    raise ValueError


GX_ROWS = 4096 + 1024


def _exchange2(p, GX, pk, idxt, gkey):
    pkt = p.sb("pkt", [128, 4, TL], BF16)
    for lo in range(4):
        _gather(p, pkt[:, lo, :], GX[:, :], idxt[:, 9 + lo:10 + lo], [gkey + "1", "idxt"], [f"pkt{lo}"])
    p.dma("sp", pk.rearrange("(l p) n -> p l n", p=128), pkt[:], r=[f"pkt{lo}" for lo in range(4)], w=["pk" + gkey])
    _cc(p, G2P, pk[:, :], GX[4096:GX_ROWS, :], ["pk" + gkey], [gkey])


HG = ([[0, 4], [1, 5], [2, 6], [3, 7]], [[0, 1], [2, 3], [4, 5], [6, 7]], [[0, 1, 2, 3], [4, 5, 6, 7]])
HOFF = (1024, 2048, 3072, 5120)
X_ROWS = HOFF[3]


def _hyper(p, X, pk, idxt, skeys, gkey):
    pkt = p.sb("pkt", [128, 4, TL], BF16)
    have = list(skeys)
    for st_ in range(3):
        for b in range(4):
            c_ = 9 + 4 * st_ + b
            _gather(p, pkt[:, b, :], X[:, :], idxt[:, c_:c_ + 1], have + ["idxt"], [f"pkt{b}"])
        p.dma("pool", pk.rearrange("(l p) n -> p l n", p=128), pkt[:], r=[f"pkt{b}" for b in range(4)], w=["pk" + gkey], sem="pkw")
        _cc(p, HG[st_], pk[:, :], X[HOFF[st_]:HOFF[st_ + 1], :], ["pk" + gkey], [f"{gkey}{st_}"])
        have.append(f"{gkey}{st_}")
    return have


def _hyper_idx(k):
    snd = lambda d: d * 128
    A_ = lambda sender, blk: HOFF[0] + ((sender >> 2) & 1) * 512 + blk * 128
    B_ = lambda sender, blk: HOFF[1] + (sender & 1) * 512 + blk * 128
    C_ = lambda sender, blk: HOFF[2] + (sender & 3) * 512 + blk * 128
    fin = {k: snd(k), k ^ 4: A_(k ^ 4, 0), k ^ 1: B_(k ^ 1, 0), k ^ 5: B_(k ^ 1, 2),
           k ^ 2: C_(k ^ 2, 0), k ^ 6: C_(k ^ 2, 1), k ^ 3: C_(k ^ 2, 2), k ^ 7: C_(k ^ 2, 3)}
    packs = [snd(k ^ 4), snd(k ^ 5), snd(k ^ 6), snd(k ^ 7),
             snd(k ^ 1), snd(k ^ 3), A_(k ^ 4, 1), A_(k ^ 4, 3),
             snd(k ^ 2), A_(k ^ 4, 2), B_(k ^ 1, 1), B_(k ^ 1, 3)]
    idx = np.empty((128, 21), np.int32)
    pp = np.arange(128)
    for s_ in range(8):
        idx[:, s_] = fin[s_] + pp
    idx[:, 8] = k * 128 + pp
    for j, base in enumerate(packs):
        idx[:, 9 + j] = base + pp
    return idx


def _gather(p, out_ap, src2d, idx_ap, rkeys, wkeys):
    p._ng = getattr(p, "_ng", 0) + 1
    p.custom("pool", lambda e: e.indirect_dma_start(out=out_ap, out_offset=None, in_=src2d,
                                                    in_offset=bass.IndirectOffsetOnAxis(ap=idx_ap, axis=0)),
             f"ig{p._ng % 4}", 16, r=rkeys, w=wkeys)


def build_fused():
    nc = _new_nc()
    dt_in = lambda name, shape, dt=F32: nc.dram_tensor(name, list(shape), dt, kind="ExternalInput").ap()
    xin0 = dt_in("xin", [TT, D])
    cc = dt_in("cc", [128, 16, 2])
    wada = dt_in("wada", [2, 16, 128, ADA_N])
    bada = dt_in("bada", [128, 2, 6, 2])
    win = dt_in("win", [2, 16, 128, DIN])
    ident = dt_in("ident", [128, 128])
    pv = dt_in("pv", [2, 128, 8, 5])
    wglu = dt_in("wglu", [2, 8, 128, DS5])
    wout = dt_in("wout", [2, 16, 128, D])
    lnb = dt_in("lnb", [2, 128, 2, D])
    par = dt_in("par", [2, 128, GP, 3])
    bri = dt_in("bri", [2, 128, GP, 16, 2])
    cri = dt_in("cri", [2, 128, GP, 16, 2])
    dcol = dt_in("dcol", [2, 128, GP])
    cst = dt_in("cst", [128, 3, 128])
    idx = dt_in("idx", [128, 21], I32)
    selm = dt_in("selm", [128, 64, 128], BF16)
    xout = nc.dram_tensor("xo", [TL, D], F32, kind="ExternalOutput").ap()
    x1 = nc.dram_tensor("x1", [TT, D], F32).ap()
    ib = lambda name, shape: nc.dram_tensor(name, list(shape), BF16).ap()
    Xu = [ib(f"Xu{l}", [X_ROWS, TL]) for l in range(2)]
    Xy = [ib(f"Xy{l}", [X_ROWS, TL]) for l in range(2)]
    usend = [Xu[l][0:1024, :] for l in range(2)]
    uctx = [ib(f"uctx{l}", [1024, CTX]) for l in range(2)]
    G2u = Xu
    pku = [ib(f"pku{l}", [512, TL]) for l in range(2)]
    ysend = [Xy[l][0:1024, :] for l in range(2)]
    G2y = Xy
    pky = [ib(f"pky{l}", [512, TL]) for l in range(2)]
    yctx = ib("yctx", [128, CTX])
    msend = nc.dram_tensor("msend", [128, 24], F32).ap()
    Mg1 = nc.dram_tensor("Mg1", [512, 24], F32).ap()
    Mg2 = nc.dram_tensor("Mg2", [1024, 24], F32).ap()
    Gc1 = ib("Gc1", [512, CTX])
    Gc2 = ib("Gc2", [1024, CTX])

    with contextlib.ExitStack() as st:
        p = Prog(nc, st)
        idf = p.sb("idf", [128, 128], F32)
        idb = p.sb("idb", [128, 128], BF16)
        ones = p.sb("ones", [128, 128], F32)
        Mall = p.sb("Mall", [128, 8, 2, 6, 2], F32)
        mall = lambda l, m, t0, t1: Mall[:, m // 6, l, m % 6, t0:t1]
        idxt = p.sb("idxt", [128, 21], I32)
        acc = [p.ps(f"acc{i}", [128, 512]) for i in range(4)]
        tps = [p.ps(f"tp{i}", [128, 512], BF16) for i in range(2)]
        qq = [p.ps(f"qq{i}", [128, 512]) for i in range(2)]
        p.dma("sp", idf[:], ident[:, :], w=["idf"])
        p.dma("sp", idxt[:], idx[:, :], w=["idxt"])
        p.op("dve", lambda e: e.tensor_copy(out=idb[:], in_=idf[:]), r=["idf"], w=["idb"])
        p.op("dve", lambda e: e.memset(ones[:], 1.0), w=["ones"])

        w0 = contextlib.ExitStack()
        wstacks = [w0, st]
        FSs = [dict(p=p, standalone=False, stage=9, par=par[l], bri=bri[l], cri=cri[l], dcol=dcol[l], cst=cst,
                    pt=qq, psS=acc[0:4], psY=acc[2:4], ptk=["qq0", "qq1"], psSk=["acc0", "acc1", "acc2", "acc3"], psYk=["acc2", "acc3"],
                    wstack=wstacks[l]) for l in range(2)]

        for l_ in (1, 0):
            p.pfx = f"L{l_}W_"
            FSs[l_].update(st=st, phase="alloc")
            _s5_body(FSs[l_])
            p.stack = st

        def derive(l, cur, no_barrier=True):
            p.pfx = f"L{l}D_"
            FSs[l].update(st=cur, phase="derive", no_barrier=no_barrier)
            _s5_body(FSs[l])
            p.stack = cur

        with contextlib.ExitStack() as sa:
            p.stack = sa
            p.pfx = "ada_"
            cct = p.sb("cct", [128, 16, 2], F32)
            sg = p.sb("sg", [128, 16, 2], F32)
            sc = p.sb("sc", [128, 16, 2], BF16)
            bat = p.sb("bat", [128, 2, 6, 2], F32)
            Mloc = p.sb("Mloc", [128, 2, 6, 2], F32)
            wbA = [p.sb(f"wbA{i}", [128, 16, 256], BF16) for i in range(3)]
            p.dma("sp", cct[:], cc[:, :, :], w=["cct"])
            p.dma("sp", bat[:], bada[:, :, :, :], w=["bat"])
            p.op("act", lambda e: e.activation(out=sg[:], in_=cct[:], func=AF.Sigmoid), r=["cct"], w=["sg"])
            p.op("dve", lambda e: e.tensor_tensor(out=sc[:], in0=cct[:], in1=sg[:], op=ALU.mult), r=["cct", "sg"], w=["sc"])
            derive(0, sa)
            p.pfx = "ada_"
            na = 0
            for l in range(2):
                for pair in range(3):
                    i = (l * 3 + pair) % 3
                    p.dma("pool", wbA[i][:], wada[l, :, :, pair * 256:(pair + 1) * 256].rearrange("k p n -> p k n"), w=[f"wbA{i}"], sem=f"dwa{i}")
                    for cl in range(2):
                        m = pair * 2 + cl
                        ps = acc[na % 4]
                        pk = f"acc{na % 4}"
                        na += 1
                        for kf in range(16):
                            p.op("pe", lambda e, ps=ps, i=i, kf=kf, cl=cl: e.matmul(ps[:, 0:2], lhsT=wbA[i][:, kf, cl * 128:(cl + 1) * 128], rhs=sc[:, kf, :],
                                                                                   start=(kf == 0), stop=(kf == 15)),
                                 r=[f"wbA{i}", "sc"], w=[pk])
                        p.op("dve", lambda e, ps=ps, l=l, m=m: e.tensor_tensor(out=Mloc[:, l, m, :], in0=ps[:, 0:2], in1=bat[:, l, m, :], op=ALU.add),
                             r=[pk, "bat"], w=["Mloc"])
            p.dma("sp", msend[:, :], Mloc[:].rearrange("p l m t -> p (l m t)"), r=["Mloc"], w=["msend"])
            _cc(p, G4, msend[:, :], Mg1[:, :], ["msend"], ["Mg1"])
            _cc(p, G2P, Mg1[:, :], Mg2[:, :], ["Mg1"], ["Mg2"])
            p.dma("sp", Mall[:].rearrange("p c l m t -> p c (l m t)"), Mg2.rearrange("(c p) n -> p c n", p=128), r=["Mg2"], w=["Mall"])
            _barrier(p)
        p.stack = st

        for l in range(2):
            with contextlib.ExitStack() as stl:
                p.stack = stl
                p.pfx = f"L{l}_"
                mst = p.sb("mst", [128, 16, 4], F32)
                for (j, m0, w_, add1) in ((0, 0, 0, False), (1, 16, 0, True), (2, 0, 1, False), (3, 16, 1, True)):
                    m = m0
                    while m < m0 + 16:
                        c_ = m // 6
                        m1 = min(m0 + 16, 6 * (c_ + 1))
                        src = Mall[:, c_, l, m - 6 * c_:m1 - 6 * c_, w_]
                        dst = mst[:, m - m0:m1 - m0, j]
                        p.op("dve", lambda e, src=src, dst=dst, add1=add1: e.tensor_scalar(out=dst, in0=src, scalar1=(1.0 if add1 else 0.0), scalar2=None, op0=ALU.add),
                             r=["Mall"], w=["mst"])
                        m = m1
                xsrc = xin0 if l == 0 else x1
                o = p.sb("o", [128, 16, TT], BF16)
                pvt = p.sb("pvt", [128, 8, 5], F32)
                p.dma("sp", pvt[:], pv[l][:, :, :], w=["pvt"])
                su = contextlib.ExitStack()
                p.stack = su
                Ut = p.sb("U", [128, GP, NCH], BF16)
                p.stack = stl

                def load_U(U, l=l):
                    p.pfx = f"L{l}U_"
                    Ufm = p.sb("Ufm", [128, 8, TL], BF16)
                    Ucx = p.sb("Ucx", [128, CTX], BF16)
                    for r in range(8):
                        _gather(p, Ufm[:, r, :], G2u[l][:, :], idxt[:, r:r + 1], ukeys["k"] + ["idxt"], [f"Ufm{r}"])
                    _gather(p, Ucx[:, :], uctx[l][:, :], idxt[:, 8:9], ["uctx", "idxt"], ["Ucx"])
                    rk = [f"Ufm{r}" for r in range(8)] + ["Ucx"]
                    n = 0
                    for g in range(GP):
                        for j in range(8):
                            q = "pool"
                            n += 1
                            p.dma(q, U[16 * j:16 * j + 16, g, 32:NCH].rearrange("p (r c) -> p r c", c=128), Ufm[16 * g:16 * g + 16, :, j * 128:(j + 1) * 128],
                                  r=rk, w=[f"Urp{g}_{j}"], sem=f"rpk_{q}")
                            p.dma(q, U[16 * j:16 * j + 16, g, 0:32], Ucx[16 * g:16 * g + 16, j * 32:(j + 1) * 32],
                                  r=rk, w=[f"Urc{g}_{j}"], sem=f"rpk_{q}")
                    tok = {"rpk_pool": p.cnt["rpk_pool"]}
                    for g in range(GP):
                        p.lastw[f"U{g}"] = dict(tok)
                        p.readers[f"U{g}"] = {}

                FS = FSs[l]
                with contextlib.ExitStack() as sA:
                    p.stack = sA
                    p.pfx = f"L{l}A_"
                    hT = p.sb("hT", [128, 16, TT], BF16)
                    def cc_piece(a, l=l):
                        pass

                    ukeys = {}

                    FA = dict(p=p, st=sA, xin=xsrc, win=win[l], idb=idb, mst=mst, hT=hT, acc=acc, tps=tps, do_ln=True, standalone=False,
                              usend=usend[l], uctx=uctx[l], cc_piece=cc_piece)
                    _tok_body("A", True, FA)
                    p.pfx = f"L{l}C1_"
                    FC1 = dict(p=p, st=sA, xin=xsrc, win=win[l], idb=idb, mst=mst, hT=hT, acc=acc, tps=tps, do_ln=False, standalone=False,
                               part=1, o=o, pvt=pvt, odbg=None, hwcast=True, nwst=1, nwb=2, tail_hook=(lambda: load_U(Ut)),
                               mid_hook=(lambda l=l: ukeys.update(k=_hyper(p, Xu[l], pku[l], idxt, [f"usend{c_}" for c_ in range(8)], "G2u"))))
                    _tok_body("C", l == 0, FC1)
                    _barrier(p)
                p.stack = stl

                yn = {"n": 0}

                def store_y(g, c0, n, ybt, yk, l=l):
                    yn["n"] += 1
                    sem = f"dys{yn['n'] % 2}"
                    if c0 == 0:
                        if l == 0:
                            p.dma("sp", yctx[:, g * 32:(g + 1) * 32], ybt[:, 0:32], r=[yk], w=["yctx"], sem=sem)
                        return
                    r0 = (c0 - 32) // 128
                    dst = ysend[l].rearrange("(r p) (g c) -> p r g c", p=128, g=GP)[:, r0:r0 + 4, g, :]
                    p.dma("sp", dst, ybt[:, 0:512].rearrange("p (r c) -> p r c", c=128), r=[yk], w=["ysend"], sem=sem)

                p.pfx = f"L{l}S_"
                with contextlib.ExitStack() as sS:
                    p.stack = sS
                    FS.update(st=sS, phase="main", load_U=load_U, store_y=store_y, U=Ut)
                    _s5_body(FS)
                    _barrier(p)
                p.stack = stl
                su.close()
                ykeys = {}
                p.pfx = f"L{l}X_"
                with contextlib.ExitStack() as sX:
                    p.stack = sX
                    ykeys["k"] = _hyper(p, Xy[l], pky[l], idxt, ["ysend"], "G2y")
                    if l == 0:
                        _cc(p, G4, yctx[:, :], Gc1[:, :], ["yctx"], ["Gc1"])
                        _cc(p, G2P, Gc1[:, :], Gc2[:, :], ["Gc1"], ["Gc2"])
                        derive(1, sX, no_barrier=True)
                    _barrier(p)
                p.stack = stl

                def load_gel(gel, l=l):
                    prev = p.stack
                    with contextlib.ExitStack() as ssel:
                        p.stack = ssel
                        Sel = p.sb("Sel", [128, 64, 128], BF16)
                        p.dma("sp", Sel[:], selm[:, :, :], w=["Sel"])
                        nq = 0
                        for hh in range(4):
                            with contextlib.ExitStack() as sg_:
                                p.stack = sg_
                                k0 = 2 * hh
                                Yr = p.sb(f"Yr{hh}", [128, 2, TL], BF16)
                                for k_ in range(2):
                                    _gather(p, Yr[:, k_, :], G2y[l][:, :], idxt[:, k0 + k_:k0 + k_ + 1], ykeys["k"] + ["idxt"], [f"Yr{k_}"])
                                if l == 0:
                                    Yc = p.sb(f"Yc{hh}", [128, 2, CTX], BF16)
                                    p.dma("sp", Yc[:], Gc2.rearrange("(k p) n -> p k n", p=128)[:, k0:k0 + 2, :], r=["Gc2"], w=["Yc"])
                                for k_ in range(2):
                                    ch = k0 + k_
                                    for i in range(8):
                                        ps = acc[nq % 4]
                                        pk = f"acc{nq % 4}"
                                        eng = "act" if nq % 2 == 0 else "dve"
                                        nq += 1
                                        for g in range(8):
                                            p.op("pe", lambda e, ps=ps, i=i, g=g, k_=k_, Yr=Yr: e.matmul(
                                                ps[:, 0:128], lhsT=Sel[:, i * 8 + g, :], rhs=Yr[:, k_, g * 128:(g + 1) * 128], start=(g == 0), stop=(g == 7)),
                                                r=["Sel", f"Yr{k_}"], w=[pk])
                                        if l == 0:
                                            for g in range(8):
                                                p.op("pe", lambda e, ps=ps, i=i, g=g, k_=k_, Yc=Yc: e.matmul(
                                                    ps[:, 128:160], lhsT=Sel[:, i * 8 + g, :], rhs=Yc[:, k_, g * 32:(g + 1) * 32], start=(g == 0), stop=(g == 7)),
                                                    r=["Sel", "Yc"], w=[pk])
                                        dl = gel[:, ch, 0:TL].rearrange("p (c i) -> p i c", i=8)[:, i, :]
                                        dc_ = gel[:, ch, TL:TT].rearrange("p (c i) -> p i c", i=8)[:, i, :]
                                        if eng == "act":
                                            p.op("act", lambda e, ps=ps, dl=dl: e.activation(out=dl, in_=ps[:, 0:128], func=AF.Identity), r=[pk], w=[f"gel{ch}"])
                                            if l == 0:
                                                p.op("act", lambda e, ps=ps, dc_=dc_: e.activation(out=dc_, in_=ps[:, 128:160], func=AF.Identity), r=[pk], w=[f"gel{ch}"])
                                        else:
                                            p.op("dve", lambda e, ps=ps, dl=dl: e.tensor_copy(out=dl, in_=ps[:, 0:128]), r=[pk], w=[f"gel{ch}"])
                                            if l == 0:
                                                p.op("dve", lambda e, ps=ps, dc_=dc_: e.tensor_copy(out=dc_, in_=ps[:, 128:160]), r=[pk], w=[f"gel{ch}"])
                                _barrier(p)
                            p.stack = ssel
                    p.stack = prev

                def make_gb(gb, l=l):
                    with contextlib.ExitStack() as sg_:
                        prev = p.stack
                        p.stack = sg_
                        dg = [p.sb(f"dg{i}", [128, 128], F32) for i in range(2)]
                        nd = 0
                        for w_ in range(2):
                            for nb in range(4):
                                ps = qq[nb % 2]
                                pk = f"qq{nb % 2}"
                                for j in range(4):
                                    kf = nb * 4 + j
                                    d = dg[nd % 2]
                                    dk = f"dg{nd % 2}"
                                    nd += 1
                                    p.op("dve", lambda e, d=d, kf=kf, w_=w_: e.tensor_scalar(out=d[:], in0=idf[:], scalar1=mall(l, 32 + kf, w_, w_ + 1), scalar2=None, op0=ALU.mult),
                                         r=["idf", "Mall"], w=[dk])
                                    p.op("pe", lambda e, ps=ps, d=d, j=j: e.matmul(ps[:, j * 128:(j + 1) * 128], lhsT=ones[:], rhs=d[:], start=True, stop=True),
                                         r=["ones", dk], w=[pk])
                                p.op("act", lambda e, ps=ps, w_=w_, nb=nb: e.activation(out=gb[:, w_, nb * 512:(nb + 1) * 512], in_=ps[:, :], func=AF.Identity),
                                     r=[pk], w=["gb"])
                        _barrier(p)
                    p.stack = prev

                p.pfx = f"L{l}C_"
                with contextlib.ExitStack() as sC:
                    p.stack = sC
                    FC = dict(p=p, st=sC, xin=xsrc, win=win[l], idb=idb, mst=mst, hT=None, acc=acc, tps=tps, do_ln=False, standalone=False,
                              part=2, o=o, pvt=pvt,
                              pv=pv[l], wglu=wglu[l], wout=wout[l], lnb=lnb[l], xo=(x1 if l == 0 else xout), load_gel=load_gel, make_gb=make_gb, odbg=None)
                    _tok_body("C", l == 0, FC)
                    _barrier(p)
                p.stack = stl
            p.stack = st
            if l == 0:
                w0.close()
        p.emit()
    return nc


def _prep_fused_inputs(x, c, ctx, c_ctx, w_ada, b_ada, w_in, s5_a_re, s5_a_im, s5_log_dt, s5_b_re, s5_b_im,
                       s5_c_re, s5_c_im, s5_d, w_glu, b_glu, conv_w, conv_b, w_out, ln_g, ln_b):
    cc = np.stack([c.reshape(D), c_ctx.reshape(D)], axis=-1)
    cc = np.ascontiguousarray(cc.reshape(16, 128, 2).transpose(1, 0, 2))
    wada_all = w_ada.reshape(2, 16, 128, NC, ADA_N)
    bada_all = np.broadcast_to(b_ada.reshape(2, NC, 6, 128).transpose(3, 1, 0, 2)[:, :, :, :, None], (128, NC, 2, 6, 2))
    win = w_in.reshape(2, 16, 128, DIN)
    pv = np.empty((2, 128, 8, 5), np.float32)
    for l in range(2):
        pv[l, :, :, 0] = b_glu[l].reshape(8, 128).T
        for j in range(3):
            pv[l, :, :, 1 + j] = conv_w[l, j].reshape(8, 128).T
        pv[l, :, :, 4] = conv_b[l].reshape(8, 128).T
    wglu = w_glu.reshape(2, 8, 128, DS5)
    wout = w_out.reshape(2, 16, 128, D)
    lnb = np.ascontiguousarray(np.broadcast_to(np.stack([ln_g, ln_b], axis=1)[:, None, :, :], (2, 128, 2, D)))
    cst = _s5_consts()
    selm = np.zeros((128, 64, 128), np.float32)
    for i_ in range(8):
        for g_ in range(8):
            for h_ in range(16):
                selm[16 * i_ + h_, i_ * 8 + g_, 16 * g_ + h_] = 1.0
    selm = selm.astype(ml_dtypes.bfloat16)
    in_maps = []
    for k in range(NC):
        gs = slice(GP * k, GP * k + GP)
        par = np.empty((2, 128, GP, 3), np.float32)
        bri = np.empty((2, 128, GP, 16, 2), np.float32)
        cri = np.empty((2, 128, GP, 16, 2), np.float32)
        dcol = np.empty((2, 128, GP), np.float32)
        for l in range(2):
            par[l, :, :, 0] = s5_a_re[l][:, gs, :].transpose(0, 2, 1).reshape(128, GP)
            par[l, :, :, 1] = s5_a_im[l][:, gs, :].transpose(0, 2, 1).reshape(128, GP)
            par[l, :, :, 2] = np.broadcast_to(s5_log_dt[l][:, None, gs], (2, 64, GP)).reshape(128, GP)
            bri[l] = np.stack([s5_b_re[l][:, gs], s5_b_im[l][:, gs]], axis=-1).transpose(0, 2, 1, 3, 4).reshape(128, GP, 16, 2)
            cri[l] = np.stack([s5_c_re[l][:, gs], s5_c_im[l][:, gs]], axis=-1).transpose(0, 3, 1, 2, 4).reshape(128, GP, 16, 2)
            dcol[l] = np.broadcast_to(s5_d[l].reshape(NG, 16)[gs].T[None, :, :], (8, 16, GP)).reshape(128, GP)
        idx = _hyper_idx(k)
        xin = np.ascontiguousarray(np.concatenate([x[0, k * TL:(k + 1) * TL], ctx[0]], axis=0))
        wada = np.ascontiguousarray(wada_all[:, :, :, k, :])
        bada = np.ascontiguousarray(bada_all[:, k])
        in_maps.append({"xin": xin, "cc": cc, "wada": wada, "bada": bada, "win": win, "ident": _IDENT, "pv": pv, "wglu": wglu,
                        "wout": wout, "lnb": lnb, "par": par, "bri": bri, "cri": cri, "dcol": dcol, "cst": cst, "idx": idx, "selm": selm})
    return in_maps


def kernel_unfused(x, c, ctx, c_ctx, w_ada, b_ada, w_in, s5_a_re, s5_a_im, s5_log_dt, s5_b_re, s5_b_im,
                   s5_c_re, s5_c_im, s5_d, w_glu, b_glu, conv_w, conv_b, w_out, ln_g, ln_b):
    return _kernel_unfused_impl(x, c, ctx, c_ctx, w_ada, b_ada, w_in, s5_a_re, s5_a_im, s5_log_dt, s5_b_re, s5_b_im,
                                s5_c_re, s5_c_im, s5_d, w_glu, b_glu, conv_w, conv_b, w_out, ln_g, ln_b)


def kernel(x, c, ctx, c_ctx, w_ada, b_ada, w_in, s5_a_re, s5_a_im, s5_log_dt, s5_b_re, s5_b_im,
           s5_c_re, s5_c_im, s5_d, w_glu, b_glu, conv_w, conv_b, w_out, ln_g, ln_b):
    f = lambda a: np.ascontiguousarray(np.asarray(a, dtype=np.float32))
    args = [f(a) for a in (x, c, ctx, c_ctx, w_ada, b_ada, w_in, s5_a_re, s5_a_im, s5_log_dt, s5_b_re, s5_b_im,
                           s5_c_re, s5_c_im, s5_d, w_glu, b_glu, conv_w, conv_b, w_out, ln_g, ln_b)]
    in_maps = _prep_fused_inputs(*args)
    nc = build_fused()
    res = _run(nc, in_maps)
    out = np.concatenate([r["xo"] for r in res], axis=0)
    return np.ascontiguousarray(out[None].astype(np.float32))


def _kernel_unfused_impl(x, c, ctx, c_ctx, w_ada, b_ada, w_in, s5_a_re, s5_a_im, s5_log_dt, s5_b_re, s5_b_im,
           s5_c_re, s5_c_im, s5_d, w_glu, b_glu, conv_w, conv_b, w_out, ln_g, ln_b):
    f = lambda a: np.ascontiguousarray(np.asarray(a, dtype=np.float32))
    x, c, ctx, c_ctx, w_ada, b_ada, w_in = map(f, (x, c, ctx, c_ctx, w_ada, b_ada, w_in))
    s5_a_re, s5_a_im, s5_log_dt, s5_b_re, s5_b_im, s5_c_re, s5_c_im, s5_d = map(
        f, (s5_a_re, s5_a_im, s5_log_dt, s5_b_re, s5_b_im, s5_c_re, s5_c_im, s5_d))
    w_glu, b_glu, conv_w, conv_b, w_out, ln_g, ln_b = map(f, (w_glu, b_glu, conv_w, conv_b, w_out, ln_g, ln_b))
    mod = run_ada(c, c_ctx, w_ada, b_ada)
    xl = x[0]
    cx = ctx[0]
    for l in range(2):
        xs = [np.ascontiguousarray(np.concatenate([xl[k * TL:(k + 1) * TL], cx], axis=0)) for k in range(NC)]
        us = run_tok_A(xs, mod[l], w_in[l])
        u_lat = np.concatenate([u.reshape(DS5, TT)[:, :TL].T for u in us], axis=0)
        u_ctx = us[0].reshape(DS5, TT)[:, TL:].T
        u_full = np.ascontiguousarray(np.concatenate([u_ctx, u_lat], axis=0))
        y = run_s5(u_full, s5_a_re[l], s5_a_im[l], s5_log_dt[l], s5_b_re[l], s5_b_im[l], s5_c_re[l], s5_c_im[l], s5_d[l])
        ys = []
        for k in range(NC):
            yk = np.concatenate([y[CTX + k * TL:CTX + (k + 1) * TL], y[:CTX]], axis=0)
            ys.append(np.ascontiguousarray(yk.T.reshape(8, 128, TT)))
        outs = run_tok_C(xs, ys, mod[l], w_in[l], w_glu[l], b_glu[l], conv_w[l], conv_b[l], w_out[l], ln_g[l], ln_b[l],
                         with_ctx=(l == 0))
        xl = np.concatenate([o[:TL] for o in outs], axis=0)
        if l == 0:
            cx = np.ascontiguousarray(outs[0][TL:])
    return np.ascontiguousarray(xl[None].astype(np.float32))
```

```python
import contextlib
import numpy as np
import ml_dtypes
import concourse.bass as bass
import concourse.mybir as mybir
from concourse.bass_utils import run_bass_kernel_spmd

F32 = mybir.dt.float32
BF16 = mybir.dt.bfloat16
I32 = mybir.dt.int32
ALU = mybir.AluOpType
AF = mybir.ActivationFunctionType
AX = mybir.AxisListType

D = 2048
SEQ = 8192
CTX = 256
NC = 8
TL = SEQ // NC
TT = TL + CTX
DIN = 6144
DS5 = 1024
NG = 64
GP = 8
NCH = (SEQ + CTX) // 8
NB = NCH // 32
ALPHA = (2.0 * 2) ** 0.25
EPS = 1e-6
TWO_PI = 2.0 * np.pi
DEBUG_O = False


class Prog:
    ENG = ("pe", "act", "dve", "pool", "sp")

    def __init__(self, nc, stack):
        self.nc = nc
        self.stack = stack
        self.sem_stack = stack
        self.q = {e: [] for e in self.ENG}
        self.sem = {}
        self.cnt = {}
        self.seen = {e: {} for e in self.ENG}
        self.lastw = {}
        self.readers = {}
        for e in self.ENG:
            self._mksem(e)

    def _mksem(self, name):
        if name not in self.sem:
            self.sem[name] = self.sem_stack.enter_context(self.nc.semaphore("s_" + name))
            self.cnt[name] = 0

    pfx = ""

    def sb(self, name, shape, dtype):
        return self.stack.enter_context(self.nc.sbuf_tensor(self.pfx + name, list(shape), dtype))

    def ps(self, name, shape, dtype=F32):
        return self.stack.enter_context(self.nc.psum_tensor(name, list(shape), dtype))

    def _waits(self, eng, r, w):
        need = {}

        def add(tok, same_ok):
            if tok is None:
                return
            for s, v in tok.items():
                if s == eng and not same_ok:
                    continue
                if need.get(s, 0) < v:
                    need[s] = v

        for k in r:
            add(self.lastw.get(k), True)
        so = eng != "pe"
        for k in w:
            add(self.lastw.get(k), so)
            add(self.readers.get(k), so)
        out = []
        for s, v in need.items():
            if self.seen[eng].get(s, 0) < v:
                self.seen[eng][s] = v
                out.append((s, v))
        return out

    def _commit(self, tok, r, w):
        for k in w:
            self.lastw[k] = dict(tok)
            self.readers[k] = {}
        for k in r:
            d = self.readers.setdefault(k, {})
            for s, v in tok.items():
                if d.get(s, 0) < v:
                    d[s] = v

    def op(self, eng, fn, r=(), w=()):
        waits = self._waits(eng, r, w)
        self.cnt[eng] += 1
        tok = {eng: self.cnt[eng]}
        self.q[eng].append((waits, fn, eng, 1))
        self._commit(tok, r, w)

    def dma(self, q, out, in_, r=(), w=(), sem=None):
        if sem is None:
            self._nu = getattr(self, "_nu", 0) + 1
            sem = f"du{self._nu}"
        self._mksem(sem)
        waits = self._waits(q, r, w)
        self.cnt[sem] += 16
        tok = {sem: self.cnt[sem]}
        self.q[q].append((waits, lambda e, o=out, i=in_: e.dma_start(out=o, in_=i), sem, 16))
        self._commit(tok, r, w)

    def custom(self, q, fn, sem, inc, r=(), w=()):
        self._mksem(sem)
        waits = self._waits(q, r, w)
        self.cnt[sem] += inc
        self.q[q].append((waits, fn, sem, inc))
        self._commit({sem: self.cnt[sem]}, r, w)

    def wait_all(self, eng, keys):
        waits = self._waits(eng, keys, ())
        if waits:
            self.q[eng].append((waits, None, None, 0))

    def emit(self):
        nc = self.nc
        sem = self.sem
        q = self.q
        with nc.Block() as block:
            def run(e, lst):
                for waits, fn, s, n in lst:
                    for ws, wv in waits:
                        e.wait_ge(sem[ws], wv)
                    if fn is not None:
                        fn(e).then_inc(sem[s], n)

            @block.tensor
            def _(e):
                run(e, q["pe"])

            @block.scalar
            def _(e):
                run(e, q["act"])

            @block.vector
            def _(e):
                run(e, q["dve"])

            @block.gpsimd
            def _(e):
                run(e, q["pool"])

            @block.sync
            def _(e):
                run(e, q["sp"])


def _new_nc():
    return bass.Bass("TRN2", target_bir_lowering=False)


def _run(nc, in_maps):
    res = run_bass_kernel_spmd(nc, in_maps, core_ids=list(range(NC)))
    return res.results


ADA_N = 3 * D // NC


def build_ada():
    nc = _new_nc()
    cc = nc.dram_tensor("cc", [128, 16, 2], F32, kind="ExternalInput").ap()
    wa = nc.dram_tensor("wa", [2, 128, 16, ADA_N], F32, kind="ExternalInput").ap()
    ba = nc.dram_tensor("ba", [2, 2, ADA_N], F32, kind="ExternalInput").ap()
    mod = nc.dram_tensor("mod", [2, 2, ADA_N], F32, kind="ExternalOutput").ap()
    with contextlib.ExitStack() as st:
        p = Prog(nc, st)
        cct = p.sb("cct", [128, 16, 2], F32)
        sc = p.sb("sc", [128, 16, 2], BF16)
        sg = p.sb("sg", [128, 16, 2], F32)
        wt = [p.sb(f"wt{l}", [128, 16, ADA_N], BF16) for l in range(2)]
        bt = p.sb("bt", [2, 2, ADA_N], F32)
        ot = p.sb("ot", [2, 2, ADA_N], F32)
        pp = [p.ps(f"pp{i}", [2, 384]) for i in range(4)]
        p.dma("sp", cct[:], cc[:, :, :], w=["cct"])
        p.dma("sp", bt[:], ba.rearrange("l t n -> t l n"), w=["bt"])
        for l in range(2):
            for h in range(2):
                p.dma("pool", wt[l][:, 8 * h:8 * h + 8, :], wa[l, :, 8 * h:8 * h + 8, :], w=[f"wt{l}{h}"], sem=f"dw{l}{h}")
        p.op("act", lambda e: e.activation(out=sg[:], in_=cct[:], func=AF.Sigmoid), r=["cct"], w=["sg"])
        p.op("dve", lambda e: e.tensor_tensor(out=sc[:], in0=cct[:], in1=sg[:], op=ALU.mult), r=["cct", "sg"], w=["sc"])
        for l in range(2):
            for nb in range(2):
                ps = pp[l * 2 + nb]
                for kf in range(16):
                    p.op("pe", lambda e, ps=ps, kf=kf, l=l, nb=nb: e.matmul(
                        ps[:, :], lhsT=sc[:, kf, :], rhs=wt[l][:, kf, nb * 384:(nb + 1) * 384],
                        start=(kf == 0), stop=(kf == 15)),
                        r=["sc", f"wt{l}{kf // 8}"], w=[f"pp{l}{nb}"])
                p.op("dve", lambda e, ps=ps, l=l, nb=nb: e.tensor_tensor(
                    out=ot[:, l, nb * 384:(nb + 1) * 384], in0=ps[:, :], in1=bt[:, l, nb * 384:(nb + 1) * 384], op=ALU.add),
                    r=[f"pp{l}{nb}", "bt"], w=["ot"])
        p.dma("sp", mod.rearrange("l t n -> t l n"), ot[:], r=["ot"], w=["mod"], sem="dout")
        p.wait_all("sp", ["mod"])
        p.emit()
    return nc


def run_ada(c, c_ctx, w_ada, b_ada):
    nc = build_ada()
    cc = np.stack([c.reshape(D), c_ctx.reshape(D)], axis=-1)
    cc = np.ascontiguousarray(cc.reshape(16, 128, 2).transpose(1, 0, 2))
    wr = w_ada.reshape(2, 16, 128, NC, ADA_N)
    in_maps = []
    for k in range(NC):
        wa = np.ascontiguousarray(wr[:, :, :, k, :].transpose(0, 2, 1, 3))
        ba = np.ascontiguousarray(np.broadcast_to(b_ada.reshape(2, 1, NC, ADA_N)[:, :, k, :], (2, 2, ADA_N)))
        in_maps.append({"cc": cc, "wa": wa, "ba": ba})
    res = _run(nc, in_maps)
    mod = np.concatenate([r["mod"] for r in res], axis=-1)
    return mod


def _maybe_stack(cond):
    if cond:
        with contextlib.ExitStack() as s_:
            yield s_


def _barrier(p):
    snap = dict(p.cnt)
    for e in p.ENG:
        waits = []
        for s, v in snap.items():
            if s == "cc":
                continue
            if v > 0 and p.seen[e].get(s, 0) < v and s != e:
                p.seen[e][s] = v
                waits.append((s, v))
        if waits:
            p.q[e].append((waits, None, None, 0))


def build_tok(mode, with_ctx):
    ntok = TT if with_ctx else TL
    nt_all = TT // 128
    nc = _new_nc()
    xin = nc.dram_tensor("xin", [TT, D], F32, kind="ExternalInput").ap()
    ms = nc.dram_tensor("ms", [128, 16, 4], F32, kind="ExternalInput").ap()
    win = nc.dram_tensor("win", [16, 128, DIN], F32, kind="ExternalInput").ap()
    ident = nc.dram_tensor("ident", [128, 128], F32, kind="ExternalInput").ap()
    if mode == "A":
        uo = nc.dram_tensor("uo", [8, 128, TT], BF16, kind="ExternalOutput").ap()
    else:
        yin = nc.dram_tensor("yin", [8, 128, TT], BF16, kind="ExternalInput").ap()
        gbc = nc.dram_tensor("gbc", [128, 2, D], F32, kind="ExternalInput").ap()
        lnb = nc.dram_tensor("lnb", [128, 2, D], F32, kind="ExternalInput").ap()
        pv = nc.dram_tensor("pv", [128, 8, 5], F32, kind="ExternalInput").ap()
        wglu = nc.dram_tensor("wglu", [8, 128, DS5], F32, kind="ExternalInput").ap()
        wout = nc.dram_tensor("wout", [16, 128, D], F32, kind="ExternalInput").ap()
        xo = nc.dram_tensor("xo", [ntok, D], F32, kind="ExternalOutput").ap()
        if DEBUG_O:
            odbg = nc.dram_tensor("odbg", [16, 128, TT], BF16, kind="ExternalOutput").ap()

    with contextlib.ExitStack() as st:
        p = Prog(nc, st)
        idf = p.sb("idf", [128, 128], F32)
        idb = p.sb("idb", [128, 128], BF16)
        mst = p.sb("mst", [128, 16, 4], F32)
        hT = p.sb("hT", [128, 16, TT], BF16)
        acc = [p.ps(f"acc{i}", [128, 512]) for i in range(4)]
        tps = [p.ps(f"tp{i}", [128, 512], BF16) for i in range(2)]
        p.dma("sp", idf[:], ident[:, :], w=["idf"])
        p.dma("sp", mst[:], ms[:, :, :], w=["mst"])
        p.op("dve", lambda e: e.tensor_copy(out=idb[:], in_=idf[:]), r=["idf"], w=["idb"])
        for j in (1, 3):
            p.op("dve", lambda e, j=j: e.tensor_scalar(out=mst[:, :, j], in0=mst[:, :, j], scalar1=1.0, scalar2=None, op0=ALU.add),
                 r=["mst"], w=["mst"])

        F = dict(p=p, st=st, xin=xin, win=win, idb=idb, mst=mst, hT=hT, acc=acc, tps=tps, do_ln=True, standalone=True)
        if mode == "A":
            F.update(uo=uo)
        else:
            F.update(yin=yin, gbc=gbc, lnb=lnb, pv=pv, wglu=wglu, wout=wout, xo=xo, odbg=(odbg if DEBUG_O else None))
        _tok_body(mode, with_ctx, F)
        p.emit()
    return nc


def _tok_body(mode, with_ctx, F):
    ntok = TT if with_ctx else TL
    nt_all = TT // 128
    p = F["p"]; st = F["st"]; xin = F["xin"]; win = F["win"]; idb = F["idb"]; mst = F["mst"]; hT = F["hT"]
    acc = F["acc"]; tps = F["tps"]; standalone = F["standalone"]
    sfx = F.get("sfx", "")
    if True:
        with contextlib.ExitStack() as st1:
            p.stack = st1
            NBUF = 3
            xt = [p.sb(f"xt{i}", [128, D], F32) for i in range(NBUF)]
            nt_ln = nt_all if F["do_ln"] else 0
            xn = [p.sb(f"xn{i}", [128, D], BF16) for i in range(NBUF)]
            stt = [p.sb(f"stt{i}", [128, 4, 6], F32) for i in range(NBUF)]
            mv = [p.sb(f"mv{i}", [128, 2], F32) for i in range(NBUF)]
            rs = [p.sb(f"rs{i}", [128, 1], F32) for i in range(NBUF)]
            def ln_s1(t):
                b = t % NBUF
                p.dma("sp", xt[b][:], xin[t * 128:(t + 1) * 128, :], w=[f"xt{b}"], sem=f"dx{b}")
                for c4 in range(4):
                    p.op("dve", lambda e, b=b, c4=c4: e.bn_stats(out=stt[b][:, c4, :], in_=xt[b][:, c4 * 512:(c4 + 1) * 512]),
                         r=[f"xt{b}"], w=[f"stt{b}"])
                p.op("dve", lambda e, b=b: e.bn_aggr(out=mv[b][:], in_=stt[b][:].rearrange("p a s -> p (a s)")),
                     r=[f"stt{b}"], w=[f"mv{b}"])
                p.op("act", lambda e, b=b: e.activation(out=rs[b][:], in_=mv[b][:, 1:2], func=AF.Sqrt, bias=EPS, scale=1.0),
                     r=[f"mv{b}"], w=[f"rs{b}"])

            def ln_s2(t):
                b = t % NBUF
                p.op("dve", lambda e, b=b: e.reciprocal(out=rs[b][:], in_=rs[b][:]), r=[f"rs{b}"], w=[f"rs{b}"])
                p.op("dve", lambda e, b=b: e.tensor_scalar(out=xn[b][:], in0=xt[b][:], scalar1=mv[b][:, 0:1], scalar2=rs[b][:, 0:1],
                                                           op0=ALU.subtract, op1=ALU.mult),
                     r=[f"xt{b}", f"mv{b}", f"rs{b}"], w=[f"xn{b}"])

            def ln_s3(t):
                b = t % NBUF
                which = 0 if t < TL // 128 else 1
                for k4 in range(4):
                    tp = tps[k4 % 2]
                    for j in range(4):
                        kf = k4 * 4 + j
                        p.op("pe", lambda e, tp=tp, j=j, kf=kf, b=b: e.transpose(
                            out=tp[:, j * 128:(j + 1) * 128], in_=xn[b][:, kf * 128:(kf + 1) * 128], identity=idb[:]),
                            r=[f"xn{b}", "idb"], w=[f"tp{k4 % 2}"])
                    for j in range(4):
                        kf = k4 * 4 + j
                        if j % 2 == 0:
                            p.op("act", lambda e, tp=tp, j=j, kf=kf, t=t, which=which: e.activation(
                                out=hT[:, kf, t * 128:(t + 1) * 128], in_=tp[:, j * 128:(j + 1) * 128], func=AF.Identity,
                                scale=mst[:, kf, 2 * which + 1:2 * which + 2], bias=mst[:, kf, 2 * which:2 * which + 1]),
                                r=[f"tp{k4 % 2}", "mst"], w=[f"hT{t}"])
                        else:
                            p.op("dve", lambda e, tp=tp, j=j, kf=kf, t=t, which=which: e.tensor_scalar(
                                out=hT[:, kf, t * 128:(t + 1) * 128], in0=tp[:, j * 128:(j + 1) * 128],
                                scalar1=mst[:, kf, 2 * which + 1:2 * which + 2], scalar2=mst[:, kf, 2 * which:2 * which + 1],
                                op0=ALU.mult, op1=ALU.add),
                                r=[f"tp{k4 % 2}", "mst"], w=[f"hT{t}"])

            for step in range(nt_ln + 2 if nt_ln else 0):
                if step < nt_ln:
                    ln_s1(step)
                if 0 <= step - 1 < nt_ln:
                    ln_s2(step - 1)
                if 0 <= step - 2 < nt_ln:
                    ln_s3(step - 2)
            _barrier(p)
        p.stack = st

        blocks = [(0, 512), (512, 512)] + ([(1024, 256)] if True else [])
        hkeys = lambda t0, n: [f"hT{t}" for t in range(t0 // 128, (t0 + n) // 128)]

        wq = {"n": 0}

        def load_w(wb, col0, ncol=256):
            i = wq["n"] % len(wb)
            if F.get("hwcast"):
                j = wq["n"] % 2
                wq["n"] += 1
                wst = F["wst"]
                p.dma("sp", wst[j][:, :, 0:ncol], win[:, :, col0:col0 + ncol].rearrange("k p n -> p k n"), w=[f"wst{j}"], sem=f"dws{j}")
                p.op("act", lambda e, i=i, j=j: e.activation(out=wb[i][:, :, 0:ncol], in_=wst[j][:, :, 0:ncol], func=AF.Identity),
                     r=[f"wst{j}"], w=[f"wb{i}"])
                return i
            wq["n"] += 1
            p.dma("pool", wb[i][:, :, 0:ncol], win[:, :, col0:col0 + ncol].rearrange("k p n -> p k n"),
                  w=[f"wb{i}"], sem=f"dwb{i}")
            return i

        def proj(wbt, wkey, cl, t0, n, ps, pkey):
            for kf in range(16):
                p.op("pe", lambda e, kf=kf: e.matmul(ps[:, 0:n], lhsT=wbt[:, kf, cl * 128:(cl + 1) * 128],
                                                      rhs=hT[:, kf, t0:t0 + n], start=(kf == 0), stop=(kf == 15)),
                     r=[wkey] + hkeys(t0, n), w=[pkey])

        pa = {"n": 0}

        def next_acc():
            i = pa["n"] % 4
            pa["n"] += 1
            return acc[i], f"acc{i}"

        if mode == "A":
            with contextlib.ExitStack() as st2:
                p.stack = st2
                wb = [p.sb(f"wb{i}", [128, 16, 256], BF16) for i in range(2 if standalone else 4)]
                ut = p.sb("ut", [128, 8, TT], BF16)
                if not standalone:
                    F["utp"] = p.sb("utp", [128, 8, TT], BF16)
                pre = [load_w(wb, pair * 256) for pair in range(4)] if not standalone else None
                for pair in range(4):
                    i = pre[pair] if pre is not None else load_w(wb, pair * 256)
                    for cl in range(2):
                        ch = pair * 2 + cl
                        for bi, (t0, n) in enumerate(blocks):
                            ps, pk = next_acc()
                            proj(wb[i], f"wb{i}", cl, t0, n, ps, pk)
                            eng = "act" if (bi % 2 == 0) else "dve"
                            if eng == "act":
                                p.op("act", lambda e, ps=ps, ch=ch, t0=t0, n=n: e.activation(out=ut[:, ch, t0:t0 + n], in_=ps[:, 0:n], func=AF.Identity),
                                     r=[pk], w=[f"ut{ch}"])
                            else:
                                p.op("dve", lambda e, ps=ps, ch=ch, t0=t0, n=n: e.tensor_copy(out=ut[:, ch, t0:t0 + n], in_=ps[:, 0:n]),
                                     r=[pk], w=[f"ut{ch}"])
                        if standalone:
                            p.dma("sp", F["uo"][ch, :, :], ut[:, ch, :], r=[f"ut{ch}"], w=[f"uo{ch}"], sem="dout")
                        else:
                            utp = F["utp"]
                            p.op("act", lambda e, ch=ch: e.activation(out=utp[:, ch, 0:TL].rearrange("p (j c) -> p c j", j=8),
                                                                      in_=ut[:, ch, 0:TL].rearrange("p (c j) -> p c j", j=8), func=AF.Identity),
                                 r=[f"ut{ch}"], w=[f"utp{ch}"])
                            p.op("dve", lambda e, ch=ch: e.tensor_copy(out=utp[:, ch, TL:TT].rearrange("p (j c) -> p c j", j=8),
                                                                       in_=ut[:, ch, TL:TT].rearrange("p (c j) -> p c j", j=8)),
                                 r=[f"ut{ch}"], w=[f"utp{ch}"])
                            p.dma("sp", F["usend"][ch * 128:(ch + 1) * 128, :], utp[:, ch, 0:TL], r=[f"utp{ch}"], w=[f"usend{ch}"])
                            p.dma("act", F["uctx"][ch * 128:(ch + 1) * 128, :], utp[:, ch, TL:TT], r=[f"utp{ch}"], w=["uctx"])
                            if ch in (2, 5, 7):
                                F["cc_piece"]({2: 0, 5: 1, 7: 2}[ch])
                if standalone:
                    p.wait_all("sp", [f"uo{ch}" for ch in range(8)])
                _barrier(p)
            p.stack = st
            return

        part = F.get("part", 0)
        if "o" in F:
            o = F["o"]
            pvt = F["pvt"]
        else:
            o = p.sb("o", [128, 16, TT], BF16)
            pvt = p.sb("pvt", [128, 8, 5], F32)
            p.dma("sp", pvt[:], F["pv"][:, :, :], w=["pvt"])
        n1 = 0 if part == 2 else 1
        n2 = 0 if part == 1 else 1
        blk = blocks if with_ctx else blocks[:2]
        wo_early = None
        if part == 2:
            wo_early = p.sb("wo", [128, 16, D], BF16)
        with contextlib.ExitStack() as st2:
            p.stack = st2
            wb = [p.sb(f"wb{i}", [128, 16, 256], BF16) for i in range(3 * n1)]
            if F.get("hwcast"):
                F["wst"] = [p.sb(f"wst{i}", [128, 16, 256], F32) for i in range(2)]
            gel = p.sb("gel", [128, 8, TT], BF16) if n2 else None
            wg = p.sb("wg", [128, 8, DS5], BF16) if n2 else None
            vt = [p.sb(f"vt{i}", [128, 512], BF16) for i in range(2 * n1)]
            s_t = [p.sb(f"s_t{i}", [128, 512], F32) for i in range(2 * n1)]
            a_t = [p.sb(f"a_t{i}", [128, 512], F32) for i in range(2 * n1)]
            sz = [p.sb(f"sz{i}", [128, 512], BF16) for i in range(2 * n1)]
            if standalone:
                for ch in range(8):
                    p.dma("sp", gel[:, ch, :], F["yin"][ch, :, :], w=[f"gel{ch}"])
            elif n2:
                F["load_gel"](gel)
                if wo_early is not None:
                    for h in range(4):
                        p.dma("pool", wo_early[:, 4 * h:4 * h + 4, :], F["wout"][4 * h:4 * h + 4, :, :].rearrange("k p n -> p k n"), w=[f"wo{h}"], sem=f"dwo{h}")
            ytmp = [p.sb(f"ytmp{i}", [128, 512], F32) for i in range(2 * n2)]
            ysq = [p.sb(f"ysq{i}", [128, 512], F32) for i in range(2 * n2)]
            sgt = [p.sb(f"sgt{i}", [128, 512], BF16) for i in range(2 * n2)]
            for h in range(2 * n2):
                p.dma("pool", wg[:, 4 * h:4 * h + 4, :], F["wglu"][4 * h:4 * h + 4, :, :].rearrange("k p n -> p k n"), w=[f"wg{h}"], sem=f"dwg{h}")
            for pair in range(4 * n1):
                i = load_w(wb, DS5 + pair * 256)
                for cl in range(2):
                    ch = pair * 2 + cl
                    for (t0, n) in blk:
                        ps, pk = next_acc()
                        proj(wb[i], f"wb{i}", cl, t0, n, ps, pk)
                        b2 = pa["n"] % 2
                        p.op("act", lambda e, ps=ps, b2=b2, n=n: e.activation(out=sz[b2][:, 0:n], in_=ps[:, 0:n], func=AF.Sigmoid),
                             r=[pk], w=[f"sz{b2}"])
                        p.op("dve", lambda e, ps=ps, b2=b2, ch=ch, t0=t0, n=n: e.tensor_tensor(out=o[:, ch, t0:t0 + n], in0=ps[:, 0:n], in1=sz[b2][:, 0:n], op=ALU.mult),
                             r=[pk, f"sz{b2}"], w=[f"o{ch}"])
            if DEBUG_O == 2:
                for k in range(8):
                    p.dma("sp", F["odbg"][k, :, :], o[:, k, :], r=[f"o{k}"], w=[f"odbg{k}"])
            if n1 and "mid_hook" in F:
                F["mid_hook"]()
            it = 0
            for pair in range(4 * n1):
                iv = load_w(wb, 2 * DS5 + pair * 256)
                ic = load_w(wb, 2 * DS5 + 2 * 1024 + pair * 256)
                for cl in range(2):
                    k = pair * 2 + cl
                    if cl == 0:
                        pass
                    for (t0, n) in blk:
                        b2 = it % 2
                        it += 1
                        rl = 64 if t0 < TL else 256
                        ps, pk = next_acc()
                        proj(wb[iv], f"wb{iv}", cl, t0, n, ps, pk)
                        p.op("act", lambda e, ps=ps, b2=b2, n=n: e.activation(out=vt[b2][:, 0:n], in_=ps[:, 0:n], func=AF.Identity),
                             r=[pk], w=[f"vt{b2}"])
                        ps, pk = next_acc()
                        proj(wb[ic], f"wb{ic}", cl, t0, n, ps, pk)
                        p.op("dve", lambda e, ps=ps, b2=b2, n=n: e.tensor_tensor(out=s_t[b2][:, 0:n], in0=ps[:, 0:n], in1=vt[b2][:, 0:n], op=ALU.mult),
                             r=[pk, f"vt{b2}"], w=[f"s_t{b2}"])
                        p.op("dve", lambda e, b2=b2, n=n, k=k: e.tensor_scalar(out=a_t[b2][:, 0:n], in0=s_t[b2][:, 0:n], scalar1=pvt[:, k, 2:3], scalar2=pvt[:, k, 4:5],
                                                                              op0=ALU.mult, op1=ALU.add),
                             r=[f"s_t{b2}", "pvt"], w=[f"a_t{b2}"])
                        sv = s_t[b2][:, 0:n].rearrange("p (r c) -> p r c", c=rl)
                        av = a_t[b2][:, 0:n].rearrange("p (r c) -> p r c", c=rl)
                        p.op("dve", lambda e, sv=sv, av=av, k=k, rl=rl: e.scalar_tensor_tensor(
                            out=av[:, :, 1:rl], in0=sv[:, :, 0:rl - 1], scalar=pvt[:, k, 1:2], in1=av[:, :, 1:rl], op0=ALU.mult, op1=ALU.add),
                            r=[f"s_t{b2}", f"a_t{b2}", "pvt"], w=[f"a_t{b2}"])
                        p.op("dve", lambda e, sv=sv, av=av, k=k, rl=rl: e.scalar_tensor_tensor(
                            out=av[:, :, 0:rl - 1], in0=sv[:, :, 1:rl], scalar=pvt[:, k, 3:4], in1=av[:, :, 0:rl - 1], op0=ALU.mult, op1=ALU.add),
                            r=[f"s_t{b2}", f"a_t{b2}", "pvt"], w=[f"a_t{b2}"])
                        p.op("dve", lambda e, b2=b2, n=n, k=k, t0=t0: e.tensor_copy(out=o[:, 8 + k, t0:t0 + n], in_=a_t[b2][:, 0:n]),
                             r=[f"a_t{b2}"], w=[f"o{8 + k}"])
            it = 0
            for pair in range(4 * n1):
                ib = load_w(wb, 2 * DS5 + 1024 + pair * 256)
                iz = load_w(wb, 2 * DS5 + 3 * 1024 + pair * 256)
                for cl in range(2):
                    k = pair * 2 + cl
                    for (t0, n) in blk:
                        b2 = it % 2
                        it += 1
                        ps, pk = next_acc()
                        proj(wb[iz], f"wb{iz}", cl, t0, n, ps, pk)
                        p.op("act", lambda e, ps=ps, b2=b2, n=n: e.activation(out=sz[b2][:, 0:n], in_=ps[:, 0:n], func=AF.Sigmoid),
                             r=[pk], w=[f"sz{b2}"])
                        p.op("dve", lambda e, ps=ps, b2=b2, n=n: e.tensor_tensor(out=sz[b2][:, 0:n], in0=ps[:, 0:n], in1=sz[b2][:, 0:n], op=ALU.mult),
                             r=[pk, f"sz{b2}"], w=[f"sz{b2}"])
                        ps, pk = next_acc()
                        proj(wb[ib], f"wb{ib}", cl, t0, n, ps, pk)
                        p.op("dve", lambda e, ps=ps, b2=b2, n=n, k=k, t0=t0: e.tensor_tensor(out=vt[b2][:, 0:n], in0=ps[:, 0:n], in1=o[:, 8 + k, t0:t0 + n], op=ALU.mult),
                             r=[pk, f"o{8 + k}"], w=[f"vt{b2}"])
                        p.op("dve", lambda e, b2=b2, n=n, k=k, t0=t0: e.tensor_tensor(out=o[:, 8 + k, t0:t0 + n], in0=vt[b2][:, 0:n], in1=sz[b2][:, 0:n], op=ALU.mult),
                             r=[f"vt{b2}", f"sz{b2}"], w=[f"o{8 + k}"])
            itg = {"n": 0}

            def gelu_blk(t0, n):
                for ch in range(8):
                    b2 = itg["n"] % 2
                    itg["n"] += 1
                    g = gel[:, ch, t0:t0 + n]
                    p.op("act", lambda e, g=g, b2=b2, n=n: e.activation(out=ysq[b2][:, 0:n], in_=g, func=AF.Square),
                         r=[f"gel{ch}"], w=[f"ysq{b2}"])
                    p.op("dve", lambda e, b2=b2, n=n: e.tensor_scalar(out=ysq[b2][:, 0:n], in0=ysq[b2][:, 0:n], scalar1=0.044715, scalar2=1.0, op0=ALU.mult, op1=ALU.add),
                         r=[f"ysq{b2}"], w=[f"ysq{b2}"])
                    p.op("dve", lambda e, g=g, b2=b2, n=n: e.tensor_tensor(out=ytmp[b2][:, 0:n], in0=ysq[b2][:, 0:n], in1=g, op=ALU.mult),
                         r=[f"ysq{b2}", f"gel{ch}"], w=[f"ytmp{b2}"])
                    p.op("act", lambda e, b2=b2, n=n: e.activation(out=ytmp[b2][:, 0:n], in_=ytmp[b2][:, 0:n], func=AF.Sigmoid, scale=1.5957691216057308),
                         r=[f"ytmp{b2}"], w=[f"ytmp{b2}"])
                    p.op("dve", lambda e, g=g, b2=b2, n=n: e.tensor_tensor(out=g, in0=ytmp[b2][:, 0:n], in1=g, op=ALU.mult),
                         r=[f"ytmp{b2}", f"gel{ch}"], w=[f"gelb{ch}_{t0}"])
            itu = {"n": 0}

            def glu_blk(t0, n):
                for nch in range(8):
                    b2 = itu["n"] % 2
                    itu["n"] += 1
                    ps, pk = next_acc()
                    for kf in range(8):
                        p.op("pe", lambda e, ps=ps, kf=kf, nch=nch, t0=t0, n=n: e.matmul(
                            ps[:, 0:n], lhsT=wg[:, kf, nch * 128:(nch + 1) * 128], rhs=gel[:, kf, t0:t0 + n], start=(kf == 0), stop=(kf == 7)),
                            r=[f"wg{kf // 4}", f"gelb{kf}_{t0}"], w=[pk])
                    p.op("act", lambda e, ps=ps, b2=b2, n=n, nch=nch: e.activation(out=sgt[b2][:, 0:n], in_=ps[:, 0:n], func=AF.Sigmoid, bias=pvt[:, nch, 0:1], scale=1.0),
                         r=[pk, "pvt"], w=[f"sgt{b2}"])
                    p.op("dve", lambda e, b2=b2, n=n, nch=nch, t0=t0: e.tensor_tensor(out=sgt[b2][:, 0:n], in0=sgt[b2][:, 0:n], in1=gel[:, nch, t0:t0 + n], op=ALU.mult),
                         r=[f"sgt{b2}", f"gelb{nch}_{t0}"], w=[f"sgt{b2}"])
                    p.op("dve", lambda e, b2=b2, n=n, nch=nch, t0=t0: e.tensor_tensor(out=o[:, nch, t0:t0 + n], in0=sgt[b2][:, 0:n], in1=o[:, nch, t0:t0 + n], op=ALU.mult),
                         r=[f"sgt{b2}", f"o{nch}"], w=[f"o{nch}"])
            if n2:
                gelu_blk(*blk[0])
                for bi_ in range(len(blk)):
                    if bi_ + 1 < len(blk):
                        gelu_blk(*blk[bi_ + 1])
                    glu_blk(*blk[bi_])
            if DEBUG_O:
                for k in range(16 if DEBUG_O == 1 else 0):
                    p.dma("sp", F["odbg"][k, :, :], o[:, k, :], r=[f"o{k}"], w=[f"odbg{k}"])
                p.wait_all("sp", [f"odbg{k}" for k in range(16 if DEBUG_O == 1 else 8)])
            _barrier(p)
        p.stack = st
        if part == 1:
            return
        with contextlib.ExitStack() as st3:
            p.stack = st3
            wo = wo_early if wo_early is not None else p.sb("wo", [128, 16, D], BF16)
            gb = p.sb("gb", [128, 2, D], F32)
            lb = p.sb("lb", [128, 2, D], F32)
            xt2 = [p.sb("x2t0", [128, D], F32)] * 2
            vv = [p.sb(f"vv{i}", [128, D], F32) for i in range(2)]
            stt2 = [p.sb(f"st2{i}", [128, 4, 6], F32) for i in range(2)]
            mv2 = [p.sb(f"mv2{i}", [128, 2], F32) for i in range(2)]
            rs2 = [p.sb(f"rs2{i}", [128, 1], F32) for i in range(2)]
            for h in range(4 if wo_early is None else 0):
                p.dma("pool", wo[:, 4 * h:4 * h + 4, :], F["wout"][4 * h:4 * h + 4, :, :].rearrange("k p n -> p k n"), w=[f"wo{h}"], sem=f"dwo{h}")
            if standalone:
                p.dma("sp", gb[:], F["gbc"][:, :, :], w=["gb"])
            else:
                F["make_gb"](gb)
            p.dma("sp", lb[:], F["lnb"][:, :, :], w=["lb"])
            okeys = [f"o{k}" for k in range(16)]
            for t in range(ntok // 128):
                b = t % 2
                which = 0 if t < TL // 128 else 1
                p.dma("sp", xt2[b][:], xin[t * 128:(t + 1) * 128, :], w=["x2t"], sem="dx2")
                for nb in range(4):
                    ps, pk = next_acc()
                    for kf in range(16):
                        p.op("pe", lambda e, ps=ps, kf=kf, t=t, nb=nb: e.matmul(
                            ps[:, :], lhsT=o[:, kf, t * 128:(t + 1) * 128], rhs=wo[:, kf, nb * 512:(nb + 1) * 512], start=(kf == 0), stop=(kf == 15)),
                            r=[f"o{kf}", f"wo{kf // 4}"], w=[pk])
                    p.op("dve", lambda e, ps=ps, b=b, nb=nb, which=which: e.tensor_tensor(
                        out=vv[b][:, nb * 512:(nb + 1) * 512], in0=ps[:, :], in1=gb[:, which, nb * 512:(nb + 1) * 512], op=ALU.mult),
                        r=[pk, "gb"], w=[f"vv{b}"])
                    p.op("dve", lambda e, b=b, nb=nb: e.scalar_tensor_tensor(
                        out=vv[b][:, nb * 512:(nb + 1) * 512], in0=xt2[b][:, nb * 512:(nb + 1) * 512], scalar=ALPHA, in1=vv[b][:, nb * 512:(nb + 1) * 512],
                        op0=ALU.mult, op1=ALU.add),
                        r=["x2t", f"vv{b}"], w=[f"vv{b}"])
                    p.op("dve", lambda e, b=b, nb=nb: e.bn_stats(out=stt2[b][:, nb, :], in_=vv[b][:, nb * 512:(nb + 1) * 512]),
                         r=[f"vv{b}"], w=[f"st2{b}"])
                p.op("dve", lambda e, b=b: e.bn_aggr(out=mv2[b][:], in_=stt2[b][:].rearrange("p a s -> p (a s)")),
                     r=[f"st2{b}"], w=[f"mv2{b}"])
                p.op("act", lambda e, b=b: e.activation(out=rs2[b][:], in_=mv2[b][:, 1:2], func=AF.Sqrt, bias=EPS, scale=1.0),
                     r=[f"mv2{b}"], w=[f"rs2{b}"])
                p.op("dve", lambda e, b=b: e.reciprocal(out=rs2[b][:], in_=rs2[b][:]), r=[f"rs2{b}"], w=[f"rs2{b}"])
                p.op("dve", lambda e, b=b: e.tensor_scalar(out=vv[b][:], in0=vv[b][:], scalar1=mv2[b][:, 0:1], scalar2=rs2[b][:, 0:1],
                                                           op0=ALU.subtract, op1=ALU.mult),
                     r=[f"vv{b}", f"mv2{b}", f"rs2{b}"], w=[f"vv{b}"])
                p.op("dve", lambda e, b=b: e.tensor_tensor(out=vv[b][:], in0=vv[b][:], in1=lb[:, 0, :], op=ALU.mult),
                     r=[f"vv{b}", "lb"], w=[f"vv{b}"])
                p.op("dve", lambda e, b=b: e.tensor_tensor(out=vv[b][:], in0=vv[b][:], in1=lb[:, 1, :], op=ALU.add),
                     r=[f"vv{b}", "lb"], w=[f"vv{b}"])
                p.dma("sp", (F["xo"] if t < TL // 128 else F.get("xo_ctx", F["xo"]))[t * 128:(t + 1) * 128, :], vv[b][:], r=[f"vv{b}"], w=[f"xo{t}"], sem=f"do{b}")
            p.wait_all("sp", [f"xo{t}" for t in range(ntok // 128)])
            _barrier(p)
        p.stack = st


_IDENT = np.eye(128, dtype=np.float32)


def _ms_layout(mod_l):
    out = np.empty((128, 16, 4), np.float32)
    for w in range(2):
        out[:, :, 2 * w] = mod_l[w, 0:D].reshape(16, 128).T
        out[:, :, 2 * w + 1] = mod_l[w, D:2 * D].reshape(16, 128).T
    return out


def run_tok_A(xs, mod_l, w_in_l, with_ctx=True):
    nc = build_tok("A", True)
    ms = _ms_layout(mod_l)
    win = w_in_l.reshape(16, 128, DIN)
    in_maps = [{"xin": xs[k], "ms": ms, "win": win, "ident": _IDENT} for k in range(NC)]
    res = _run(nc, in_maps)
    return [r["uo"] for r in res]


def run_tok_C(xs, ys, mod_l, w_in_l, w_glu_l, b_glu_l, conv_w_l, conv_b_l, w_out_l, ln_g_l, ln_b_l, with_ctx):
    nc = build_tok("C", with_ctx)
    ms = _ms_layout(mod_l)
    win = w_in_l.reshape(16, 128, DIN)
    gbc = np.ascontiguousarray(np.broadcast_to(mod_l[:, 2 * D:3 * D][None, :, :], (128, 2, D)))
    lnb = np.ascontiguousarray(np.broadcast_to(np.stack([ln_g_l, ln_b_l])[None, :, :], (128, 2, D)))
    pv = np.empty((128, 8, 5), np.float32)
    pv[:, :, 0] = b_glu_l.reshape(8, 128).T
    for j in range(3):
        pv[:, :, 1 + j] = conv_w_l[j].reshape(8, 128).T
    pv[:, :, 4] = conv_b_l.reshape(8, 128).T
    wglu = w_glu_l.reshape(8, 128, DS5)
    wout = w_out_l.reshape(16, 128, D)
    in_maps = [{"xin": xs[k], "ms": ms, "win": win, "ident": _IDENT, "yin": ys[k], "gbc": gbc, "lnb": lnb,
                "pv": pv, "wglu": wglu, "wout": wout} for k in range(NC)]
    res = _run(nc, in_maps)
    if DEBUG_O:
        return [r["xo"] for r in res], [r["odbg"] for r in res]
    return [r["xo"] for r in res]


SEGS = [(0, 32), (32, 512), (544, 512)]


def build_s5(stage=9):
    nc = _new_nc()
    u2 = nc.dram_tensor("u2", [GP, 128, NCH], BF16, kind="ExternalInput").ap()
    par = nc.dram_tensor("par", [128, GP, 3], F32, kind="ExternalInput").ap()
    bri = nc.dram_tensor("bri", [128, GP, 16, 2], F32, kind="ExternalInput").ap()
    cri = nc.dram_tensor("cri", [128, GP, 16, 2], F32, kind="ExternalInput").ap()
    dcol = nc.dram_tensor("dcol", [128, GP], F32, kind="ExternalInput").ap()
    cst = nc.dram_tensor("cst", [128, 3, 128], F32, kind="ExternalInput").ap()
    y2 = nc.dram_tensor("y2", [GP, 128, NCH], BF16, kind="ExternalOutput").ap()

    with contextlib.ExitStack() as st:
        p = Prog(nc, st)
        pt = [p.ps(f"pt{i}", [128, 512]) for i in range(2)]
        psS = [p.ps(f"psS{i}", [128, 512]) for i in range(2)]
        psY = [p.ps(f"psY{i}", [128, 512]) for i in range(2)]
        F = dict(p=p, st=st, standalone=True, stage=stage, u2=u2, par=par, bri=bri, cri=cri, dcol=dcol, cst=cst, y2=y2,
                 pt=pt, psS=psS, psY=psY, ptk=["pt0", "pt1"], psSk=["psS0", "psS1"], psYk=["psY0", "psY1"])
        _s5_body(F)
        p.emit()
    return nc


def _s5_body(F):
    p = F["p"]; st = F["st"]; stage = F["stage"]; standalone = F["standalone"]
    par = F["par"]; bri = F["bri"]; cri = F["cri"]; dcol = F["dcol"]; cst = F["cst"]
    phase = F.get("phase", "all")
    if True:
        if phase != "main" and "wts" not in F:
            p.stack = F.get("wstack", st)
            BS = p.sb("BS", [128, GP, 2, 128], BF16)
            CAf = p.sb("CAf", [128, GP, 2, 128], BF16)
            CAb = p.sb("CAb", [128, GP, 2, 128], BF16)
            Tbf = p.sb("Tbf", [128, GP, 128], BF16)
            A1 = p.sb("A1", [128, GP, 2], F32)
            A2 = p.sb("A2", [128, GP, 2], F32)
            B1 = p.sb("B1", [128, GP, 2], F32)
            B2 = p.sb("B2", [128, GP, 2], F32)
            TAB1 = p.sb("TAB1", [128, GP, 32, 2], F32)
            TAB2 = p.sb("TAB2", [128, GP, 32, 2], F32)
            cs = p.sb("cs", [128, 3, 128], F32)
            dc = p.sb("dc", [128, GP], F32)
            F["wts"] = (BS, CAf, CAb, Tbf, A1, A2, B1, B2, TAB1, TAB2, cs, dc)
            p.stack = st
        else:
            (BS, CAf, CAb, Tbf, A1, A2, B1, B2, TAB1, TAB2, cs, dc) = F["wts"]
        if phase == "alloc":
            return
        if phase not in ("derive",):
            U = p.sb("U", [128, GP, NCH], BF16)
            G = p.sb("G", [128, GP, NB, 2], F32)
        pt = F["pt"]
        psS = F["psS"]
        psY = F["psY"]
        ptk = F["ptk"]; psSk = F["psSk"]; psYk = F["psYk"]
        if standalone:
            for g in range(GP):
                p.dma("sp", U[:, g, :], F["u2"][g, :, :], w=[f"U{g}"])
        if phase in ("all", "derive"):
            p.dma("sp", cs[:], cst[:, :, :], w=["cs"])
            p.dma("sp", dc[:], dcol[:, :], w=["dc"])

        def tt(eng, out, a, b, op, r, w):
            p.op(eng, lambda e: e.tensor_tensor(out=out, in0=a, in1=b, op=op), r=r, w=w)

        def ts(eng, out, a, s1, s2, op0, op1, r, w):
            if s2 is None:
                p.op(eng, lambda e: e.tensor_scalar(out=out, in0=a, scalar1=s1, scalar2=None, op0=op0), r=r, w=w)
            else:
                p.op(eng, lambda e: e.tensor_scalar(out=out, in0=a, scalar1=s1, scalar2=s2, op0=op0, op1=op1), r=r, w=w)

        def cp(eng, out, a, r, w):
            if eng == "act":
                p.op(eng, lambda e: e.activation(out=out, in_=a, func=AF.Identity), r=r, w=w)
            else:
                p.op(eng, lambda e: e.tensor_copy(out=out, in_=a), r=r, w=w)

        if phase == "main":
            with contextlib.ExitStack() as stu:
                p.stack = stu
                F["load_U"](U)
                _barrier(p)
            p.stack = st
        for st1 in _maybe_stack(phase != "main"):
            p.stack = st1
            if phase == "all" and not standalone:
                F["load_U"](U)
            pr = p.sb("pr", [128, GP, 3], F32)
            Bt = p.sb("Bt", [128, GP, 16, 2], F32)
            Ct = p.sb("Ct", [128, GP, 16, 2], F32)
            p.dma("sp", pr[:], par[:, :, :], w=["pr"])
            p.dma("sp", Bt[:], bri[:, :, :, :], w=["Bt"])
            p.dma("sp", Ct[:], cri[:, :, :, :], w=["Ct"])
            sm = {}
            for nm in ("dt", "xr", "th", "mag", "t", "tf", "fr", "m", "sn", "cn", "are", "aim", "n2", "rn", "ire", "iim",
                       "nre", "clre", "clim", "wre", "wim"):
                sm[nm] = p.sb("s_" + nm, [128, GP], F32)
            ti = p.sb("s_ti", [128, GP], I32)
            Pre = p.sb("Pre", [128, GP, 8], F32)
            Pim = p.sb("Pim", [128, GP, 8], F32)
            Tre = p.sb("Tre", [128, GP, 32], F32)
            Tim = p.sb("Tim", [128, GP, 32], F32)
            PWre = p.sb("PWre", [128, GP, 8], F32)
            PWim = p.sb("PWim", [128, GP, 8], F32)
            EXre = p.sb("EXre", [128, GP, 8], F32)
            EXim = p.sb("EXim", [128, GP, 8], F32)
            Bbre = p.sb("Bbre", [128, GP, 16], F32)
            Bbim = p.sb("Bbim", [128, GP, 16], F32)
            Xre = p.sb("Xre", [128, GP, 8, 16], F32)
            Xim = p.sb("Xim", [128, GP, 8, 16], F32)
            Cre = p.sb("CAre", [128, GP, 8, 16], F32)
            Cim = p.sb("CAim", [128, GP, 8, 16], F32)
            Rre = p.sb("Rre", [128, GP, 8, 16], F32)
            Rim = p.sb("Rim", [128, GP, 8, 16], F32)
            R2re = p.sb("R2re", [128, GP, 8, 16], F32)
            R2im = p.sb("R2im", [128, GP, 8, 16], F32)
            tq = [p.sb(f"tq{i}", [128, GP * 128], F32) for i in range(4)]

            def cmul(o_re, o_im, okeys, a_re, a_im, akeys, b_re, b_im, bkeys, shape, eng="dve"):
                n = int(np.prod(shape))
                pat = {1: None, 2: "p (a b) -> p a b", 3: "p (a b c) -> p a b c"}[len(shape)]
                kw = dict(zip("abc", shape))
                tv = [t[:, 0:n] if len(shape) == 1 else t[:, 0:n].rearrange(pat, **kw) for t in tq]
                tk = [f"tq{i}" for i in range(4)]
                tt(eng, tv[0], a_re, b_re, ALU.mult, akeys + bkeys, [tk[0]])
                tt(eng, tv[1], a_im, b_im, ALU.mult, akeys + bkeys, [tk[1]])
                tt(eng, tv[2], a_re, b_im, ALU.mult, akeys + bkeys, [tk[2]])
                tt(eng, tv[3], a_im, b_re, ALU.mult, akeys + bkeys, [tk[3]])
                tt(eng, o_re, tv[0], tv[1], ALU.subtract, [tk[0], tk[1]], okeys)
                tt(eng, o_im, tv[2], tv[3], ALU.add, [tk[2], tk[3]], okeys)

            s = sm
            p.op("act", lambda e: e.activation(out=s["dt"][:], in_=pr[:, :, 2], func=AF.Exp), r=["pr"], w=["dt"])
            tt("dve", s["xr"][:], s["dt"][:], pr[:, :, 0], ALU.mult, ["dt", "pr"], ["xr"])
            tt("dve", s["th"][:], s["dt"][:], pr[:, :, 1], ALU.mult, ["dt", "pr"], ["th"])
            p.op("act", lambda e: e.activation(out=s["mag"][:], in_=s["xr"][:], func=AF.Exp), r=["xr"], w=["mag"])

            def sin_of(off, out_nm):
                ts("dve", s["t"][:], s["th"][:], 1.0 / TWO_PI, off, ALU.mult, ALU.add, ["th"], ["t"])
                cp("dve", ti[:], s["t"][:], ["t"], ["ti"])
                cp("dve", s["tf"][:], ti[:], ["ti"], ["tf"])
                tt("dve", s["fr"][:], s["t"][:], s["tf"][:], ALU.subtract, ["t", "tf"], ["fr"])
                ts("dve", s["m"][:], s["fr"][:], 0.0, None, ALU.is_lt, None, ["fr"], ["m"])
                tt("dve", s["fr"][:], s["fr"][:], s["m"][:], ALU.add, ["fr", "m"], ["fr"])
                ts("dve", s["m"][:], s["fr"][:], 1.0, None, ALU.is_ge, None, ["fr"], ["m"])
                tt("dve", s["fr"][:], s["fr"][:], s["m"][:], ALU.subtract, ["fr", "m"], ["fr"])
                ts("dve", s["fr"][:], s["fr"][:], TWO_PI, -np.pi, ALU.mult, ALU.add, ["fr"], ["fr"])
                ts("dve", s["fr"][:], s["fr"][:], 3.1415925, -3.1415925, ALU.min, ALU.max, ["fr"], ["fr"])
                p.op("act", lambda e: e.activation(out=s[out_nm][:], in_=s["fr"][:], func=AF.Sin), r=["fr"], w=[out_nm])

            sin_of(0.5, "sn")
            sin_of(0.75, "cn")
            tt("dve", s["are"][:], s["mag"][:], s["cn"][:], ALU.mult, ["mag", "cn"], ["are"])
            tt("dve", s["aim"][:], s["mag"][:], s["sn"][:], ALU.mult, ["mag", "sn"], ["aim"])
            cp("dve", Pre[:, :, 0], s["are"][:], ["are"], ["P"])
            cp("dve", Pim[:, :, 0], s["aim"][:], ["aim"], ["P"])
            cmul(Pre[:, :, 1], Pim[:, :, 1], ["P"], s["are"][:], s["aim"][:], ["are", "aim"], s["are"][:], s["aim"][:], [], [GP])
            for w_ in (2, 4):
                bsh = [128, GP, w_]
                cmul(Pre[:, :, w_:2 * w_], Pim[:, :, w_:2 * w_], ["P"], Pre[:, :, 0:w_], Pim[:, :, 0:w_], ["P"],
                     Pre[:, :, w_ - 1:w_].broadcast_to(bsh), Pim[:, :, w_ - 1:w_].broadcast_to(bsh), [], [GP, w_])
            cp("dve", Tre[:, :, 0], Pre[:, :, 7], ["P"], ["T"])
            cp("dve", Tim[:, :, 0], Pim[:, :, 7], ["P"], ["T"])
            cmul(Tre[:, :, 1], Tim[:, :, 1], ["T"], Pre[:, :, 7], Pim[:, :, 7], ["P"], Pre[:, :, 7], Pim[:, :, 7], [], [GP])
            for w_ in (2, 4, 8, 16):
                bsh = [128, GP, w_]
                cmul(Tre[:, :, w_:2 * w_], Tim[:, :, w_:2 * w_], ["T"], Tre[:, :, 0:w_], Tim[:, :, 0:w_], ["T"],
                     Tre[:, :, w_ - 1:w_].broadcast_to(bsh), Tim[:, :, w_ - 1:w_].broadcast_to(bsh), [], [GP, w_])
            for (t1, t2, sre, sim, key) in ((A1, A2, Pre[:, :, 7], Pim[:, :, 7], "P"), (B1, B2, Tre[:, :, 31], Tim[:, :, 31], "T")):
                cp("dve", t1[:, :, 0], sre, [key], ["AB"])
                cp("dve", t1[:, :, 1], sre, [key], ["AB"])
                ts("dve", t2[:, :, 0], sim, -1.0, None, ALU.mult, None, [key], ["AB"])
                cp("dve", t2[:, :, 1], sim, [key], ["AB"])
            cp("dve", TAB1[:, :, :, 0], Tre[:], ["T"], ["TAB"])
            cp("dve", TAB1[:, :, :, 1], Tre[:], ["T"], ["TAB"])
            ts("dve", TAB2[:, :, :, 0], Tim[:], -1.0, None, ALU.mult, None, ["T"], ["TAB"])
            cp("dve", TAB2[:, :, :, 1], Tim[:], ["T"], ["TAB"])
            tt("dve", s["n2"][:], Pre[:, :, 7], Pre[:, :, 7], ALU.mult, ["P"], ["n2"])
            tt("dve", s["rn"][:], Pim[:, :, 7], Pim[:, :, 7], ALU.mult, ["P"], ["rn"])
            tt("dve", s["n2"][:], s["n2"][:], s["rn"][:], ALU.add, ["n2", "rn"], ["n2"])
            p.op("dve", lambda e: e.reciprocal(out=s["rn"][:], in_=s["n2"][:]), r=["n2"], w=["rn"])
            tt("dve", s["ire"][:], Pre[:, :, 7], s["rn"][:], ALU.mult, ["P", "rn"], ["ire"])
            tt("dve", s["iim"][:], Pim[:, :, 7], s["rn"][:], ALU.mult, ["P", "rn"], ["iim"])
            ts("dve", s["iim"][:], s["iim"][:], -1.0, None, ALU.mult, None, ["iim"], ["iim"])
            tt("dve", s["n2"][:], pr[:, :, 0], pr[:, :, 0], ALU.mult, ["pr", "ire", "iim"], ["n2"])
            tt("dve", s["rn"][:], pr[:, :, 1], pr[:, :, 1], ALU.mult, ["pr"], ["rn"])
            tt("dve", s["n2"][:], s["n2"][:], s["rn"][:], ALU.add, ["n2", "rn"], ["n2"])
            p.op("dve", lambda e: e.reciprocal(out=s["rn"][:], in_=s["n2"][:]), r=["n2"], w=["rn"])
            tt("dve", s["clre"][:], pr[:, :, 0], s["rn"][:], ALU.mult, ["pr", "rn"], ["clre"])
            tt("dve", s["clim"][:], pr[:, :, 1], s["rn"][:], ALU.mult, ["pr", "rn"], ["clim"])
            ts("dve", s["clim"][:], s["clim"][:], -1.0, None, ALU.mult, None, ["clim"], ["clim"])
            ts("dve", s["nre"][:], s["are"][:], -1.0, None, ALU.add, None, ["are"], ["nre"])
            cmul(s["wre"][:], s["wim"][:], ["w"], s["nre"][:], s["aim"][:], ["nre", "aim"], s["clre"][:], s["clim"][:], ["clre", "clim"], [GP])
            bsh = [128, GP, 16]
            cmul(Bbre[:], Bbim[:], ["Bb"], Bt[:, :, :, 0], Bt[:, :, :, 1], ["Bt"],
                 s["wre"][:].unsqueeze(2).broadcast_to(bsh), s["wim"][:].unsqueeze(2).broadcast_to(bsh), ["w"], [GP, 16])
            p.op("dve", lambda e: e.memset(PWre[:], 1.0), w=["PW"])
            p.op("dve", lambda e: e.memset(PWim[:], 0.0), w=["PW"])
            for (dst, src) in ((PWre, Pre), (PWim, Pim)):
                for j in range(7):
                    cp("dve", dst[0:64, :, j], src[0:64, :, 6 - j], ["P"], ["PW"])
                cp("dve", dst[64:128, :, 1:8], src[64:128, :, 0:7], ["P"], ["PW"])
            for (dst, src) in ((EXre, Pre), (EXim, Pim)):
                cp("dve", dst[0:64, :, :], src[0:64, :, :], ["P"], ["EX"])
                for i in range(8):
                    cp("dve", dst[64:128, :, i], src[64:128, :, 7 - i], ["P"], ["EX"])
            xsh = [128, GP, 8, 16]
            cmul(Xre[:], Xim[:], ["X"], PWre[:].unsqueeze(3).broadcast_to(xsh), PWim[:].unsqueeze(3).broadcast_to(xsh), ["PW"],
                 Bbre[:].unsqueeze(2).broadcast_to(xsh), Bbim[:].unsqueeze(2).broadcast_to(xsh), ["Bb"], [GP, 8, 16])
            cmul(Cre[:], Cim[:], ["CA"], EXre[:].unsqueeze(3).broadcast_to(xsh), EXim[:].unsqueeze(3).broadcast_to(xsh), ["EX"],
                 Ct[:, :, :, 0].unsqueeze(2).broadcast_to(xsh), Ct[:, :, :, 1].unsqueeze(2).broadcast_to(xsh), ["Ct"], [GP, 8, 16])
            g3 = [128, GP, 128]
            cmul(Rre[:].rearrange("p g i h -> p g (i h)"), Rim[:].rearrange("p g i h -> p g (i h)"), ["R"],
                 Cre[:].rearrange("p g i h -> p g (i h)"), Cim[:].rearrange("p g i h -> p g (i h)"), ["CA"],
                 s["ire"][:].unsqueeze(2).broadcast_to(g3), s["iim"][:].unsqueeze(2).broadcast_to(g3), ["ire", "iim"], [GP, 128])
            ts("dve", Rim[:], Rim[:], -1.0, None, ALU.mult, None, ["R"], ["R"])
            p.op("dve", lambda e: e.memset(CAf[:], 0.0), w=["CAs"])
            p.op("dve", lambda e: e.memset(CAb[:], 0.0), w=["CAs"])
            for (dstT, sl) in ((CAf, slice(0, 64)), (CAb, slice(64, 128))):
                cp("dve", dstT[sl, :, 0, :], Cre[sl].rearrange("p g i h -> p g (i h)"), ["CA"], ["CAs"])
                ts("dve", dstT[sl, :, 1, :], Cim[sl].rearrange("p g i h -> p g (i h)"), -1.0, None, ALU.mult, None, ["CA"], ["CAs"])
            p.op("dve", lambda e: e.memset(R2re[:], 0.0), w=["R2"])
            p.op("dve", lambda e: e.memset(R2im[:], 0.0), w=["R2"])
            cp("dve", R2re[64:128], Rre[64:128], ["R"], ["R2"])
            cp("dve", R2im[64:128], Rim[64:128], ["R"], ["R2"])
            p.op("dve", lambda e: e.memset(Rre[64:128], 0.0), r=["R2"], w=["R"])
            p.op("dve", lambda e: e.memset(Rim[64:128], 0.0), r=["R2"], w=["R"])
            for g in range(GP):
                for ri, Xp in enumerate((Xre, Xim)):
                    ptile = pt[(2 * g + ri) % 2]
                    pk = ptk[(2 * g + ri) % 2]
                    p.op("pe", lambda e, ptile=ptile, Xp=Xp, g=g: e.transpose(out=ptile[:, 0:128], in_=Xp[:, g].rearrange("p j h -> p (j h)"), identity=cs[:, 2, :]),
                         r=["X", "cs"], w=[pk])
                    cp("act" if ri == 0 else "dve", BS[:, g, ri, :], ptile[:, 0:128], [pk], [f"BS{g}"])
            for g in range(GP):
                for d in range(2):
                    ptile = pt[d]
                    pk = ptk[d]
                    Ra, Rb_ = (Rre, Rim) if d == 0 else (R2re, R2im)
                    p.op("pe", lambda e, ptile=ptile, Ra=Ra, g=g: e.matmul(ptile[:, 0:128], lhsT=Xre[:, g].rearrange("p j h -> p (j h)"),
                                                                           rhs=Ra[:, g].rearrange("p i h -> p (i h)"), start=True, stop=False),
                         r=["X", "R", "R2"], w=[pk])
                    p.op("pe", lambda e, ptile=ptile, Rb_=Rb_, g=g: e.matmul(ptile[:, 0:128], lhsT=Xim[:, g].rearrange("p j h -> p (j h)"),
                                                                             rhs=Rb_[:, g].rearrange("p i h -> p (i h)"), start=False, stop=True),
                         r=["X", "R", "R2"], w=[pk])
                tA = tq[0][:, 0:128]
                tB = tq[1][:, 0:128]
                tt("dve", tA, pt[0][:, 0:128], cs[:, 0, :], ALU.mult, [ptk[0], "cs"], ["tq0"])
                tt("dve", tB, pt[1][:, 0:128], cs[:, 1, :], ALU.mult, [ptk[1], "cs"], ["tq1"])
                tt("dve", tA, tA, tB, ALU.add, ["tq0", "tq1"], ["tq0"])
                p.op("dve", lambda e, g=g, tA=tA: e.scalar_tensor_tensor(out=Tbf[:, g, :], in0=cs[:, 2, :], scalar=dc[:, g:g + 1], in1=tA,
                                                                      op0=ALU.mult, op1=ALU.add),
                     r=["cs", "dc", "tq0"], w=[f"Tbf{g}"])
            if not F.get("no_barrier"):
                _barrier(p)
        p.stack = st

        if phase == "derive":
            return
        W = p.sb("W", [128, GP, NCH, 2], F32)
        ne = 0
        for g in range(GP if stage >= 2 else 0):
            for ri in range(2):
                for (c0, n) in SEGS:
                    ps = psS[ne % len(psS)]
                    pk = psSk[ne % len(psS)]
                    p.op("pe", lambda e, ps=ps, g=g, ri=ri, c0=c0, n=n: e.matmul(ps[:, 0:n], lhsT=BS[:, g, ri, :], rhs=U[:, g, c0:c0 + n], start=True, stop=True),
                         r=[f"BS{g}", f"U{g}"], w=[pk])
                    if c0 == 0:
                        qs = slice(31, None, -1)
                    else:
                        qs = slice(1087 - c0, 1087 - c0 - n, -1)
                    e1, e2 = ("act", "dve")
                    if e1 == "act":
                        p.op("act", lambda e, ps=ps, g=g, ri=ri, c0=c0, n=n: e.activation(out=W[0:64, g, c0:c0 + n, ri], in_=ps[0:64, 0:n], func=AF.Identity),
                             r=[pk], w=[f"W{g}"])
                        cp("dve", W[64:128, g, qs, ri], ps[64:128, 0:n], [pk], [f"W{g}"])
                    else:
                        cp("dve", W[0:64, g, c0:c0 + n, ri], ps[0:64, 0:n], [pk], [f"W{g}"])
                        p.op("act", lambda e, ps=ps, g=g, ri=ri, qs=qs, n=n: e.activation(out=W[64:128, g, qs, ri], in_=ps[64:128, 0:n], func=AF.Identity),
                             r=[pk], w=[f"W{g}"])
                    ne += 1

        with contextlib.ExitStack() as st2:
            p.stack = st2
            Hn = p.sb("Hn", [128, GP, NCH, 2], BF16)
            tmpd = p.sb("tmpd", [128, 9 * 32 * 2], F32)
            tmpp = p.sb("tmpp", [128, 9 * 32 * 2], F32)
            yb = [p.sb(f"yb{i}", [128, 512], BF16) for i in range(2)]
            Wv = W[:].rearrange("p g (b s) r -> p g b s r", s=32)
            Hv = Hn[:].rearrange("p g (b s) r -> p g b s r", s=32)
            splits = (("dve", 0, 5, tmpd, "tmpd"), ("pool", 5, 8, tmpp, "tmpp"))
            for (eng, g0, g1, tmp, tk) in (splits if stage >= 3 else ()):
                ng = g1 - g0
                wk = [f"W{g}" for g in range(g0, g1)]
                tv = tmp[:, 0:ng * NB * 2].rearrange("p (g b r) -> p g b r", g=ng, b=NB)
                sh = [128, ng, NB, 2]
                a1 = A1[:, g0:g1, :].unsqueeze(2).broadcast_to(sh)
                a2 = A2[:, g0:g1, :].unsqueeze(2).broadcast_to(sh)
                for s_ in range(1, 32):
                    prev = Wv[:, g0:g1, :, s_ - 1, :]
                    prevs = Wv[:, g0:g1, :, s_ - 1, ::-1]
                    cur = Wv[:, g0:g1, :, s_, :]
                    tt(eng, tv, prev, a1, ALU.mult, wk + ["AB"], [tk])
                    tt(eng, cur, cur, tv, ALU.add, wk + [tk], wk)
                    tt(eng, tv, prevs, a2, ALU.mult, wk + ["AB"], [tk])
                    tt(eng, cur, cur, tv, ALU.add, wk + [tk], wk)
            p.op("dve", lambda e: e.memset(G[:, :, 0, :], 0.0), w=["G"])
            allw = [f"W{g}" for g in range(GP)]
            tg = tmpd[:, 0:GP * 2].rearrange("p (g r) -> p g r", r=2)
            cp("dve", G[:, :, 1, :], Wv[:, :, 0, 31, :], allw, ["G"])
            for b in range(1, NB - 1 if stage >= 4 else 0):
                tt("dve", tg, G[:, :, b, :], B1[:], ALU.mult, ["G", "AB"], ["tmpd"])
                tt("dve", G[:, :, b + 1, :], tg, Wv[:, :, b, 31, :], ALU.add, ["tmpd"] + allw, ["G"])
                tt("dve", tg, G[:, :, b, ::-1], B2[:], ALU.mult, ["G", "AB"], ["tmpd"])
                tt("dve", G[:, :, b + 1, :], G[:, :, b + 1, :], tg, ALU.add, ["G", "tmpd"], ["G"])
            nyd = {"n": 0}

            def emit_out(g):
                for (c0, n) in SEGS:
                    ny = nyd["n"]
                    ps = psY[ny % 2]
                    pk = psYk[ny % 2]
                    ybt = yb[ny % 2]
                    yk = f"yb{ny % 2}"
                    ny += 1
                    nyd["n"] = ny
                    rk = [f"U{g}", f"Tbf{g}", f"Hn{g}", "CAs"]
                    p.op("pe", lambda e, ps=ps, g=g, c0=c0, n=n: e.matmul(ps[:, 0:n], lhsT=Tbf[:, g, :], rhs=U[:, g, c0:c0 + n], start=True, stop=False),
                         r=rk, w=[pk])
                    for ri in range(2):
                        if c0 == 0:
                            p.op("pe", lambda e, ps=ps, g=g, ri=ri, n=n: e.matmul(ps[:, 1:n], lhsT=CAf[:, g, ri, :], rhs=Hn[:, g, 0:n - 1, ri], start=False, stop=False),
                                 r=rk, w=[pk])
                        else:
                            p.op("pe", lambda e, ps=ps, g=g, ri=ri, c0=c0, n=n: e.matmul(ps[:, 0:n], lhsT=CAf[:, g, ri, :], rhs=Hn[:, g, c0 - 1:c0 + n - 1, ri], start=False, stop=False),
                                 r=rk, w=[pk])
                    for ri in range(2):
                        last = (ri == 1)
                        if c0 == 0:
                            p.op("pe", lambda e, ps=ps, g=g, ri=ri, n=n, last=last: e.matmul(ps[:, 0:n - 1], lhsT=CAb[:, g, ri, :], rhs=Hn[:, g, 30::-1, ri], start=False, stop=last),
                                 r=rk, w=[pk])
                        else:
                            hi = 1086 - c0
                            p.op("pe", lambda e, ps=ps, g=g, ri=ri, hi=hi, n=n, last=last: e.matmul(ps[:, 0:n], lhsT=CAb[:, g, ri, :], rhs=Hn[:, g, hi:hi - n:-1, ri], start=False, stop=last),
                                 r=rk, w=[pk])
                    p.op("act", lambda e, ps=ps, ybt=ybt, n=n: e.activation(out=ybt[:, 0:n], in_=ps[:, 0:n], func=AF.Identity), r=[pk], w=[yk])
                    if standalone:
                        p.dma("sp", F["y2"][g, :, c0:c0 + n], ybt[:, 0:n], r=[yk], w=[f"y2_{g}_{c0}"], sem=f"dy{ny % 2}")
                    else:
                        F["store_y"](g, c0, n, ybt, yk)
            for g in range(GP if stage >= 5 else 0):
                eng, tmp, tk = ("dve", tmpd, "tmpd") if g < 5 else ("pool", tmpp, "tmpp")
                for (b0, b1) in ((0, 9), (9, 17), (17, 25), (25, NB)):
                    nb_ = b1 - b0
                    tv = tmp[:, 0:nb_ * 32 * 2].rearrange("p (b s r) -> p b s r", b=nb_, s=32)
                    sh = [128, nb_, 32, 2]
                    t1 = TAB1[:, g, :, :].unsqueeze(1).broadcast_to(sh)
                    t2 = TAB2[:, g, :, :].unsqueeze(1).broadcast_to(sh)
                    gg = G[:, g, b0:b1, :].unsqueeze(2).broadcast_to(sh)
                    ggs = G[:, g, b0:b1, ::-1].unsqueeze(2).broadcast_to(sh)
                    tt(eng, tv, t1, gg, ALU.mult, ["TAB", "G"], [tk])
                    tt(eng, Wv[:, g, b0:b1], Wv[:, g, b0:b1], tv, ALU.add, [f"W{g}", tk], [f"W{g}"])
                    tt(eng, tv, t2, ggs, ALU.mult, ["TAB", "G"], [tk])
                    tt(eng, Hv[:, g, b0:b1], Wv[:, g, b0:b1], tv, ALU.add, [f"W{g}", tk], [f"Hn{g}"])
                if stage >= 6:
                    emit_out(g)
            if standalone:
                if stage < 6:
                    p.dma("sp", F["y2"][0, :, 0:512], yb[0][:, :], w=["y2_dummy"], sem="dy0")
                    p.wait_all("sp", ["y2_dummy"])
                p.wait_all("sp", [f"y2_{g}_{c0}" for g in range(GP) for (c0, n) in SEGS])
            _barrier(p)
        p.stack = st


def _s5_consts():
    jj = np.arange(128) // 16
    mf = (jj[None, :] >= jj[:, None]).astype(np.float32)
    mb = (jj[:, None] >= jj[None, :]).astype(np.float32)
    return np.ascontiguousarray(np.stack([mf, mb, np.eye(128, dtype=np.float32)], axis=1))


def run_s5(u_full, a_re, a_im, log_dt, b_re, b_im, c_re, c_im, d_skip, stage=9):
    nc = build_s5(stage)
    cst = _s5_consts()
    ug = u_full.reshape(NCH, 8, NG, 16)
    in_maps = []
    for k in range(NC):
        gs = slice(GP * k, GP * k + GP)
        u2 = np.ascontiguousarray(ug[:, :, gs, :].transpose(2, 1, 3, 0).reshape(GP, 128, NCH))
        par = np.empty((128, GP, 3), np.float32)
        par[:, :, 0] = a_re[:, gs, :].transpose(0, 2, 1).reshape(128, GP)
        par[:, :, 1] = a_im[:, gs, :].transpose(0, 2, 1).reshape(128, GP)
        par[:, :, 2] = np.broadcast_to(log_dt[:, None, gs], (2, 64, GP)).reshape(128, GP)
        bri = np.stack([b_re[:, gs], b_im[:, gs]], axis=-1)
        bri = np.ascontiguousarray(bri.transpose(0, 2, 1, 3, 4).reshape(128, GP, 16, 2))
        cri = np.stack([c_re[:, gs], c_im[:, gs]], axis=-1)
        cri = np.ascontiguousarray(cri.transpose(0, 3, 1, 2, 4).reshape(128, GP, 16, 2))
        dcol = np.ascontiguousarray(np.broadcast_to(d_skip.reshape(NG, 16)[gs].T[None, :, :], (8, 16, GP)).reshape(128, GP))
        in_maps.append({"u2": u2, "par": par, "bri": bri, "cri": cri, "dcol": dcol, "cst": cst})
    res = _run(nc, in_maps)
    yg = np.stack([r["y2"] for r in res])
    y = yg.reshape(NC, GP, 8, 16, NCH).transpose(4, 2, 0, 1, 3).reshape(NCH * 8, DS5)
    return y


RA = [(0, 384), (384, 768), (768, 1024)]
G4 = [[0, 1, 2, 3], [4, 5, 6, 7]]
G2P = [[0, 4], [1, 5], [2, 6], [3, 7]]


def _g2_row(r, rho):
    hi, lo = divmod(r, 4)
    for (a0, a1) in RA:
        if a0 <= rho < a1:
            la = a1 - a0
            b, lo2 = divmod(lo, 2)
            return 8 * a0 + b * 4 * la + hi * 2 * la + lo2 * la + (rho - a0)
    raise ValueError


def _cc(p, groups, src, dst, rkeys, wkeys):
    p.custom("pool", lambda e: e.collective_compute("AllGather", ALU.bypass, replica_groups=groups, ins=[src], outs=[dst]),
             "cc", 1, r=rkeys, w=wkeys)


def _exchange(p, send, G1, G2, skey, gkey, stage=0):
    for (a0, a1) in (RA if stage in (0, 1) else ()):
        _cc(p, G4, send[a0:a1, :], G1[4 * a0:4 * a1, :], [skey], [gkey + "1"])
    for (a0, a1) in (RA if stage in (0, 2) else ()):
        la = a1 - a0
        for b in range(2):
            base = 8 * a0 + b * 4 * la
            _cc(p, G2P, G1[4 * a0 + 2 * b * la:4 * a0 + 2 * (b + 1) * la, :], G2[base:base + 4 * la, :], [gkey + "1"], [gkey])


def _g1_row(lo, rho):
    for (a0, a1) in RA:
        if a0 <= rho < a1:
            return 4 * a0 + lo * (a1 - a0) + (rho - a0)
    raise ValueError


GX_ROWS = 4096 + 1024


def _exchange2(p, GX, pk, idxt, gkey):
    pkt = p.sb("pkt", [128, 4, TL], BF16)
    for lo in range(4):
        _gather(p, pkt[:, lo, :], GX[:, :], idxt[:, 9 + lo:10 + lo], [gkey + "1", "idxt"], [f"pkt{lo}"])
    p.dma("sp", pk.rearrange("(l p) n -> p l n", p=128), pkt[:], r=[f"pkt{lo}" for lo in range(4)], w=["pk" + gkey])
    _cc(p, G2P, pk[:, :], GX[4096:GX_ROWS, :], ["pk" + gkey], [gkey])


HG = ([[0, 4], [1, 5], [2, 6], [3, 7]], [[0, 1], [2, 3], [4, 5], [6, 7]], [[0, 1, 2, 3], [4, 5, 6, 7]])
HOFF = (1024, 2048, 3072, 5120)
X_ROWS = HOFF[3]


def _hyper(p, X, pk, idxt, skeys, gkey):
    pkt = p.sb("pkt", [128, 4, TL], BF16)
    have = list(skeys)
    for st_ in range(3):
        for b in range(4):
            c_ = 9 + 4 * st_ + b
            _gather(p, pkt[:, b, :], X[:, :], idxt[:, c_:c_ + 1], have + ["idxt"], [f"pkt{b}"])
        p.dma("pool", pk.rearrange("(l p) n -> p l n", p=128), pkt[:], r=[f"pkt{b}" for b in range(4)], w=["pk" + gkey], sem="pkw")
        _cc(p, HG[st_], pk[:, :], X[HOFF[st_]:HOFF[st_ + 1], :], ["pk" + gkey], [f"{gkey}{st_}"])
        have.append(f"{gkey}{st_}")
    return have


def _hyper_idx(k):
    snd = lambda d: d * 128
    A_ = lambda sender, blk: HOFF[0] + ((sender >> 2) & 1) * 512 + blk * 128
    B_ = lambda sender, blk: HOFF[1] + (sender & 1) * 512 + blk * 128
    C_ = lambda sender, blk: HOFF[2] + (sender & 3) * 512 + blk * 128
    fin = {k: snd(k), k ^ 4: A_(k ^ 4, 0), k ^ 1: B_(k ^ 1, 0), k ^ 5: B_(k ^ 1, 2),
           k ^ 2: C_(k ^ 2, 0), k ^ 6: C_(k ^ 2, 1), k ^ 3: C_(k ^ 2, 2), k ^ 7: C_(k ^ 2, 3)}
    packs = [snd(k ^ 4), snd(k ^ 5), snd(k ^ 6), snd(k ^ 7),
             snd(k ^ 1), snd(k ^ 3), A_(k ^ 4, 1), A_(k ^ 4, 3),
             snd(k ^ 2), A_(k ^ 4, 2), B_(k ^ 1, 1), B_(k ^ 1, 3)]
    idx = np.empty((128, 21), np.int32)
    pp = np.arange(128)
    for s_ in range(8):
        idx[:, s_] = fin[s_] + pp
    idx[:, 8] = k * 128 + pp
    for j, base in enumerate(packs):
        idx[:, 9 + j] = base + pp
    return idx


def _gather(p, out_ap, src2d, idx_ap, rkeys, wkeys):
    p._ng = getattr(p, "_ng", 0) + 1
    p.custom("pool", lambda e: e.indirect_dma_start(out=out_ap, out_offset=None, in_=src2d,
                                                    in_offset=bass.IndirectOffsetOnAxis(ap=idx_ap, axis=0)),
             f"ig{p._ng % 4}", 16, r=rkeys, w=wkeys)


def build_fused():
    nc = _new_nc()
    dt_in = lambda name, shape, dt=F32: nc.dram_tensor(name, list(shape), dt, kind="ExternalInput").ap()
    xin0 = dt_in("xin", [TT, D])
    cc = dt_in("cc", [128, 16, 2])
    wada = dt_in("wada", [2, 16, 128, ADA_N])
    bada = dt_in("bada", [128, 2, 6, 2])
    win = dt_in("win", [2, 16, 128, DIN])
    ident = dt_in("ident", [128, 128])
    pv = dt_in("pv", [2, 128, 8, 5])
    wglu = dt_in("wglu", [2, 8, 128, DS5])
    wout = dt_in("wout", [2, 16, 128, D])
    lnb = dt_in("lnb", [2, 128, 2, D])
    par = dt_in("par", [2, 128, GP, 3])
    bri = dt_in("bri", [2, 128, GP, 16, 2])
    cri = dt_in("cri", [2, 128, GP, 16, 2])
    dcol = dt_in("dcol", [2, 128, GP])
    cst = dt_in("cst", [128, 3, 128])
    idx = dt_in("idx", [128, 21], I32)
    selm = dt_in("selm", [128, 64, 128], BF16)
    xout = nc.dram_tensor("xo", [TL, D], F32, kind="ExternalOutput").ap()
    x1 = nc.dram_tensor("x1", [TT, D], F32).ap()
    ib = lambda name, shape: nc.dram_tensor(name, list(shape), BF16).ap()
    Xu = [ib(f"Xu{l}", [X_ROWS, TL]) for l in range(2)]
    Xy = [ib(f"Xy{l}", [X_ROWS, TL]) for l in range(2)]
    usend = [Xu[l][0:1024, :] for l in range(2)]
    uctx = [ib(f"uctx{l}", [1024, CTX]) for l in range(2)]
    G2u = Xu
    pku = [ib(f"pku{l}", [512, TL]) for l in range(2)]
    ysend = [Xy[l][0:1024, :] for l in range(2)]
    G2y = Xy
    pky = [ib(f"pky{l}", [512, TL]) for l in range(2)]
    yctx = ib("yctx", [128, CTX])
    msend = nc.dram_tensor("msend", [128, 24], F32).ap()
    Mg1 = nc.dram_tensor("Mg1", [512, 24], F32).ap()
    Mg2 = nc.dram_tensor("Mg2", [1024, 24], F32).ap()
    Gc1 = ib("Gc1", [512, CTX])
    Gc2 = ib("Gc2", [1024, CTX])

    with contextlib.ExitStack() as st:
        p = Prog(nc, st)
        idf = p.sb("idf", [128, 128], F32)
        idb = p.sb("idb", [128, 128], BF16)
        ones = p.sb("ones", [128, 128], F32)
        Mall = p.sb("Mall", [128, 8, 2, 6, 2], F32)
        mall = lambda l, m, t0, t1: Mall[:, m // 6, l, m % 6, t0:t1]
        idxt = p.sb("idxt", [128, 21], I32)
        acc = [p.ps(f"acc{i}", [128, 512]) for i in range(4)]
        tps = [p.ps(f"tp{i}", [128, 512], BF16) for i in range(2)]
        qq = [p.ps(f"qq{i}", [128, 512]) for i in range(2)]
        p.dma("sp", idf[:], ident[:, :], w=["idf"])
        p.dma("sp", idxt[:], idx[:, :], w=["idxt"])
        p.op("dve", lambda e: e.tensor_copy(out=idb[:], in_=idf[:]), r=["idf"], w=["idb"])
        p.op("dve", lambda e: e.memset(ones[:], 1.0), w=["ones"])

        w0 = contextlib.ExitStack()
        wstacks = [w0, st]
        FSs = [dict(p=p, standalone=False, stage=9, par=par[l], bri=bri[l], cri=cri[l], dcol=dcol[l], cst=cst,
                    pt=qq, psS=acc[0:4], psY=acc[2:4], ptk=["qq0", "qq1"], psSk=["acc0", "acc1", "acc2", "acc3"], psYk=["acc2", "acc3"],
                    wstack=wstacks[l]) for l in range(2)]

        for l_ in (1, 0):
            p.pfx = f"L{l_}W_"
            FSs[l_].update(st=st, phase="alloc")
            _s5_body(FSs[l_])
            p.stack = st

        def derive(l, cur, no_barrier=True):
            p.pfx = f"L{l}D_"
            FSs[l].update(st=cur, phase="derive", no_barrier=no_barrier)
            _s5_body(FSs[l])
            p.stack = cur

        with contextlib.ExitStack() as sa:
            p.stack = sa
            p.pfx = "ada_"
            cct = p.sb("cct", [128, 16, 2], F32)
            sg = p.sb("sg", [128, 16, 2], F32)
            sc = p.sb("sc", [128, 16, 2], BF16)
            bat = p.sb("bat", [128, 2, 6, 2], F32)
            Mloc = p.sb("Mloc", [128, 2, 6, 2], F32)
            wbA = [p.sb(f"wbA{i}", [128, 16, 256], BF16) for i in range(3)]
            p.dma("sp", cct[:], cc[:, :, :], w=["cct"])
            p.dma("sp", bat[:], bada[:, :, :, :], w=["bat"])
            p.op("act", lambda e: e.activation(out=sg[:], in_=cct[:], func=AF.Sigmoid), r=["cct"], w=["sg"])
            p.op("dve", lambda e: e.tensor_tensor(out=sc[:], in0=cct[:], in1=sg[:], op=ALU.mult), r=["cct", "sg"], w=["sc"])
            derive(0, sa)
            p.pfx = "ada_"
            na = 0
            for l in range(2):
                for pair in range(3):
                    i = (l * 3 + pair) % 3
                    p.dma("pool", wbA[i][:], wada[l, :, :, pair * 256:(pair + 1) * 256].rearrange("k p n -> p k n"), w=[f"wbA{i}"], sem=f"dwa{i}")
                    for cl in range(2):
                        m = pair * 2 + cl
                        ps = acc[na % 4]
                        pk = f"acc{na % 4}"
                        na += 1
                        for kf in range(16):
                            p.op("pe", lambda e, ps=ps, i=i, kf=kf, cl=cl: e.matmul(ps[:, 0:2], lhsT=wbA[i][:, kf, cl * 128:(cl + 1) * 128], rhs=sc[:, kf, :],
                                                                                   start=(kf == 0), stop=(kf == 15)),
                                 r=[f"wbA{i}", "sc"], w=[pk])
                        p.op("dve", lambda e, ps=ps, l=l, m=m: e.tensor_tensor(out=Mloc[:, l, m, :], in0=ps[:, 0:2], in1=bat[:, l, m, :], op=ALU.add),
                             r=[pk, "bat"], w=["Mloc"])
            p.dma("sp", msend[:, :], Mloc[:].rearrange("p l m t -> p (l m t)"), r=["Mloc"], w=["msend"])
            _cc(p, G4, msend[:, :], Mg1[:, :], ["msend"], ["Mg1"])
            _cc(p, G2P, Mg1[:, :], Mg2[:, :], ["Mg1"], ["Mg2"])
            p.dma("sp", Mall[:].rearrange("p c l m t -> p c (l m t)"), Mg2.rearrange("(c p) n -> p c n", p=128), r=["Mg2"], w=["Mall"])
            _barrier(p)
        p.stack = st

        for l in range(2):
            with contextlib.ExitStack() as stl:
                p.stack = stl
                p.pfx = f"L{l}_"
                mst = p.sb("mst", [128, 16, 4], F32)
                for (j, m0, w_, add1) in ((0, 0, 0, False), (1, 16, 0, True), (2, 0, 1, False), (3, 16, 1, True)):
                    m = m0
                    while m < m0 + 16:
                        c_ = m // 6
                        m1 = min(m0 + 16, 6 * (c_ + 1))
                        src = Mall[:, c_, l, m - 6 * c_:m1 - 6 * c_, w_]
                        dst = mst[:, m - m0:m1 - m0, j]
                        p.op("dve", lambda e, src=src, dst=dst, add1=add1: e.tensor_scalar(out=dst, in0=src, scalar1=(1.0 if add1 else 0.0), scalar2=None, op0=ALU.add),
                             r=["Mall"], w=["mst"])
                        m = m1
                xsrc = xin0 if l == 0 else x1
                o = p.sb("o", [128, 16, TT], BF16)
                pvt = p.sb("pvt", [128, 8, 5], F32)
                p.dma("sp", pvt[:], pv[l][:, :, :], w=["pvt"])
                FS = FSs[l]
                with contextlib.ExitStack() as sA:
                    p.stack = sA
                    p.pfx = f"L{l}A_"
                    hT = p.sb("hT", [128, 16, TT], BF16)
                    def cc_piece(a, l=l):
                        pass

                    ukeys = {}

                    FA = dict(p=p, st=sA, xin=xsrc, win=win[l], idb=idb, mst=mst, hT=hT, acc=acc, tps=tps, do_ln=True, standalone=False,
                              usend=usend[l], uctx=uctx[l], cc_piece=cc_piece)
                    _tok_body("A", True, FA)
                    p.pfx = f"L{l}C1_"
                    FC1 = dict(p=p, st=sA, xin=xsrc, win=win[l], idb=idb, mst=mst, hT=hT, acc=acc, tps=tps, do_ln=False, standalone=False,
                               part=1, o=o, pvt=pvt, odbg=None, hwcast=True,
                               mid_hook=(lambda l=l: ukeys.update(k=_hyper(p, Xu[l], pku[l], idxt, [f"usend{c_}" for c_ in range(8)], "G2u"))))
                    _tok_body("C", l == 0, FC1)
                    _barrier(p)
                p.stack = stl

                def load_U(U, l=l):
                    Ufm = p.sb("Ufm", [128, 8, TL], BF16)
                    Ucx = p.sb("Ucx", [128, CTX], BF16)
                    for r in range(8):
                        _gather(p, Ufm[:, r, :], G2u[l][:, :], idxt[:, r:r + 1], ukeys["k"] + ["idxt"], [f"Ufm{r}"])
                    _gather(p, Ucx[:, :], uctx[l][:, :], idxt[:, 8:9], ["uctx", "idxt"], ["Ucx"])
                    rk = [f"Ufm{r}" for r in range(8)] + ["Ucx"]
                    n = 0
                    for g in range(GP):
                        for j in range(8):
                            q = "sp" if n % 2 == 0 else "act"
                            n += 1
                            p.dma(q, U[16 * j:16 * j + 16, g, 32:NCH].rearrange("p (r c) -> p r c", c=128), Ufm[16 * g:16 * g + 16, :, j * 128:(j + 1) * 128],
                                  r=rk, w=[f"Urp{g}_{j}"], sem=f"rpk_{q}")
                            p.dma(q, U[16 * j:16 * j + 16, g, 0:32], Ucx[16 * g:16 * g + 16, j * 32:(j + 1) * 32],
                                  r=rk, w=[f"Urc{g}_{j}"], sem=f"rpk_{q}")
                    tok = {s_: p.cnt[s_] for s_ in ("rpk_sp", "rpk_act")}
                    for g in range(GP):
                        p.lastw[f"U{g}"] = dict(tok)
                        p.readers[f"U{g}"] = {}

                yn = {"n": 0}

                def store_y(g, c0, n, ybt, yk, l=l):
                    yn["n"] += 1
                    sem = f"dys{yn['n'] % 2}"
                    if c0 == 0:
                        if l == 0:
                            p.dma("sp", yctx[:, g * 32:(g + 1) * 32], ybt[:, 0:32], r=[yk], w=["yctx"], sem=sem)
                        return
                    r0 = (c0 - 32) // 128
                    dst = ysend[l].rearrange("(r p) (g c) -> p r g c", p=128, g=GP)[:, r0:r0 + 4, g, :]
                    p.dma("sp", dst, ybt[:, 0:512].rearrange("p (r c) -> p r c", c=128), r=[yk], w=["ysend"], sem=sem)

                p.pfx = f"L{l}S_"
                with contextlib.ExitStack() as sS:
                    p.stack = sS
                    FS.update(st=sS, phase="main", load_U=load_U, store_y=store_y)
                    _s5_body(FS)
                    _barrier(p)
                p.stack = stl
                ykeys = {}
                p.pfx = f"L{l}X_"
                with contextlib.ExitStack() as sX:
                    p.stack = sX
                    ykeys["k"] = _hyper(p, Xy[l], pky[l], idxt, ["ysend"], "G2y")
                    if l == 0:
                        _cc(p, G4, yctx[:, :], Gc1[:, :], ["yctx"], ["Gc1"])
                        _cc(p, G2P, Gc1[:, :], Gc2[:, :], ["Gc1"], ["Gc2"])
                        derive(1, sX, no_barrier=True)
                    _barrier(p)
                p.stack = stl

                def load_gel(gel, l=l):
                    prev = p.stack
                    with contextlib.ExitStack() as ssel:
                        p.stack = ssel
                        Sel = p.sb("Sel", [128, 64, 128], BF16)
                        p.dma("sp", Sel[:], selm[:, :, :], w=["Sel"])
                        nq = 0
                        for hh in range(4):
                            with contextlib.ExitStack() as sg_:
                                p.stack = sg_
                                k0 = 2 * hh
                                Yr = p.sb(f"Yr{hh}", [128, 2, TL], BF16)
                                for k_ in range(2):
                                    _gather(p, Yr[:, k_, :], G2y[l][:, :], idxt[:, k0 + k_:k0 + k_ + 1], ykeys["k"] + ["idxt"], [f"Yr{k_}"])
                                if l == 0:
                                    Yc = p.sb(f"Yc{hh}", [128, 2, CTX], BF16)
                                    p.dma("sp", Yc[:], Gc2.rearrange("(k p) n -> p k n", p=128)[:, k0:k0 + 2, :], r=["Gc2"], w=["Yc"])
                                for k_ in range(2):
                                    ch = k0 + k_
                                    for i in range(8):
                                        ps = acc[nq % 4]
                                        pk = f"acc{nq % 4}"
                                        eng = "act" if nq % 2 == 0 else "dve"
                                        nq += 1
                                        for g in range(8):
                                            p.op("pe", lambda e, ps=ps, i=i, g=g, k_=k_, Yr=Yr: e.matmul(
                                                ps[:, 0:128], lhsT=Sel[:, i * 8 + g, :], rhs=Yr[:, k_, g * 128:(g + 1) * 128], start=(g == 0), stop=(g == 7)),
                                                r=["Sel", f"Yr{k_}"], w=[pk])
                                        if l == 0:
                                            for g in range(8):
                                                p.op("pe", lambda e, ps=ps, i=i, g=g, k_=k_, Yc=Yc: e.matmul(
                                                    ps[:, 128:160], lhsT=Sel[:, i * 8 + g, :], rhs=Yc[:, k_, g * 32:(g + 1) * 32], start=(g == 0), stop=(g == 7)),
                                                    r=["Sel", "Yc"], w=[pk])
                                        dl = gel[:, ch, 0:TL].rearrange("p (c i) -> p i c", i=8)[:, i, :]
                                        dc_ = gel[:, ch, TL:TT].rearrange("p (c i) -> p i c", i=8)[:, i, :]
                                        if eng == "act":
                                            p.op("act", lambda e, ps=ps, dl=dl: e.activation(out=dl, in_=ps[:, 0:128], func=AF.Identity), r=[pk], w=[f"gel{ch}"])
                                            if l == 0:
                                                p.op("act", lambda e, ps=ps, dc_=dc_: e.activation(out=dc_, in_=ps[:, 128:160], func=AF.Identity), r=[pk], w=[f"gel{ch}"])
                                        else:
                                            p.op("dve", lambda e, ps=ps, dl=dl: e.tensor_copy(out=dl, in_=ps[:, 0:128]), r=[pk], w=[f"gel{ch}"])
                                            if l == 0:
                                                p.op("dve", lambda e, ps=ps, dc_=dc_: e.tensor_copy(out=dc_, in_=ps[:, 128:160]), r=[pk], w=[f"gel{ch}"])
                                _barrier(p)
                            p.stack = ssel
                    p.stack = prev

                def make_gb(gb, l=l):
                    with contextlib.ExitStack() as sg_:
                        prev = p.stack
                        p.stack = sg_
                        dg = [p.sb(f"dg{i}", [128, 128], F32) for i in range(2)]
                        nd = 0
                        for w_ in range(2):
                            for nb in range(4):
                                ps = qq[nb % 2]
                                pk = f"qq{nb % 2}"
                                for j in range(4):
                                    kf = nb * 4 + j
                                    d = dg[nd % 2]
                                    dk = f"dg{nd % 2}"
                                    nd += 1
                                    p.op("dve", lambda e, d=d, kf=kf, w_=w_: e.tensor_scalar(out=d[:], in0=idf[:], scalar1=mall(l, 32 + kf, w_, w_ + 1), scalar2=None, op0=ALU.mult),
                                         r=["idf", "Mall"], w=[dk])
                                    p.op("pe", lambda e, ps=ps, d=d, j=j: e.matmul(ps[:, j * 128:(j + 1) * 128], lhsT=ones[:], rhs=d[:], start=True, stop=True),
                                         r=["ones", dk], w=[pk])
                                p.op("act", lambda e, ps=ps, w_=w_, nb=nb: e.activation(out=gb[:, w_, nb * 512:(nb + 1) * 512], in_=ps[:, :], func=AF.Identity),
                                     r=[pk], w=["gb"])
                        _barrier(p)
                    p.stack = prev

                p.pfx = f"L{l}C_"
                with contextlib.ExitStack() as sC:
                    p.stack = sC
                    FC = dict(p=p, st=sC, xin=xsrc, win=win[l], idb=idb, mst=mst, hT=None, acc=acc, tps=tps, do_ln=False, standalone=False,
                              part=2, o=o, pvt=pvt,
                              pv=pv[l], wglu=wglu[l], wout=wout[l], lnb=lnb[l], xo=(x1 if l == 0 else xout), load_gel=load_gel, make_gb=make_gb, odbg=None)
                    _tok_body("C", l == 0, FC)
                    _barrier(p)
                p.stack = stl
            p.stack = st
            if l == 0:
                w0.close()
        p.emit()
    return nc


def _prep_fused_inputs(x, c, ctx, c_ctx, w_ada, b_ada, w_in, s5_a_re, s5_a_im, s5_log_dt, s5_b_re, s5_b_im,
                       s5_c_re, s5_c_im, s5_d, w_glu, b_glu, conv_w, conv_b, w_out, ln_g, ln_b):
    cc = np.stack([c.reshape(D), c_ctx.reshape(D)], axis=-1)
    cc = np.ascontiguousarray(cc.reshape(16, 128, 2).transpose(1, 0, 2))
    wada_all = w_ada.reshape(2, 16, 128, NC, ADA_N)
    bada_all = np.broadcast_to(b_ada.reshape(2, NC, 6, 128).transpose(3, 1, 0, 2)[:, :, :, :, None], (128, NC, 2, 6, 2))
    win = w_in.reshape(2, 16, 128, DIN)
    pv = np.empty((2, 128, 8, 5), np.float32)
    for l in range(2):
        pv[l, :, :, 0] = b_glu[l].reshape(8, 128).T
        for j in range(3):
            pv[l, :, :, 1 + j] = conv_w[l, j].reshape(8, 128).T
        pv[l, :, :, 4] = conv_b[l].reshape(8, 128).T
    wglu = w_glu.reshape(2, 8, 128, DS5)
    wout = w_out.reshape(2, 16, 128, D)
    lnb = np.ascontiguousarray(np.broadcast_to(np.stack([ln_g, ln_b], axis=1)[:, None, :, :], (2, 128, 2, D)))
    cst = _s5_consts()
    selm = np.zeros((128, 64, 128), np.float32)
    for i_ in range(8):
        for g_ in range(8):
            for h_ in range(16):
                selm[16 * i_ + h_, i_ * 8 + g_, 16 * g_ + h_] = 1.0
    selm = selm.astype(ml_dtypes.bfloat16)
    in_maps = []
    for k in range(NC):
        gs = slice(GP * k, GP * k + GP)
        par = np.empty((2, 128, GP, 3), np.float32)
        bri = np.empty((2, 128, GP, 16, 2), np.float32)
        cri = np.empty((2, 128, GP, 16, 2), np.float32)
        dcol = np.empty((2, 128, GP), np.float32)
        for l in range(2):
            par[l, :, :, 0] = s5_a_re[l][:, gs, :].transpose(0, 2, 1).reshape(128, GP)
            par[l, :, :, 1] = s5_a_im[l][:, gs, :].transpose(0, 2, 1).reshape(128, GP)
            par[l, :, :, 2] = np.broadcast_to(s5_log_dt[l][:, None, gs], (2, 64, GP)).reshape(128, GP)
            bri[l] = np.stack([s5_b_re[l][:, gs], s5_b_im[l][:, gs]], axis=-1).transpose(0, 2, 1, 3, 4).reshape(128, GP, 16, 2)
            cri[l] = np.stack([s5_c_re[l][:, gs], s5_c_im[l][:, gs]], axis=-1).transpose(0, 3, 1, 2, 4).reshape(128, GP, 16, 2)
            dcol[l] = np.broadcast_to(s5_d[l].reshape(NG, 16)[gs].T[None, :, :], (8, 16, GP)).reshape(128, GP)
        idx = _hyper_idx(k)
        xin = np.ascontiguousarray(np.concatenate([x[0, k * TL:(k + 1) * TL], ctx[0]], axis=0))
        wada = np.ascontiguousarray(wada_all[:, :, :, k, :])
        bada = np.ascontiguousarray(bada_all[:, k])
        in_maps.append({"xin": xin, "cc": cc, "wada": wada, "bada": bada, "win": win, "ident": _IDENT, "pv": pv, "wglu": wglu,
                        "wout": wout, "lnb": lnb, "par": par, "bri": bri, "cri": cri, "dcol": dcol, "cst": cst, "idx": idx, "selm": selm})
    return in_maps


def kernel_unfused(x, c, ctx, c_ctx, w_ada, b_ada, w_in, s5_a_re, s5_a_im, s5_log_dt, s5_b_re, s5_b_im,
                   s5_c_re, s5_c_im, s5_d, w_glu, b_glu, conv_w, conv_b, w_out, ln_g, ln_b):
    return _kernel_unfused_impl(x, c, ctx, c_ctx, w_ada, b_ada, w_in, s5_a_re, s5_a_im, s5_log_dt, s5_b_re, s5_b_im,
                                s5_c_re, s5_c_im, s5_d, w_glu, b_glu, conv_w, conv_b, w_out, ln_g, ln_b)


def kernel(x, c, ctx, c_ctx, w_ada, b_ada, w_in, s5_a_re, s5_a_im, s5_log_dt, s5_b_re, s5_b_im,
           s5_c_re, s5_c_im, s5_d, w_glu, b_glu, conv_w, conv_b, w_out, ln_g, ln_b):
    f = lambda a: np.ascontiguousarray(np.asarray(a, dtype=np.float32))
    args = [f(a) for a in (x, c, ctx, c_ctx, w_ada, b_ada, w_in, s5_a_re, s5_a_im, s5_log_dt, s5_b_re, s5_b_im,
                           s5_c_re, s5_c_im, s5_d, w_glu, b_glu, conv_w, conv_b, w_out, ln_g, ln_b)]
    in_maps = _prep_fused_inputs(*args)
    nc = build_fused()
    res = _run(nc, in_maps)
    out = np.concatenate([r["xo"] for r in res], axis=0)
    return np.ascontiguousarray(out[None].astype(np.float32))


def _kernel_unfused_impl(x, c, ctx, c_ctx, w_ada, b_ada, w_in, s5_a_re, s5_a_im, s5_log_dt, s5_b_re, s5_b_im,
           s5_c_re, s5_c_im, s5_d, w_glu, b_glu, conv_w, conv_b, w_out, ln_g, ln_b):
    f = lambda a: np.ascontiguousarray(np.asarray(a, dtype=np.float32))
    x, c, ctx, c_ctx, w_ada, b_ada, w_in = map(f, (x, c, ctx, c_ctx, w_ada, b_ada, w_in))
    s5_a_re, s5_a_im, s5_log_dt, s5_b_re, s5_b_im, s5_c_re, s5_c_im, s5_d = map(
        f, (s5_a_re, s5_a_im, s5_log_dt, s5_b_re, s5_b_im, s5_c_re, s5_c_im, s5_d))
    w_glu, b_glu, conv_w, conv_b, w_out, ln_g, ln_b = map(f, (w_glu, b_glu, conv_w, conv_b, w_out, ln_g, ln_b))
    mod = run_ada(c, c_ctx, w_ada, b_ada)
    xl = x[0]
    cx = ctx[0]
    for l in range(2):
        xs = [np.ascontiguousarray(np.concatenate([xl[k * TL:(k + 1) * TL], cx], axis=0)) for k in range(NC)]
        us = run_tok_A(xs, mod[l], w_in[l])
        u_lat = np.concatenate([u.reshape(DS5, TT)[:, :TL].T for u in us], axis=0)
        u_ctx = us[0].reshape(DS5, TT)[:, TL:].T
        u_full = np.ascontiguousarray(np.concatenate([u_ctx, u_lat], axis=0))
        y = run_s5(u_full, s5_a_re[l], s5_a_im[l], s5_log_dt[l], s5_b_re[l], s5_b_im[l], s5_c_re[l], s5_c_im[l], s5_d[l])
        ys = []
        for k in range(NC):
            yk = np.concatenate([y[CTX + k * TL:CTX + (k + 1) * TL], y[:CTX]], axis=0)
            ys.append(np.ascontiguousarray(yk.T.reshape(8, 128, TT)))
        outs = run_tok_C(xs, ys, mod[l], w_in[l], w_glu[l], b_glu[l], conv_w[l], conv_b[l], w_out[l], ln_g[l], ln_b[l],
                         with_ctx=(l == 0))
        xl = np.concatenate([o[:TL] for o in outs], axis=0)
        if l == 0:
            cx = np.ascontiguousarray(outs[0][TL:])
    return np.ascontiguousarray(xl[None].astype(np.float32))
```

```python
import contextlib
import numpy as np
import ml_dtypes
import concourse.bass as bass
import concourse.mybir as mybir
from concourse.bass_utils import run_bass_kernel_spmd

F32 = mybir.dt.float32
BF16 = mybir.dt.bfloat16
I32 = mybir.dt.int32
ALU = mybir.AluOpType
AF = mybir.ActivationFunctionType
AX = mybir.AxisListType

D = 2048
SEQ = 8192
CTX = 256
NC = 8
TL = SEQ // NC
TT = TL + CTX
DIN = 6144
DS5 = 1024
NG = 64
GP = 8
NCH = (SEQ + CTX) // 8
NB = NCH // 32
ALPHA = (2.0 * 2) ** 0.25
EPS = 1e-6
TWO_PI = 2.0 * np.pi
DEBUG_O = False


class Prog:
    ENG = ("pe", "act", "dve", "pool", "sp")

    def __init__(self, nc, stack):
        self.nc = nc
        self.stack = stack
        self.sem_stack = stack
        self.q = {e: [] for e in self.ENG}
        self.sem = {}
        self.cnt = {}
        self.seen = {e: {} for e in self.ENG}
        self.lastw = {}
        self.readers = {}
        for e in self.ENG:
            self._mksem(e)

    def _mksem(self, name):
        if name not in self.sem:
            self.sem[name] = self.sem_stack.enter_context(self.nc.semaphore("s_" + name))
            self.cnt[name] = 0

    pfx = ""

    def sb(self, name, shape, dtype):
        return self.stack.enter_context(self.nc.sbuf_tensor(self.pfx + name, list(shape), dtype))

    def ps(self, name, shape, dtype=F32):
        return self.stack.enter_context(self.nc.psum_tensor(name, list(shape), dtype))

    def _waits(self, eng, r, w):
        need = {}

        def add(tok, same_ok):
            if tok is None:
                return
            for s, v in tok.items():
                if s == eng and not same_ok:
                    continue
                if need.get(s, 0) < v:
                    need[s] = v

        for k in r:
            add(self.lastw.get(k), True)
        so = eng != "pe"
        for k in w:
            add(self.lastw.get(k), so)
            add(self.readers.get(k), so)
        out = []
        for s, v in need.items():
            if self.seen[eng].get(s, 0) < v:
                self.seen[eng][s] = v
                out.append((s, v))
        return out

    def _commit(self, tok, r, w):
        for k in w:
            self.lastw[k] = dict(tok)
            self.readers[k] = {}
        for k in r:
            d = self.readers.setdefault(k, {})
            for s, v in tok.items():
                if d.get(s, 0) < v:
                    d[s] = v

    def op(self, eng, fn, r=(), w=()):
        waits = self._waits(eng, r, w)
        self.cnt[eng] += 1
        tok = {eng: self.cnt[eng]}
        self.q[eng].append((waits, fn, eng, 1))
        self._commit(tok, r, w)

    def dma(self, q, out, in_, r=(), w=(), sem=None):
        if sem is None:
            self._nu = getattr(self, "_nu", 0) + 1
            sem = f"du{self._nu}"
        self._mksem(sem)
        waits = self._waits(q, r, w)
        self.cnt[sem] += 16
        tok = {sem: self.cnt[sem]}
        self.q[q].append((waits, lambda e, o=out, i=in_: e.dma_start(out=o, in_=i), sem, 16))
        self._commit(tok, r, w)

    def custom(self, q, fn, sem, inc, r=(), w=()):
        self._mksem(sem)
        waits = self._waits(q, r, w)
        self.cnt[sem] += inc
        self.q[q].append((waits, fn, sem, inc))
        self._commit({sem: self.cnt[sem]}, r, w)

    def wait_all(self, eng, keys):
        waits = self._waits(eng, keys, ())
        if waits:
            self.q[eng].append((waits, None, None, 0))

    def emit(self):
        nc = self.nc
        sem = self.sem
        q = self.q
        with nc.Block() as block:
            def run(e, lst):
                for waits, fn, s, n in lst:
                    for ws, wv in waits:
                        e.wait_ge(sem[ws], wv)
                    if fn is not None:
                        fn(e).then_inc(sem[s], n)

            @block.tensor
            def _(e):
                run(e, q["pe"])

            @block.scalar
            def _(e):
                run(e, q["act"])

            @block.vector
            def _(e):
                run(e, q["dve"])

            @block.gpsimd
            def _(e):
                run(e, q["pool"])

            @block.sync
            def _(e):
                run(e, q["sp"])


def _new_nc():
    return bass.Bass("TRN2", target_bir_lowering=False)


def _run(nc, in_maps):
    res = run_bass_kernel_spmd(nc, in_maps, core_ids=list(range(NC)))
    return res.results


ADA_N = 3 * D // NC


def build_ada():
    nc = _new_nc()
    cc = nc.dram_tensor("cc", [128, 16, 2], F32, kind="ExternalInput").ap()
    wa = nc.dram_tensor("wa", [2, 128, 16, ADA_N], F32, kind="ExternalInput").ap()
    ba = nc.dram_tensor("ba", [2, 2, ADA_N], F32, kind="ExternalInput").ap()
    mod = nc.dram_tensor("mod", [2, 2, ADA_N], F32, kind="ExternalOutput").ap()
    with contextlib.ExitStack() as st:
        p = Prog(nc, st)
        cct = p.sb("cct", [128, 16, 2], F32)
        sc = p.sb("sc", [128, 16, 2], BF16)
        sg = p.sb("sg", [128, 16, 2], F32)
        wt = [p.sb(f"wt{l}", [128, 16, ADA_N], BF16) for l in range(2)]
        bt = p.sb("bt", [2, 2, ADA_N], F32)
        ot = p.sb("ot", [2, 2, ADA_N], F32)
        pp = [p.ps(f"pp{i}", [2, 384]) for i in range(4)]
        p.dma("sp", cct[:], cc[:, :, :], w=["cct"])
        p.dma("sp", bt[:], ba.rearrange("l t n -> t l n"), w=["bt"])
        for l in range(2):
            for h in range(2):
                p.dma("pool", wt[l][:, 8 * h:8 * h + 8, :], wa[l, :, 8 * h:8 * h + 8, :], w=[f"wt{l}{h}"], sem=f"dw{l}{h}")
        p.op("act", lambda e: e.activation(out=sg[:], in_=cct[:], func=AF.Sigmoid), r=["cct"], w=["sg"])
        p.op("dve", lambda e: e.tensor_tensor(out=sc[:], in0=cct[:], in1=sg[:], op=ALU.mult), r=["cct", "sg"], w=["sc"])
        for l in range(2):
            for nb in range(2):
                ps = pp[l * 2 + nb]
                for kf in range(16):
                    p.op("pe", lambda e, ps=ps, kf=kf, l=l, nb=nb: e.matmul(
                        ps[:, :], lhsT=sc[:, kf, :], rhs=wt[l][:, kf, nb * 384:(nb + 1) * 384],
                        start=(kf == 0), stop=(kf == 15)),
                        r=["sc", f"wt{l}{kf // 8}"], w=[f"pp{l}{nb}"])
                p.op("dve", lambda e, ps=ps, l=l, nb=nb: e.tensor_tensor(
                    out=ot[:, l, nb * 384:(nb + 1) * 384], in0=ps[:, :], in1=bt[:, l, nb * 384:(nb + 1) * 384], op=ALU.add),
                    r=[f"pp{l}{nb}", "bt"], w=["ot"])
        p.dma("sp", mod.rearrange("l t n -> t l n"), ot[:], r=["ot"], w=["mod"], sem="dout")
        p.wait_all("sp", ["mod"])
        p.emit()
    return nc


def run_ada(c, c_ctx, w_ada, b_ada):
    nc = build_ada()
    cc = np.stack([c.reshape(D), c_ctx.reshape(D)], axis=-1)
    cc = np.ascontiguousarray(cc.reshape(16, 128, 2).transpose(1, 0, 2))
    wr = w_ada.reshape(2, 16, 128, NC, ADA_N)
    in_maps = []
    for k in range(NC):
        wa = np.ascontiguousarray(wr[:, :, :, k, :].transpose(0, 2, 1, 3))
        ba = np.ascontiguousarray(np.broadcast_to(b_ada.reshape(2, 1, NC, ADA_N)[:, :, k, :], (2, 2, ADA_N)))
        in_maps.append({"cc": cc, "wa": wa, "ba": ba})
    res = _run(nc, in_maps)
    mod = np.concatenate([r["mod"] for r in res], axis=-1)
    return mod


def _maybe_stack(cond):
    if cond:
        with contextlib.ExitStack() as s_:
            yield s_


def _barrier(p):
    snap = dict(p.cnt)
    for e in p.ENG:
        waits = []
        for s, v in snap.items():
            if s == "cc":
                continue
            if v > 0 and p.seen[e].get(s, 0) < v and s != e:
                p.seen[e][s] = v
                waits.append((s, v))
        if waits:
            p.q[e].append((waits, None, None, 0))


def build_tok(mode, with_ctx):
    ntok = TT if with_ctx else TL
    nt_all = TT // 128
    nc = _new_nc()
    xin = nc.dram_tensor("xin", [TT, D], F32, kind="ExternalInput").ap()
    ms = nc.dram_tensor("ms", [128, 16, 4], F32, kind="ExternalInput").ap()
    win = nc.dram_tensor("win", [16, 128, DIN], F32, kind="ExternalInput").ap()
    ident = nc.dram_tensor("ident", [128, 128], F32, kind="ExternalInput").ap()
    if mode == "A":
        uo = nc.dram_tensor("uo", [8, 128, TT], BF16, kind="ExternalOutput").ap()
    else:
        yin = nc.dram_tensor("yin", [8, 128, TT], BF16, kind="ExternalInput").ap()
        gbc = nc.dram_tensor("gbc", [128, 2, D], F32, kind="ExternalInput").ap()
        lnb = nc.dram_tensor("lnb", [128, 2, D], F32, kind="ExternalInput").ap()
        pv = nc.dram_tensor("pv", [128, 8, 5], F32, kind="ExternalInput").ap()
        wglu = nc.dram_tensor("wglu", [8, 128, DS5], F32, kind="ExternalInput").ap()
        wout = nc.dram_tensor("wout", [16, 128, D], F32, kind="ExternalInput").ap()
        xo = nc.dram_tensor("xo", [ntok, D], F32, kind="ExternalOutput").ap()
        if DEBUG_O:
            odbg = nc.dram_tensor("odbg", [16, 128, TT], BF16, kind="ExternalOutput").ap()

    with contextlib.ExitStack() as st:
        p = Prog(nc, st)
        idf = p.sb("idf", [128, 128], F32)
        idb = p.sb("idb", [128, 128], BF16)
        mst = p.sb("mst", [128, 16, 4], F32)
        hT = p.sb("hT", [128, 16, TT], BF16)
        acc = [p.ps(f"acc{i}", [128, 512]) for i in range(4)]
        tps = [p.ps(f"tp{i}", [128, 512], BF16) for i in range(2)]
        p.dma("sp", idf[:], ident[:, :], w=["idf"])
        p.dma("sp", mst[:], ms[:, :, :], w=["mst"])
        p.op("dve", lambda e: e.tensor_copy(out=idb[:], in_=idf[:]), r=["idf"], w=["idb"])
        for j in (1, 3):
            p.op("dve", lambda e, j=j: e.tensor_scalar(out=mst[:, :, j], in0=mst[:, :, j], scalar1=1.0, scalar2=None, op0=ALU.add),
                 r=["mst"], w=["mst"])

        F = dict(p=p, st=st, xin=xin, win=win, idb=idb, mst=mst, hT=hT, acc=acc, tps=tps, do_ln=True, standalone=True)
        if mode == "A":
            F.update(uo=uo)
        else:
            F.update(yin=yin, gbc=gbc, lnb=lnb, pv=pv, wglu=wglu, wout=wout, xo=xo, odbg=(odbg if DEBUG_O else None))
        _tok_body(mode, with_ctx, F)
        p.emit()
    return nc


def _tok_body(mode, with_ctx, F):
    ntok = TT if with_ctx else TL
    nt_all = TT // 128
    p = F["p"]; st = F["st"]; xin = F["xin"]; win = F["win"]; idb = F["idb"]; mst = F["mst"]; hT = F["hT"]
    acc = F["acc"]; tps = F["tps"]; standalone = F["standalone"]
    sfx = F.get("sfx", "")
    if True:
        with contextlib.ExitStack() as st1:
            p.stack = st1
            NBUF = 3
            xt = [p.sb(f"xt{i}", [128, D], F32) for i in range(NBUF)]
            nt_ln = nt_all if F["do_ln"] else 0
            xn = [p.sb(f"xn{i}", [128, D], BF16) for i in range(NBUF)]
            stt = [p.sb(f"stt{i}", [128, 4, 6], F32) for i in range(NBUF)]
            mv = [p.sb(f"mv{i}", [128, 2], F32) for i in range(NBUF)]
            rs = [p.sb(f"rs{i}", [128, 1], F32) for i in range(NBUF)]
            def ln_s1(t):
                b = t % NBUF
                p.dma("sp", xt[b][:], xin[t * 128:(t + 1) * 128, :], w=[f"xt{b}"], sem=f"dx{b}")
                for c4 in range(4):
                    p.op("dve", lambda e, b=b, c4=c4: e.bn_stats(out=stt[b][:, c4, :], in_=xt[b][:, c4 * 512:(c4 + 1) * 512]),
                         r=[f"xt{b}"], w=[f"stt{b}"])
                p.op("dve", lambda e, b=b: e.bn_aggr(out=mv[b][:], in_=stt[b][:].rearrange("p a s -> p (a s)")),
                     r=[f"stt{b}"], w=[f"mv{b}"])
                p.op("act", lambda e, b=b: e.activation(out=rs[b][:], in_=mv[b][:, 1:2], func=AF.Sqrt, bias=EPS, scale=1.0),
                     r=[f"mv{b}"], w=[f"rs{b}"])

            def ln_s2(t):
                b = t % NBUF
                p.op("dve", lambda e, b=b: e.reciprocal(out=rs[b][:], in_=rs[b][:]), r=[f"rs{b}"], w=[f"rs{b}"])
                p.op("dve", lambda e, b=b: e.tensor_scalar(out=xn[b][:], in0=xt[b][:], scalar1=mv[b][:, 0:1], scalar2=rs[b][:, 0:1],
                                                           op0=ALU.subtract, op1=ALU.mult),
                     r=[f"xt{b}", f"mv{b}", f"rs{b}"], w=[f"xn{b}"])

            def ln_s3(t):
                b = t % NBUF
                which = 0 if t < TL // 128 else 1
                for k4 in range(4):
                    tp = tps[k4 % 2]
                    for j in range(4):
                        kf = k4 * 4 + j
                        p.op("pe", lambda e, tp=tp, j=j, kf=kf, b=b: e.transpose(
                            out=tp[:, j * 128:(j + 1) * 128], in_=xn[b][:, kf * 128:(kf + 1) * 128], identity=idb[:]),
                            r=[f"xn{b}", "idb"], w=[f"tp{k4 % 2}"])
                    for j in range(4):
                        kf = k4 * 4 + j
                        if j % 2 == 0:
                            p.op("act", lambda e, tp=tp, j=j, kf=kf, t=t, which=which: e.activation(
                                out=hT[:, kf, t * 128:(t + 1) * 128], in_=tp[:, j * 128:(j + 1) * 128], func=AF.Identity,
                                scale=mst[:, kf, 2 * which + 1:2 * which + 2], bias=mst[:, kf, 2 * which:2 * which + 1]),
                                r=[f"tp{k4 % 2}", "mst"], w=[f"hT{t}"])
                        else:
                            p.op("dve", lambda e, tp=tp, j=j, kf=kf, t=t, which=which: e.tensor_scalar(
                                out=hT[:, kf, t * 128:(t + 1) * 128], in0=tp[:, j * 128:(j + 1) * 128],
                                scalar1=mst[:, kf, 2 * which + 1:2 * which + 2], scalar2=mst[:, kf, 2 * which:2 * which + 1],
                                op0=ALU.mult, op1=ALU.add),
                                r=[f"tp{k4 % 2}", "mst"], w=[f"hT{t}"])

            for step in range(nt_ln + 2 if nt_ln else 0):
                if step < nt_ln:
                    ln_s1(step)
                if 0 <= step - 1 < nt_ln:
                    ln_s2(step - 1)
                if 0 <= step - 2 < nt_ln:
                    ln_s3(step - 2)
            _barrier(p)
        p.stack = st

        blocks = [(0, 512), (512, 512)] + ([(1024, 256)] if True else [])
        hkeys = lambda t0, n: [f"hT{t}" for t in range(t0 // 128, (t0 + n) // 128)]

        wq = {"n": 0}

        def load_w(wb, col0, ncol=256):
            i = wq["n"] % len(wb)
            if F.get("hwcast"):
                j = wq["n"] % 2
                wq["n"] += 1
                wst = F["wst"]
                p.dma("sp", wst[j][:, :, 0:ncol], win[:, :, col0:col0 + ncol].rearrange("k p n -> p k n"), w=[f"wst{j}"], sem=f"dws{j}")
                p.op("act", lambda e, i=i, j=j: e.activation(out=wb[i][:, :, 0:ncol], in_=wst[j][:, :, 0:ncol], func=AF.Identity),
                     r=[f"wst{j}"], w=[f"wb{i}"])
                return i
            wq["n"] += 1
            p.dma("pool", wb[i][:, :, 0:ncol], win[:, :, col0:col0 + ncol].rearrange("k p n -> p k n"),
                  w=[f"wb{i}"], sem=f"dwb{i}")
            return i

        def proj(wbt, wkey, cl, t0, n, ps, pkey):
            for kf in range(16):
                p.op("pe", lambda e, kf=kf: e.matmul(ps[:, 0:n], lhsT=wbt[:, kf, cl * 128:(cl + 1) * 128],
                                                      rhs=hT[:, kf, t0:t0 + n], start=(kf == 0), stop=(kf == 15)),
                     r=[wkey] + hkeys(t0, n), w=[pkey])

        pa = {"n": 0}

        def next_acc():
            i = pa["n"] % 4
            pa["n"] += 1
            return acc[i], f"acc{i}"

        if mode == "A":
            with contextlib.ExitStack() as st2:
                p.stack = st2
                wb = [p.sb(f"wb{i}", [128, 16, 256], BF16) for i in range(2 if standalone else 4)]
                ut = p.sb("ut", [128, 8, TT], BF16)
                if not standalone:
                    F["utp"] = p.sb("utp", [128, 8, TT], BF16)
                pre = [load_w(wb, pair * 256) for pair in range(4)] if not standalone else None
                for pair in range(4):
                    i = pre[pair] if pre is not None else load_w(wb, pair * 256)
                    for cl in range(2):
                        ch = pair * 2 + cl
                        for bi, (t0, n) in enumerate(blocks):
                            ps, pk = next_acc()
                            proj(wb[i], f"wb{i}", cl, t0, n, ps, pk)
                            eng = "act" if (bi % 2 == 0) else "dve"
                            if eng == "act":
                                p.op("act", lambda e, ps=ps, ch=ch, t0=t0, n=n: e.activation(out=ut[:, ch, t0:t0 + n], in_=ps[:, 0:n], func=AF.Identity),
                                     r=[pk], w=[f"ut{ch}"])
                            else:
                                p.op("dve", lambda e, ps=ps, ch=ch, t0=t0, n=n: e.tensor_copy(out=ut[:, ch, t0:t0 + n], in_=ps[:, 0:n]),
                                     r=[pk], w=[f"ut{ch}"])
                        if standalone:
                            p.dma("sp", F["uo"][ch, :, :], ut[:, ch, :], r=[f"ut{ch}"], w=[f"uo{ch}"], sem="dout")
                        else:
                            utp = F["utp"]
                            p.op("act", lambda e, ch=ch: e.activation(out=utp[:, ch, 0:TL].rearrange("p (j c) -> p c j", j=8),
                                                                      in_=ut[:, ch, 0:TL].rearrange("p (c j) -> p c j", j=8), func=AF.Identity),
                                 r=[f"ut{ch}"], w=[f"utp{ch}"])
                            p.op("dve", lambda e, ch=ch: e.tensor_copy(out=utp[:, ch, TL:TT].rearrange("p (j c) -> p c j", j=8),
                                                                       in_=ut[:, ch, TL:TT].rearrange("p (c j) -> p c j", j=8)),
                                 r=[f"ut{ch}"], w=[f"utp{ch}"])
                            p.dma("sp", F["usend"][ch * 128:(ch + 1) * 128, :], utp[:, ch, 0:TL], r=[f"utp{ch}"], w=[f"usend{ch}"])
                            p.dma("act", F["uctx"][ch * 128:(ch + 1) * 128, :], utp[:, ch, TL:TT], r=[f"utp{ch}"], w=["uctx"])
                            if ch in (2, 5, 7):
                                F["cc_piece"]({2: 0, 5: 1, 7: 2}[ch])
                if standalone:
                    p.wait_all("sp", [f"uo{ch}" for ch in range(8)])
                _barrier(p)
            p.stack = st
            return

        part = F.get("part", 0)
        if "o" in F:
            o = F["o"]
            pvt = F["pvt"]
        else:
            o = p.sb("o", [128, 16, TT], BF16)
            pvt = p.sb("pvt", [128, 8, 5], F32)
            p.dma("sp", pvt[:], F["pv"][:, :, :], w=["pvt"])
        n1 = 0 if part == 2 else 1
        n2 = 0 if part == 1 else 1
        blk = blocks if with_ctx else blocks[:2]
        wo_early = None
        if part == 2:
            wo_early = p.sb("wo", [128, 16, D], BF16)
        with contextlib.ExitStack() as st2:
            p.stack = st2
            wb = [p.sb(f"wb{i}", [128, 16, 256], BF16) for i in range(3 * n1)]
            if F.get("hwcast"):
                F["wst"] = [p.sb(f"wst{i}", [128, 16, 256], F32) for i in range(2)]
            gel = p.sb("gel", [128, 8, TT], BF16) if n2 else None
            wg = p.sb("wg", [128, 8, DS5], BF16) if n2 else None
            vt = [p.sb(f"vt{i}", [128, 512], BF16) for i in range(2 * n1)]
            s_t = [p.sb(f"s_t{i}", [128, 512], F32) for i in range(2 * n1)]
            a_t = [p.sb(f"a_t{i}", [128, 512], F32) for i in range(2 * n1)]
            sz = [p.sb(f"sz{i}", [128, 512], BF16) for i in range(2 * n1)]
            if standalone:
                for ch in range(8):
                    p.dma("sp", gel[:, ch, :], F["yin"][ch, :, :], w=[f"gel{ch}"])
            elif n2:
                F["load_gel"](gel)
                if wo_early is not None:
                    for h in range(4):
                        p.dma("pool", wo_early[:, 4 * h:4 * h + 4, :], F["wout"][4 * h:4 * h + 4, :, :].rearrange("k p n -> p k n"), w=[f"wo{h}"], sem=f"dwo{h}")
            ytmp = [p.sb(f"ytmp{i}", [128, 512], F32) for i in range(2 * n2)]
            ysq = [p.sb(f"ysq{i}", [128, 512], F32) for i in range(2 * n2)]
            sgt = [p.sb(f"sgt{i}", [128, 512], BF16) for i in range(2 * n2)]
            for h in range(2 * n2):
                p.dma("pool", wg[:, 4 * h:4 * h + 4, :], F["wglu"][4 * h:4 * h + 4, :, :].rearrange("k p n -> p k n"), w=[f"wg{h}"], sem=f"dwg{h}")
            for pair in range(4 * n1):
                i = load_w(wb, DS5 + pair * 256)
                for cl in range(2):
                    ch = pair * 2 + cl
                    for (t0, n) in blk:
                        ps, pk = next_acc()
                        proj(wb[i], f"wb{i}", cl, t0, n, ps, pk)
                        b2 = pa["n"] % 2
                        p.op("act", lambda e, ps=ps, b2=b2, n=n: e.activation(out=sz[b2][:, 0:n], in_=ps[:, 0:n], func=AF.Sigmoid),
                             r=[pk], w=[f"sz{b2}"])
                        p.op("dve", lambda e, ps=ps, b2=b2, ch=ch, t0=t0, n=n: e.tensor_tensor(out=o[:, ch, t0:t0 + n], in0=ps[:, 0:n], in1=sz[b2][:, 0:n], op=ALU.mult),
                             r=[pk, f"sz{b2}"], w=[f"o{ch}"])
            if DEBUG_O == 2:
                for k in range(8):
                    p.dma("sp", F["odbg"][k, :, :], o[:, k, :], r=[f"o{k}"], w=[f"odbg{k}"])
            if n1 and "mid_hook" in F:
                F["mid_hook"]()
            it = 0
            for pair in range(4 * n1):
                iv = load_w(wb, 2 * DS5 + pair * 256)
                ic = load_w(wb, 2 * DS5 + 2 * 1024 + pair * 256)
                for cl in range(2):
                    k = pair * 2 + cl
                    if cl == 0:
                        pass
                    for (t0, n) in blk:
                        b2 = it % 2
                        it += 1
                        rl = 64 if t0 < TL else 256
                        ps, pk = next_acc()
                        proj(wb[iv], f"wb{iv}", cl, t0, n, ps, pk)
                        p.op("act", lambda e, ps=ps, b2=b2, n=n: e.activation(out=vt[b2][:, 0:n], in_=ps[:, 0:n], func=AF.Identity),
                             r=[pk], w=[f"vt{b2}"])
                        ps, pk = next_acc()
                        proj(wb[ic], f"wb{ic}", cl, t0, n, ps, pk)
                        p.op("dve", lambda e, ps=ps, b2=b2, n=n: e.tensor_tensor(out=s_t[b2][:, 0:n], in0=ps[:, 0:n], in1=vt[b2][:, 0:n], op=ALU.mult),
                             r=[pk, f"vt{b2}"], w=[f"s_t{b2}"])
                        p.op("dve", lambda e, b2=b2, n=n, k=k: e.tensor_scalar(out=a_t[b2][:, 0:n], in0=s_t[b2][:, 0:n], scalar1=pvt[:, k, 2:3], scalar2=pvt[:, k, 4:5],
                                                                              op0=ALU.mult, op1=ALU.add),
                             r=[f"s_t{b2}", "pvt"], w=[f"a_t{b2}"])
                        sv = s_t[b2][:, 0:n].rearrange("p (r c) -> p r c", c=rl)
                        av = a_t[b2][:, 0:n].rearrange("p (r c) -> p r c", c=rl)
                        p.op("dve", lambda e, sv=sv, av=av, k=k, rl=rl: e.scalar_tensor_tensor(
                            out=av[:, :, 1:rl], in0=sv[:, :, 0:rl - 1], scalar=pvt[:, k, 1:2], in1=av[:, :, 1:rl], op0=ALU.mult, op1=ALU.add),
                            r=[f"s_t{b2}", f"a_t{b2}", "pvt"], w=[f"a_t{b2}"])
                        p.op("dve", lambda e, sv=sv, av=av, k=k, rl=rl: e.scalar_tensor_tensor(
                            out=av[:, :, 0:rl - 1], in0=sv[:, :, 1:rl], scalar=pvt[:, k, 3:4], in1=av[:, :, 0:rl - 1], op0=ALU.mult, op1=ALU.add),
                            r=[f"s_t{b2}", f"a_t{b2}", "pvt"], w=[f"a_t{b2}"])
                        p.op("dve", lambda e, b2=b2, n=n, k=k, t0=t0: e.tensor_copy(out=o[:, 8 + k, t0:t0 + n], in_=a_t[b2][:, 0:n]),
                             r=[f"a_t{b2}"], w=[f"o{8 + k}"])
            it = 0
            for pair in range(4 * n1):
                ib = load_w(wb, 2 * DS5 + 1024 + pair * 256)
                iz = load_w(wb, 2 * DS5 + 3 * 1024 + pair * 256)
                for cl in range(2):
                    k = pair * 2 + cl
                    for (t0, n) in blk:
                        b2 = it % 2
                        it += 1
                        ps, pk = next_acc()
                        proj(wb[iz], f"wb{iz}", cl, t0, n, ps, pk)
                        p.op("act", lambda e, ps=ps, b2=b2, n=n: e.activation(out=sz[b2][:, 0:n], in_=ps[:, 0:n], func=AF.Sigmoid),
                             r=[pk], w=[f"sz{b2}"])
                        p.op("dve", lambda e, ps=ps, b2=b2, n=n: e.tensor_tensor(out=sz[b2][:, 0:n], in0=ps[:, 0:n], in1=sz[b2][:, 0:n], op=ALU.mult),
                             r=[pk, f"sz{b2}"], w=[f"sz{b2}"])
                        ps, pk = next_acc()
                        proj(wb[ib], f"wb{ib}", cl, t0, n, ps, pk)
                        p.op("dve", lambda e, ps=ps, b2=b2, n=n, k=k, t0=t0: e.tensor_tensor(out=vt[b2][:, 0:n], in0=ps[:, 0:n], in1=o[:, 8 + k, t0:t0 + n], op=ALU.mult),
                             r=[pk, f"o{8 + k}"], w=[f"vt{b2}"])
                        p.op("dve", lambda e, b2=b2, n=n, k=k, t0=t0: e.tensor_tensor(out=o[:, 8 + k, t0:t0 + n], in0=vt[b2][:, 0:n], in1=sz[b2][:, 0:n], op=ALU.mult),
                             r=[f"vt{b2}", f"sz{b2}"], w=[f"o{8 + k}"])
            itg = {"n": 0}

            def gelu_blk(t0, n):
                for ch in range(8):
                    b2 = itg["n"] % 2
                    itg["n"] += 1
                    g = gel[:, ch, t0:t0 + n]
                    p.op("act", lambda e, g=g, b2=b2, n=n: e.activation(out=ysq[b2][:, 0:n], in_=g, func=AF.Square),
                         r=[f"gel{ch}"], w=[f"ysq{b2}"])
                    p.op("dve", lambda e, b2=b2, n=n: e.tensor_scalar(out=ysq[b2][:, 0:n], in0=ysq[b2][:, 0:n], scalar1=0.044715, scalar2=1.0, op0=ALU.mult, op1=ALU.add),
                         r=[f"ysq{b2}"], w=[f"ysq{b2}"])
                    p.op("dve", lambda e, g=g, b2=b2, n=n: e.tensor_tensor(out=ytmp[b2][:, 0:n], in0=ysq[b2][:, 0:n], in1=g, op=ALU.mult),
                         r=[f"ysq{b2}", f"gel{ch}"], w=[f"ytmp{b2}"])
                    p.op("act", lambda e, b2=b2, n=n: e.activation(out=ytmp[b2][:, 0:n], in_=ytmp[b2][:, 0:n], func=AF.Sigmoid, scale=1.5957691216057308),
                         r=[f"ytmp{b2}"], w=[f"ytmp{b2}"])
                    p.op("dve", lambda e, g=g, b2=b2, n=n: e.tensor_tensor(out=g, in0=ytmp[b2][:, 0:n], in1=g, op=ALU.mult),
                         r=[f"ytmp{b2}", f"gel{ch}"], w=[f"gelb{ch}_{t0}"])
            itu = {"n": 0}

            def glu_blk(t0, n):
                for nch in range(8):
                    b2 = itu["n"] % 2
                    itu["n"] += 1
                    ps, pk = next_acc()
                    for kf in range(8):
                        p.op("pe", lambda e, ps=ps, kf=kf, nch=nch, t0=t0, n=n: e.matmul(
                            ps[:, 0:n], lhsT=wg[:, kf, nch * 128:(nch + 1) * 128], rhs=gel[:, kf, t0:t0 + n], start=(kf == 0), stop=(kf == 7)),
                            r=[f"wg{kf // 4}", f"gelb{kf}_{t0}"], w=[pk])
                    p.op("act", lambda e, ps=ps, b2=b2, n=n, nch=nch: e.activation(out=sgt[b2][:, 0:n], in_=ps[:, 0:n], func=AF.Sigmoid, bias=pvt[:, nch, 0:1], scale=1.0),
                         r=[pk, "pvt"], w=[f"sgt{b2}"])
                    p.op("dve", lambda e, b2=b2, n=n, nch=nch, t0=t0: e.tensor_tensor(out=sgt[b2][:, 0:n], in0=sgt[b2][:, 0:n], in1=gel[:, nch, t0:t0 + n], op=ALU.mult),
                         r=[f"sgt{b2}", f"gelb{nch}_{t0}"], w=[f"sgt{b2}"])
                    p.op("dve", lambda e, b2=b2, n=n, nch=nch, t0=t0: e.tensor_tensor(out=o[:, nch, t0:t0 + n], in0=sgt[b2][:, 0:n], in1=o[:, nch, t0:t0 + n], op=ALU.mult),
                         r=[f"sgt{b2}", f"o{nch}"], w=[f"o{nch}"])
            if n2:
                gelu_blk(*blk[0])
                for bi_ in range(len(blk)):
                    if bi_ + 1 < len(blk):
                        gelu_blk(*blk[bi_ + 1])
                    glu_blk(*blk[bi_])
            if DEBUG_O:
                for k in range(16 if DEBUG_O == 1 else 0):
                    p.dma("sp", F["odbg"][k, :, :], o[:, k, :], r=[f"o{k}"], w=[f"odbg{k}"])
                p.wait_all("sp", [f"odbg{k}" for k in range(16 if DEBUG_O == 1 else 8)])
            _barrier(p)
        p.stack = st
        if part == 1:
            return
        with contextlib.ExitStack() as st3:
            p.stack = st3
            wo = wo_early if wo_early is not None else p.sb("wo", [128, 16, D], BF16)
            gb = p.sb("gb", [128, 2, D], F32)
            lb = p.sb("lb", [128, 2, D], F32)
            xt2 = [p.sb("x2t0", [128, D], F32)] * 2
            vv = [p.sb(f"vv{i}", [128, D], F32) for i in range(2)]
            stt2 = [p.sb(f"st2{i}", [128, 4, 6], F32) for i in range(2)]
            mv2 = [p.sb(f"mv2{i}", [128, 2], F32) for i in range(2)]
            rs2 = [p.sb(f"rs2{i}", [128, 1], F32) for i in range(2)]
            for h in range(4 if wo_early is None else 0):
                p.dma("pool", wo[:, 4 * h:4 * h + 4, :], F["wout"][4 * h:4 * h + 4, :, :].rearrange("k p n -> p k n"), w=[f"wo{h}"], sem=f"dwo{h}")
            if standalone:
                p.dma("sp", gb[:], F["gbc"][:, :, :], w=["gb"])
            else:
                F["make_gb"](gb)
            p.dma("sp", lb[:], F["lnb"][:, :, :], w=["lb"])
            okeys = [f"o{k}" for k in range(16)]
            for t in range(ntok // 128):
                b = t % 2
                which = 0 if t < TL // 128 else 1
                p.dma("sp", xt2[b][:], xin[t * 128:(t + 1) * 128, :], w=["x2t"], sem="dx2")
                for nb in range(4):
                    ps, pk = next_acc()
                    for kf in range(16):
                        p.op("pe", lambda e, ps=ps, kf=kf, t=t, nb=nb: e.matmul(
                            ps[:, :], lhsT=o[:, kf, t * 128:(t + 1) * 128], rhs=wo[:, kf, nb * 512:(nb + 1) * 512], start=(kf == 0), stop=(kf == 15)),
                            r=[f"o{kf}", f"wo{kf // 4}"], w=[pk])
                    p.op("dve", lambda e, ps=ps, b=b, nb=nb, which=which: e.tensor_tensor(
                        out=vv[b][:, nb * 512:(nb + 1) * 512], in0=ps[:, :], in1=gb[:, which, nb * 512:(nb + 1) * 512], op=ALU.mult),
                        r=[pk, "gb"], w=[f"vv{b}"])
                    p.op("dve", lambda e, b=b, nb=nb: e.scalar_tensor_tensor(
                        out=vv[b][:, nb * 512:(nb + 1) * 512], in0=xt2[b][:, nb * 512:(nb + 1) * 512], scalar=ALPHA, in1=vv[b][:, nb * 512:(nb + 1) * 512],
                        op0=ALU.mult, op1=ALU.add),
                        r=["x2t", f"vv{b}"], w=[f"vv{b}"])
                    p.op("dve", lambda e, b=b, nb=nb: e.bn_stats(out=stt2[b][:, nb, :], in_=vv[b][:, nb * 512:(nb + 1) * 512]),
                         r=[f"vv{b}"], w=[f"st2{b}"])
                p.op("dve", lambda e, b=b: e.bn_aggr(out=mv2[b][:], in_=stt2[b][:].rearrange("p a s -> p (a s)")),
                     r=[f"st2{b}"], w=[f"mv2{b}"])
                p.op("act", lambda e, b=b: e.activation(out=rs2[b][:], in_=mv2[b][:, 1:2], func=AF.Sqrt, bias=EPS, scale=1.0),
                     r=[f"mv2{b}"], w=[f"rs2{b}"])
                p.op("dve", lambda e, b=b: e.reciprocal(out=rs2[b][:], in_=rs2[b][:]), r=[f"rs2{b}"], w=[f"rs2{b}"])
                p.op("dve", lambda e, b=b: e.tensor_scalar(out=vv[b][:], in0=vv[b][:], scalar1=mv2[b][:, 0:1], scalar2=rs2[b][:, 0:1],
                                                           op0=ALU.subtract, op1=ALU.mult),
                     r=[f"vv{b}", f"mv2{b}", f"rs2{b}"], w=[f"vv{b}"])
                p.op("dve", lambda e, b=b: e.tensor_tensor(out=vv[b][:], in0=vv[b][:], in1=lb[:, 0, :], op=ALU.mult),
                     r=[f"vv{b}", "lb"], w=[f"vv{b}"])
                p.op("dve", lambda e, b=b: e.tensor_tensor(out=vv[b][:], in0=vv[b][:], in1=lb[:, 1, :], op=ALU.add),
                     r=[f"vv{b}", "lb"], w=[f"vv{b}"])
                p.dma("sp", (F["xo"] if t < TL // 128 else F.get("xo_ctx", F["xo"]))[t * 128:(t + 1) * 128, :], vv[b][:], r=[f"vv{b}"], w=[f"xo{t}"], sem=f"do{b}")
            p.wait_all("sp", [f"xo{t}" for t in range(ntok // 128)])
            _barrier(p)
        p.stack = st


_IDENT = np.eye(128, dtype=np.float32)


def _ms_layout(mod_l):
    out = np.empty((128, 16, 4), np.float32)
    for w in range(2):
        out[:, :, 2 * w] = mod_l[w, 0:D].reshape(16, 128).T
        out[:, :, 2 * w + 1] = mod_l[w, D:2 * D].reshape(16, 128).T
    return out


def run_tok_A(xs, mod_l, w_in_l, with_ctx=True):
    nc = build_tok("A", True)
    ms = _ms_layout(mod_l)
    win = w_in_l.reshape(16, 128, DIN)
    in_maps = [{"xin": xs[k], "ms": ms, "win": win, "ident": _IDENT} for k in range(NC)]
    res = _run(nc, in_maps)
    return [r["uo"] for r in res]


def run_tok_C(xs, ys, mod_l, w_in_l, w_glu_l, b_glu_l, conv_w_l, conv_b_l, w_out_l, ln_g_l, ln_b_l, with_ctx):
    nc = build_tok("C", with_ctx)
    ms = _ms_layout(mod_l)
    win = w_in_l.reshape(16, 128, DIN)
    gbc = np.ascontiguousarray(np.broadcast_to(mod_l[:, 2 * D:3 * D][None, :, :], (128, 2, D)))
    lnb = np.ascontiguousarray(np.broadcast_to(np.stack([ln_g_l, ln_b_l])[None, :, :], (128, 2, D)))
    pv = np.empty((128, 8, 5), np.float32)
    pv[:, :, 0] = b_glu_l.reshape(8, 128).T
    for j in range(3):
        pv[:, :, 1 + j] = conv_w_l[j].reshape(8, 128).T
    pv[:, :, 4] = conv_b_l.reshape(8, 128).T
    wglu = w_glu_l.reshape(8, 128, DS5)
    wout = w_out_l.reshape(16, 128, D)
    in_maps = [{"xin": xs[k], "ms": ms, "win": win, "ident": _IDENT, "yin": ys[k], "gbc": gbc, "lnb": lnb,
                "pv": pv, "wglu": wglu, "wout": wout} for k in range(NC)]
    res = _run(nc, in_maps)
    if DEBUG_O:
        return [r["xo"] for r in res], [r["odbg"] for r in res]
    return [r["xo"] for r in res]


SEGS = [(0, 32), (32, 512), (544, 512)]


def build_s5(stage=9):
    nc = _new_nc()
    u2 = nc.dram_tensor("u2", [GP, 128, NCH], BF16, kind="ExternalInput").ap()
    par = nc.dram_tensor("par", [128, GP, 3], F32, kind="ExternalInput").ap()
    bri = nc.dram_tensor("bri", [128, GP, 16, 2], F32, kind="ExternalInput").ap()
    cri = nc.dram_tensor("cri", [128, GP, 16, 2], F32, kind="ExternalInput").ap()
    dcol = nc.dram_tensor("dcol", [128, GP], F32, kind="ExternalInput").ap()
    cst = nc.dram_tensor("cst", [128, 3, 128], F32, kind="ExternalInput").ap()
    y2 = nc.dram_tensor("y2", [GP, 128, NCH], BF16, kind="ExternalOutput").ap()

    with contextlib.ExitStack() as st:
        p = Prog(nc, st)
        pt = [p.ps(f"pt{i}", [128, 512]) for i in range(2)]
        psS = [p.ps(f"psS{i}", [128, 512]) for i in range(2)]
        psY = [p.ps(f"psY{i}", [128, 512]) for i in range(2)]
        F = dict(p=p, st=st, standalone=True, stage=stage, u2=u2, par=par, bri=bri, cri=cri, dcol=dcol, cst=cst, y2=y2,
                 pt=pt, psS=psS, psY=psY, ptk=["pt0", "pt1"], psSk=["psS0", "psS1"], psYk=["psY0", "psY1"])
        _s5_body(F)
        p.emit()
    return nc


def _s5_body(F):
    p = F["p"]; st = F["st"]; stage = F["stage"]; standalone = F["standalone"]
    par = F["par"]; bri = F["bri"]; cri = F["cri"]; dcol = F["dcol"]; cst = F["cst"]
    phase = F.get("phase", "all")
    if True:
        if phase != "main" and "wts" not in F:
            p.stack = F.get("wstack", st)
            BS = p.sb("BS", [128, GP, 2, 128], BF16)
            CAf = p.sb("CAf", [128, GP, 2, 128], BF16)
            CAb = p.sb("CAb", [128, GP, 2, 128], BF16)
            Tbf = p.sb("Tbf", [128, GP, 128], BF16)
            A1 = p.sb("A1", [128, GP, 2], F32)
            A2 = p.sb("A2", [128, GP, 2], F32)
            B1 = p.sb("B1", [128, GP, 2], F32)
            B2 = p.sb("B2", [128, GP, 2], F32)
            TAB1 = p.sb("TAB1", [128, GP, 32, 2], F32)
            TAB2 = p.sb("TAB2", [128, GP, 32, 2], F32)
            cs = p.sb("cs", [128, 3, 128], F32)
            dc = p.sb("dc", [128, GP], F32)
            F["wts"] = (BS, CAf, CAb, Tbf, A1, A2, B1, B2, TAB1, TAB2, cs, dc)
            p.stack = st
        else:
            (BS, CAf, CAb, Tbf, A1, A2, B1, B2, TAB1, TAB2, cs, dc) = F["wts"]
        if phase == "alloc":
            return
        if phase not in ("derive",):
            U = p.sb("U", [128, GP, NCH], BF16)
            G = p.sb("G", [128, GP, NB, 2], F32)
        pt = F["pt"]
        psS = F["psS"]
        psY = F["psY"]
        ptk = F["ptk"]; psSk = F["psSk"]; psYk = F["psYk"]
        if standalone:
            for g in range(GP):
                p.dma("sp", U[:, g, :], F["u2"][g, :, :], w=[f"U{g}"])
        if phase in ("all", "derive"):
            p.dma("sp", cs[:], cst[:, :, :], w=["cs"])
            p.dma("sp", dc[:], dcol[:, :], w=["dc"])

        def tt(eng, out, a, b, op, r, w):
            p.op(eng, lambda e: e.tensor_tensor(out=out, in0=a, in1=b, op=op), r=r, w=w)

        def ts(eng, out, a, s1, s2, op0, op1, r, w):
            if s2 is None:
                p.op(eng, lambda e: e.tensor_scalar(out=out, in0=a, scalar1=s1, scalar2=None, op0=op0), r=r, w=w)
            else:
                p.op(eng, lambda e: e.tensor_scalar(out=out, in0=a, scalar1=s1, scalar2=s2, op0=op0, op1=op1), r=r, w=w)

        def cp(eng, out, a, r, w):
            if eng == "act":
                p.op(eng, lambda e: e.activation(out=out, in_=a, func=AF.Identity), r=r, w=w)
            else:
                p.op(eng, lambda e: e.tensor_copy(out=out, in_=a), r=r, w=w)

        if phase == "main":
            with contextlib.ExitStack() as stu:
                p.stack = stu
                F["load_U"](U)
                _barrier(p)
            p.stack = st
        for st1 in _maybe_stack(phase != "main"):
            p.stack = st1
            if phase == "all" and not standalone:
                F["load_U"](U)
            pr = p.sb("pr", [128, GP, 3], F32)
            Bt = p.sb("Bt", [128, GP, 16, 2], F32)
            Ct = p.sb("Ct", [128, GP, 16, 2], F32)
            p.dma("sp", pr[:], par[:, :, :], w=["pr"])
            p.dma("sp", Bt[:], bri[:, :, :, :], w=["Bt"])
            p.dma("sp", Ct[:], cri[:, :, :, :], w=["Ct"])
            sm = {}
            for nm in ("dt", "xr", "th", "mag", "t", "tf", "fr", "m", "sn", "cn", "are", "aim", "n2", "rn", "ire", "iim",
                       "nre", "clre", "clim", "wre", "wim"):
                sm[nm] = p.sb("s_" + nm, [128, GP], F32)
            ti = p.sb("s_ti", [128, GP], I32)
            Pre = p.sb("Pre", [128, GP, 8], F32)
            Pim = p.sb("Pim", [128, GP, 8], F32)
            Tre = p.sb("Tre", [128, GP, 32], F32)
            Tim = p.sb("Tim", [128, GP, 32], F32)
            PWre = p.sb("PWre", [128, GP, 8], F32)
            PWim = p.sb("PWim", [128, GP, 8], F32)
            EXre = p.sb("EXre", [128, GP, 8], F32)
            EXim = p.sb("EXim", [128, GP, 8], F32)
            Bbre = p.sb("Bbre", [128, GP, 16], F32)
            Bbim = p.sb("Bbim", [128, GP, 16], F32)
            Xre = p.sb("Xre", [128, GP, 8, 16], F32)
            Xim = p.sb("Xim", [128, GP, 8, 16], F32)
            Cre = p.sb("CAre", [128, GP, 8, 16], F32)
            Cim = p.sb("CAim", [128, GP, 8, 16], F32)
            Rre = p.sb("Rre", [128, GP, 8, 16], F32)
            Rim = p.sb("Rim", [128, GP, 8, 16], F32)
            R2re = p.sb("R2re", [128, GP, 8, 16], F32)
            R2im = p.sb("R2im", [128, GP, 8, 16], F32)
            tq = [p.sb(f"tq{i}", [128, GP * 128], F32) for i in range(4)]

            def cmul(o_re, o_im, okeys, a_re, a_im, akeys, b_re, b_im, bkeys, shape, eng="dve"):
                n = int(np.prod(shape))
                pat = {1: None, 2: "p (a b) -> p a b", 3: "p (a b c) -> p a b c"}[len(shape)]
                kw = dict(zip("abc", shape))
                tv = [t[:, 0:n] if len(shape) == 1 else t[:, 0:n].rearrange(pat, **kw) for t in tq]
                tk = [f"tq{i}" for i in range(4)]
                tt(eng, tv[0], a_re, b_re, ALU.mult, akeys + bkeys, [tk[0]])
                tt(eng, tv[1], a_im, b_im, ALU.mult, akeys + bkeys, [tk[1]])
                tt(eng, tv[2], a_re, b_im, ALU.mult, akeys + bkeys, [tk[2]])
                tt(eng, tv[3], a_im, b_re, ALU.mult, akeys + bkeys, [tk[3]])
                tt(eng, o_re, tv[0], tv[1], ALU.subtract, [tk[0], tk[1]], okeys)
                tt(eng, o_im, tv[2], tv[3], ALU.add, [tk[2], tk[3]], okeys)

            s = sm
            p.op("act", lambda e: e.activation(out=s["dt"][:], in_=pr[:, :, 2], func=AF.Exp), r=["pr"], w=["dt"])
            tt("dve", s["xr"][:], s["dt"][:], pr[:, :, 0], ALU.mult, ["dt", "pr"], ["xr"])
            tt("dve", s["th"][:], s["dt"][:], pr[:, :, 1], ALU.mult, ["dt", "pr"], ["th"])
            p.op("act", lambda e: e.activation(out=s["mag"][:], in_=s["xr"][:], func=AF.Exp), r=["xr"], w=["mag"])

            def sin_of(off, out_nm):
                ts("dve", s["t"][:], s["th"][:], 1.0 / TWO_PI, off, ALU.mult, ALU.add, ["th"], ["t"])
                cp("dve", ti[:], s["t"][:], ["t"], ["ti"])
                cp("dve", s["tf"][:], ti[:], ["ti"], ["tf"])
                tt("dve", s["fr"][:], s["t"][:], s["tf"][:], ALU.subtract, ["t", "tf"], ["fr"])
                ts("dve", s["m"][:], s["fr"][:], 0.0, None, ALU.is_lt, None, ["fr"], ["m"])
                tt("dve", s["fr"][:], s["fr"][:], s["m"][:], ALU.add, ["fr", "m"], ["fr"])
                ts("dve", s["m"][:], s["fr"][:], 1.0, None, ALU.is_ge, None, ["fr"], ["m"])
                tt("dve", s["fr"][:], s["fr"][:], s["m"][:], ALU.subtract, ["fr", "m"], ["fr"])
                ts("dve", s["fr"][:], s["fr"][:], TWO_PI, -np.pi, ALU.mult, ALU.add, ["fr"], ["fr"])
                ts("dve", s["fr"][:], s["fr"][:], 3.1415925, -3.1415925, ALU.min, ALU.max, ["fr"], ["fr"])
                p.op("act", lambda e: e.activation(out=s[out_nm][:], in_=s["fr"][:], func=AF.Sin), r=["fr"], w=[out_nm])

            sin_of(0.5, "sn")
            sin_of(0.75, "cn")
            tt("dve", s["are"][:], s["mag"][:], s["cn"][:], ALU.mult, ["mag", "cn"], ["are"])
            tt("dve", s["aim"][:], s["mag"][:], s["sn"][:], ALU.mult, ["mag", "sn"], ["aim"])
            cp("dve", Pre[:, :, 0], s["are"][:], ["are"], ["P"])
            cp("dve", Pim[:, :, 0], s["aim"][:], ["aim"], ["P"])
            cmul(Pre[:, :, 1], Pim[:, :, 1], ["P"], s["are"][:], s["aim"][:], ["are", "aim"], s["are"][:], s["aim"][:], [], [GP])
            for w_ in (2, 4):
                bsh = [128, GP, w_]
                cmul(Pre[:, :, w_:2 * w_], Pim[:, :, w_:2 * w_], ["P"], Pre[:, :, 0:w_], Pim[:, :, 0:w_], ["P"],
                     Pre[:, :, w_ - 1:w_].broadcast_to(bsh), Pim[:, :, w_ - 1:w_].broadcast_to(bsh), [], [GP, w_])
            cp("dve", Tre[:, :, 0], Pre[:, :, 7], ["P"], ["T"])
            cp("dve", Tim[:, :, 0], Pim[:, :, 7], ["P"], ["T"])
            cmul(Tre[:, :, 1], Tim[:, :, 1], ["T"], Pre[:, :, 7], Pim[:, :, 7], ["P"], Pre[:, :, 7], Pim[:, :, 7], [], [GP])
            for w_ in (2, 4, 8, 16):
                bsh = [128, GP, w_]
                cmul(Tre[:, :, w_:2 * w_], Tim[:, :, w_:2 * w_], ["T"], Tre[:, :, 0:w_], Tim[:, :, 0:w_], ["T"],
                     Tre[:, :, w_ - 1:w_].broadcast_to(bsh), Tim[:, :, w_ - 1:w_].broadcast_to(bsh), [], [GP, w_])
            for (t1, t2, sre, sim, key) in ((A1, A2, Pre[:, :, 7], Pim[:, :, 7], "P"), (B1, B2, Tre[:, :, 31], Tim[:, :, 31], "T")):
                cp("dve", t1[:, :, 0], sre, [key], ["AB"])
                cp("dve", t1[:, :, 1], sre, [key], ["AB"])
                ts("dve", t2[:, :, 0], sim, -1.0, None, ALU.mult, None, [key], ["AB"])
                cp("dve", t2[:, :, 1], sim, [key], ["AB"])
            cp("dve", TAB1[:, :, :, 0], Tre[:], ["T"], ["TAB"])
            cp("dve", TAB1[:, :, :, 1], Tre[:], ["T"], ["TAB"])
            ts("dve", TAB2[:, :, :, 0], Tim[:], -1.0, None, ALU.mult, None, ["T"], ["TAB"])
            cp("dve", TAB2[:, :, :, 1], Tim[:], ["T"], ["TAB"])
            tt("dve", s["n2"][:], Pre[:, :, 7], Pre[:, :, 7], ALU.mult, ["P"], ["n2"])
            tt("dve", s["rn"][:], Pim[:, :, 7], Pim[:, :, 7], ALU.mult, ["P"], ["rn"])
            tt("dve", s["n2"][:], s["n2"][:], s["rn"][:], ALU.add, ["n2", "rn"], ["n2"])
            p.op("dve", lambda e: e.reciprocal(out=s["rn"][:], in_=s["n2"][:]), r=["n2"], w=["rn"])
            tt("dve", s["ire"][:], Pre[:, :, 7], s["rn"][:], ALU.mult, ["P", "rn"], ["ire"])
            tt("dve", s["iim"][:], Pim[:, :, 7], s["rn"][:], ALU.mult, ["P", "rn"], ["iim"])
            ts("dve", s["iim"][:], s["iim"][:], -1.0, None, ALU.mult, None, ["iim"], ["iim"])
            tt("dve", s["n2"][:], pr[:, :, 0], pr[:, :, 0], ALU.mult, ["pr", "ire", "iim"], ["n2"])
            tt("dve", s["rn"][:], pr[:, :, 1], pr[:, :, 1], ALU.mult, ["pr"], ["rn"])
            tt("dve", s["n2"][:], s["n2"][:], s["rn"][:], ALU.add, ["n2", "rn"], ["n2"])
            p.op("dve", lambda e: e.reciprocal(out=s["rn"][:], in_=s["n2"][:]), r=["n2"], w=["rn"])
            tt("dve", s["clre"][:], pr[:, :, 0], s["rn"][:], ALU.mult, ["pr", "rn"], ["clre"])
            tt("dve", s["clim"][:], pr[:, :, 1], s["rn"][:], ALU.mult, ["pr", "rn"], ["clim"])
            ts("dve", s["clim"][:], s["clim"][:], -1.0, None, ALU.mult, None, ["clim"], ["clim"])
            ts("dve", s["nre"][:], s["are"][:], -1.0, None, ALU.add, None, ["are"], ["nre"])
            cmul(s["wre"][:], s["wim"][:], ["w"], s["nre"][:], s["aim"][:], ["nre", "aim"], s["clre"][:], s["clim"][:], ["clre", "clim"], [GP])
            bsh = [128, GP, 16]
            cmul(Bbre[:], Bbim[:], ["Bb"], Bt[:, :, :, 0], Bt[:, :, :, 1], ["Bt"],
                 s["wre"][:].unsqueeze(2).broadcast_to(bsh), s["wim"][:].unsqueeze(2).broadcast_to(bsh), ["w"], [GP, 16])
            p.op("dve", lambda e: e.memset(PWre[:], 1.0), w=["PW"])
            p.op("dve", lambda e: e.memset(PWim[:], 0.0), w=["PW"])
            for (dst, src) in ((PWre, Pre), (PWim, Pim)):
                for j in range(7):
                    cp("dve", dst[0:64, :, j], src[0:64, :, 6 - j], ["P"], ["PW"])
                cp("dve", dst[64:128, :, 1:8], src[64:128, :, 0:7], ["P"], ["PW"])
            for (dst, src) in ((EXre, Pre), (EXim, Pim)):
                cp("dve", dst[0:64, :, :], src[0:64, :, :], ["P"], ["EX"])
                for i in range(8):
                    cp("dve", dst[64:128, :, i], src[64:128, :, 7 - i], ["P"], ["EX"])
            xsh = [128, GP, 8, 16]
            cmul(Xre[:], Xim[:], ["X"], PWre[:].unsqueeze(3).broadcast_to(xsh), PWim[:].unsqueeze(3).broadcast_to(xsh), ["PW"],
                 Bbre[:].unsqueeze(2).broadcast_to(xsh), Bbim[:].unsqueeze(2).broadcast_to(xsh), ["Bb"], [GP, 8, 16])
            cmul(Cre[:], Cim[:], ["CA"], EXre[:].unsqueeze(3).broadcast_to(xsh), EXim[:].unsqueeze(3).broadcast_to(xsh), ["EX"],
                 Ct[:, :, :, 0].unsqueeze(2).broadcast_to(xsh), Ct[:, :, :, 1].unsqueeze(2).broadcast_to(xsh), ["Ct"], [GP, 8, 16])
            g3 = [128, GP, 128]
            cmul(Rre[:].rearrange("p g i h -> p g (i h)"), Rim[:].rearrange("p g i h -> p g (i h)"), ["R"],
                 Cre[:].rearrange("p g i h -> p g (i h)"), Cim[:].rearrange("p g i h -> p g (i h)"), ["CA"],
                 s["ire"][:].unsqueeze(2).broadcast_to(g3), s["iim"][:].unsqueeze(2).broadcast_to(g3), ["ire", "iim"], [GP, 128])
            ts("dve", Rim[:], Rim[:], -1.0, None, ALU.mult, None, ["R"], ["R"])
            p.op("dve", lambda e: e.memset(CAf[:], 0.0), w=["CAs"])
            p.op("dve", lambda e: e.memset(CAb[:], 0.0), w=["CAs"])
            for (dstT, sl) in ((CAf, slice(0, 64)), (CAb, slice(64, 128))):
                cp("dve", dstT[sl, :, 0, :], Cre[sl].rearrange("p g i h -> p g (i h)"), ["CA"], ["CAs"])
                ts("dve", dstT[sl, :, 1, :], Cim[sl].rearrange("p g i h -> p g (i h)"), -1.0, None, ALU.mult, None, ["CA"], ["CAs"])
            p.op("dve", lambda e: e.memset(R2re[:], 0.0), w=["R2"])
            p.op("dve", lambda e: e.memset(R2im[:], 0.0), w=["R2"])
            cp("dve", R2re[64:128], Rre[64:128], ["R"], ["R2"])
            cp("dve", R2im[64:128], Rim[64:128], ["R"], ["R2"])
            p.op("dve", lambda e: e.memset(Rre[64:128], 0.0), r=["R2"], w=["R"])
            p.op("dve", lambda e: e.memset(Rim[64:128], 0.0), r=["R2"], w=["R"])
            for g in range(GP):
                for ri, Xp in enumerate((Xre, Xim)):
                    ptile = pt[(2 * g + ri) % 2]
                    pk = ptk[(2 * g + ri) % 2]
                    p.op("pe", lambda e, ptile=ptile, Xp=Xp, g=g: e.transpose(out=ptile[:, 0:128], in_=Xp[:, g].rearrange("p j h -> p (j h)"), identity=cs[:, 2, :]),
                         r=["X", "cs"], w=[pk])
                    cp("act" if ri == 0 else "dve", BS[:, g, ri, :], ptile[:, 0:128], [pk], [f"BS{g}"])
            for g in range(GP):
                for d in range(2):
                    ptile = pt[d]
                    pk = ptk[d]
                    Ra, Rb_ = (Rre, Rim) if d == 0 else (R2re, R2im)
                    p.op("pe", lambda e, ptile=ptile, Ra=Ra, g=g: e.matmul(ptile[:, 0:128], lhsT=Xre[:, g].rearrange("p j h -> p (j h)"),
                                                                           rhs=Ra[:, g].rearrange("p i h -> p (i h)"), start=True, stop=False),
                         r=["X", "R", "R2"], w=[pk])
                    p.op("pe", lambda e, ptile=ptile, Rb_=Rb_, g=g: e.matmul(ptile[:, 0:128], lhsT=Xim[:, g].rearrange("p j h -> p (j h)"),
                                                                             rhs=Rb_[:, g].rearrange("p i h -> p (i h)"), start=False, stop=True),
                         r=["X", "R", "R2"], w=[pk])
                tA = tq[0][:, 0:128]
                tB = tq[1][:, 0:128]
                tt("dve", tA, pt[0][:, 0:128], cs[:, 0, :], ALU.mult, [ptk[0], "cs"], ["tq0"])
                tt("dve", tB, pt[1][:, 0:128], cs[:, 1, :], ALU.mult, [ptk[1], "cs"], ["tq1"])
                tt("dve", tA, tA, tB, ALU.add, ["tq0", "tq1"], ["tq0"])
                p.op("dve", lambda e, g=g, tA=tA: e.scalar_tensor_tensor(out=Tbf[:, g, :], in0=cs[:, 2, :], scalar=dc[:, g:g + 1], in1=tA,
                                                                      op0=ALU.mult, op1=ALU.add),
                     r=["cs", "dc", "tq0"], w=[f"Tbf{g}"])
            if not F.get("no_barrier"):
                _barrier(p)
        p.stack = st

        if phase == "derive":
            return
        W = p.sb("W", [128, GP, NCH, 2], F32)
        ne = 0
        for g in range(GP if stage >= 2 else 0):
            for ri in range(2):
                for (c0, n) in SEGS:
                    ps = psS[ne % len(psS)]
                    pk = psSk[ne % len(psS)]
                    p.op("pe", lambda e, ps=ps, g=g, ri=ri, c0=c0, n=n: e.matmul(ps[:, 0:n], lhsT=BS[:, g, ri, :], rhs=U[:, g, c0:c0 + n], start=True, stop=True),
                         r=[f"BS{g}", f"U{g}"], w=[pk])
                    if c0 == 0:
                        qs = slice(31, None, -1)
                    else:
                        qs = slice(1087 - c0, 1087 - c0 - n, -1)
                    e1, e2 = ("act", "dve")
                    if e1 == "act":
                        p.op("act", lambda e, ps=ps, g=g, ri=ri, c0=c0, n=n: e.activation(out=W[0:64, g, c0:c0 + n, ri], in_=ps[0:64, 0:n], func=AF.Identity),
                             r=[pk], w=[f"W{g}"])
                        cp("dve", W[64:128, g, qs, ri], ps[64:128, 0:n], [pk], [f"W{g}"])
                    else:
                        cp("dve", W[0:64, g, c0:c0 + n, ri], ps[0:64, 0:n], [pk], [f"W{g}"])
                        p.op("act", lambda e, ps=ps, g=g, ri=ri, qs=qs, n=n: e.activation(out=W[64:128, g, qs, ri], in_=ps[64:128, 0:n], func=AF.Identity),
                             r=[pk], w=[f"W{g}"])
                    ne += 1

        with contextlib.ExitStack() as st2:
            p.stack = st2
            Hn = p.sb("Hn", [128, GP, NCH, 2], BF16)
            tmpd = p.sb("tmpd", [128, 9 * 32 * 2], F32)
            tmpp = p.sb("tmpp", [128, 9 * 32 * 2], F32)
            yb = [p.sb(f"yb{i}", [128, 512], BF16) for i in range(2)]
            Wv = W[:].rearrange("p g (b s) r -> p g b s r", s=32)
            Hv = Hn[:].rearrange("p g (b s) r -> p g b s r", s=32)
            splits = (("dve", 0, 5, tmpd, "tmpd"), ("pool", 5, 8, tmpp, "tmpp"))
            for (eng, g0, g1, tmp, tk) in (splits if stage >= 3 else ()):
                ng = g1 - g0
                wk = [f"W{g}" for g in range(g0, g1)]
                tv = tmp[:, 0:ng * NB * 2].rearrange("p (g b r) -> p g b r", g=ng, b=NB)
                sh = [128, ng, NB, 2]
                a1 = A1[:, g0:g1, :].unsqueeze(2).broadcast_to(sh)
                a2 = A2[:, g0:g1, :].unsqueeze(2).broadcast_to(sh)
                for s_ in range(1, 32):
                    prev = Wv[:, g0:g1, :, s_ - 1, :]
                    prevs = Wv[:, g0:g1, :, s_ - 1, ::-1]
                    cur = Wv[:, g0:g1, :, s_, :]
                    tt(eng, tv, prev, a1, ALU.mult, wk + ["AB"], [tk])
                    tt(eng, cur, cur, tv, ALU.add, wk + [tk], wk)
                    tt(eng, tv, prevs, a2, ALU.mult, wk + ["AB"], [tk])
                    tt(eng, cur, cur, tv, ALU.add, wk + [tk], wk)
            p.op("dve", lambda e: e.memset(G[:, :, 0, :], 0.0), w=["G"])
            allw = [f"W{g}" for g in range(GP)]
            tg = tmpd[:, 0:GP * 2].rearrange("p (g r) -> p g r", r=2)
            cp("dve", G[:, :, 1, :], Wv[:, :, 0, 31, :], allw, ["G"])
            for b in range(1, NB - 1 if stage >= 4 else 0):
                tt("dve", tg, G[:, :, b, :], B1[:], ALU.mult, ["G", "AB"], ["tmpd"])
                tt("dve", G[:, :, b + 1, :], tg, Wv[:, :, b, 31, :], ALU.add, ["tmpd"] + allw, ["G"])
                tt("dve", tg, G[:, :, b, ::-1], B2[:], ALU.mult, ["G", "AB"], ["tmpd"])
                tt("dve", G[:, :, b + 1, :], G[:, :, b + 1, :], tg, ALU.add, ["G", "tmpd"], ["G"])
            nyd = {"n": 0}

            def emit_out(g):
                for (c0, n) in SEGS:
                    ny = nyd["n"]
                    ps = psY[ny % 2]
                    pk = psYk[ny % 2]
                    ybt = yb[ny % 2]
                    yk = f"yb{ny % 2}"
                    ny += 1
                    nyd["n"] = ny
                    rk = [f"U{g}", f"Tbf{g}", f"Hn{g}", "CAs"]
                    p.op("pe", lambda e, ps=ps, g=g, c0=c0, n=n: e.matmul(ps[:, 0:n], lhsT=Tbf[:, g, :], rhs=U[:, g, c0:c0 + n], start=True, stop=False),
                         r=rk, w=[pk])
                    for ri in range(2):
                        if c0 == 0:
                            p.op("pe", lambda e, ps=ps, g=g, ri=ri, n=n: e.matmul(ps[:, 1:n], lhsT=CAf[:, g, ri, :], rhs=Hn[:, g, 0:n - 1, ri], start=False, stop=False),
                                 r=rk, w=[pk])
                        else:
                            p.op("pe", lambda e, ps=ps, g=g, ri=ri, c0=c0, n=n: e.matmul(ps[:, 0:n], lhsT=CAf[:, g, ri, :], rhs=Hn[:, g, c0 - 1:c0 + n - 1, ri], start=False, stop=False),
                                 r=rk, w=[pk])
                    for ri in range(2):
                        last = (ri == 1)
                        if c0 == 0:
                            p.op("pe", lambda e, ps=ps, g=g, ri=ri, n=n, last=last: e.matmul(ps[:, 0:n - 1], lhsT=CAb[:, g, ri, :], rhs=Hn[:, g, 30::-1, ri], start=False, stop=last),
                                 r=rk, w=[pk])
                        else:
                            hi = 1086 - c0
                            p.op("pe", lambda e, ps=ps, g=g, ri=ri, hi=hi, n=n, last=last: e.matmul(ps[:, 0:n], lhsT=CAb[:, g, ri, :], rhs=Hn[:, g, hi:hi - n:-1, ri], start=False, stop=last),
                                 r=rk, w=[pk])
                    p.op("act", lambda e, ps=ps, ybt=ybt, n=n: e.activation(out=ybt[:, 0:n], in_=ps[:, 0:n], func=AF.Identity), r=[pk], w=[yk])
                    if standalone:
                        p.dma("sp", F["y2"][g, :, c0:c0 + n], ybt[:, 0:n], r=[yk], w=[f"y2_{g}_{c0}"], sem=f"dy{ny % 2}")
                    else:
                        F["store_y"](g, c0, n, ybt, yk)
            for g in range(GP if stage >= 5 else 0):
                eng, tmp, tk = ("dve", tmpd, "tmpd") if g < 5 else ("pool", tmpp, "tmpp")
                for (b0, b1) in ((0, 9), (9, 17), (17, 25), (25, NB)):
                    nb_ = b1 - b0
                    tv = tmp[:, 0:nb_ * 32 * 2].rearrange("p (b s r) -> p b s r", b=nb_, s=32)
                    sh = [128, nb_, 32, 2]
                    t1 = TAB1[:, g, :, :].unsqueeze(1).broadcast_to(sh)
                    t2 = TAB2[:, g, :, :].unsqueeze(1).broadcast_to(sh)
                    gg = G[:, g, b0:b1, :].unsqueeze(2).broadcast_to(sh)
                    ggs = G[:, g, b0:b1, ::-1].unsqueeze(2).broadcast_to(sh)
                    tt(eng, tv, t1, gg, ALU.mult, ["TAB", "G"], [tk])
                    tt(eng, Wv[:, g, b0:b1], Wv[:, g, b0:b1], tv, ALU.add, [f"W{g}", tk], [f"W{g}"])
                    tt(eng, tv, t2, ggs, ALU.mult, ["TAB", "G"], [tk])
                    tt(eng, Hv[:, g, b0:b1], Wv[:, g, b0:b1], tv, ALU.add, [f"W{g}", tk], [f"Hn{g}"])
                if stage >= 6:
                    emit_out(g)
            if standalone:
                if stage < 6:
                    p.dma("sp", F["y2"][0, :, 0:512], yb[0][:, :], w=["y2_dummy"], sem="dy0")
                    p.wait_all("sp", ["y2_dummy"])
                p.wait_all("sp", [f"y2_{g}_{c0}" for g in range(GP) for (c0, n) in SEGS])
            _barrier(p)
        p.stack = st


def _s5_consts():
    jj = np.arange(128) // 16
    mf = (jj[None, :] >= jj[:, None]).astype(np.float32)
    mb = (jj[:, None] >= jj[None, :]).astype(np.float32)
    return np.ascontiguousarray(np.stack([mf, mb, np.eye(128, dtype=np.float32)], axis=1))


def run_s5(u_full, a_re, a_im, log_dt, b_re, b_im, c_re, c_im, d_skip, stage=9):
    nc = build_s5(stage)
    cst = _s5_consts()
    ug = u_full.reshape(NCH, 8, NG, 16)
    in_maps = []
    for k in range(NC):
        gs = slice(GP * k, GP * k + GP)
        u2 = np.ascontiguousarray(ug[:, :, gs, :].transpose(2, 1, 3, 0).reshape(GP, 128, NCH))
        par = np.empty((128, GP, 3), np.float32)
        par[:, :, 0] = a_re[:, gs, :].transpose(0, 2, 1).reshape(128, GP)
        par[:, :, 1] = a_im[:, gs, :].transpose(0, 2, 1).reshape(128, GP)
        par[:, :, 2] = np.broadcast_to(log_dt[:, None, gs], (2, 64, GP)).reshape(128, GP)
        bri = np.stack([b_re[:, gs], b_im[:, gs]], axis=-1)
        bri = np.ascontiguousarray(bri.transpose(0, 2, 1, 3, 4).reshape(128, GP, 16, 2))
        cri = np.stack([c_re[:, gs], c_im[:, gs]], axis=-1)
        cri = np.ascontiguousarray(cri.transpose(0, 3, 1, 2, 4).reshape(128, GP, 16, 2))
        dcol = np.ascontiguousarray(np.broadcast_to(d_skip.reshape(NG, 16)[gs].T[None, :, :], (8, 16, GP)).reshape(128, GP))
        in_maps.append({"u2": u2, "par": par, "bri": bri, "cri": cri, "dcol": dcol, "cst": cst})
    res = _run(nc, in_maps)
    yg = np.stack([r["y2"] for r in res])
    y = yg.reshape(NC, GP, 8, 16, NCH).transpose(4, 2, 0, 1, 3).reshape(NCH * 8, DS5)
    return y


RA = [(0, 384), (384, 768), (768, 1024)]
G4 = [[0, 1, 2, 3], [4, 5, 6, 7]]
G2P = [[0, 4], [1, 5], [2, 6], [3, 7]]


def _g2_row(r, rho):
    hi, lo = divmod(r, 4)
    for (a0, a1) in RA:
        if a0 <= rho < a1:
            la = a1 - a0
            b, lo2 = divmod(lo, 2)
            return 8 * a0 + b * 4 * la + hi * 2 * la + lo2 * la + (rho - a0)
    raise ValueError


def _cc(p, groups, src, dst, rkeys, wkeys):
    p.custom("pool", lambda e: e.collective_compute("AllGather", ALU.bypass, replica_groups=groups, ins=[src], outs=[dst]),
             "cc", 1, r=rkeys, w=wkeys)


def _exchange(p, send, G1, G2, skey, gkey, stage=0):
    for (a0, a1) in (RA if stage in (0, 1) else ()):
        _cc(p, G4, send[a0:a1, :], G1[4 * a0:4 * a1, :], [skey], [gkey + "1"])
    for (a0, a1) in (RA if stage in (0, 2) else ()):
        la = a1 - a0
        for b in range(2):
            base = 8 * a0 + b * 4 * la
            _cc(p, G2P, G1[4 * a0 + 2 * b * la:4 * a0 + 2 * (b + 1) * la, :], G2[base:base + 4 * la, :], [gkey + "1"], [gkey])


def _g1_row(lo, rho):
    for (a0, a1) in RA:
        if a0 <= rho < a1:
            return 4 * a0 + lo * (a1 - a0) + (rho - a0)
    raise ValueError


GX_ROWS = 4096 + 1024


def _exchange2(p, GX, pk, idxt, gkey):
    pkt = p.sb("pkt", [128, 4, TL], BF16)
    for lo in range(4):
        _gather(p, pkt[:, lo, :], GX[:, :], idxt[:, 9 + lo:10 + lo], [gkey + "1", "idxt"], [f"pkt{lo}"])
    p.dma("sp", pk.rearrange("(l p) n -> p l n", p=128), pkt[:], r=[f"pkt{lo}" for lo in range(4)], w=["pk" + gkey])
    _cc(p, G2P, pk[:, :], GX[4096:GX_ROWS, :], ["pk" + gkey], [gkey])


HG = ([[0, 4], [1, 5], [2, 6], [3, 7]], [[0, 1], [2, 3], [4, 5], [6, 7]], [[0, 1, 2, 3], [4, 5, 6, 7]])
HOFF = (1024, 2048, 3072, 5120)
X_ROWS = HOFF[3]


def _hyper(p, X, pk, idxt, skeys, gkey):
    pkt = p.sb("pkt", [128, 4, TL], BF16)
    have = list(skeys)
    for st_ in range(3):
        for b in range(4):
            c_ = 9 + 4 * st_ + b
            _gather(p, pkt[:, b, :], X[:, :], idxt[:, c_:c_ + 1], have + ["idxt"], [f"pkt{b}"])
        p.dma("pool", pk.rearrange("(l p) n -> p l n", p=128), pkt[:], r=[f"pkt{b}" for b in range(4)], w=["pk" + gkey], sem="pkw")
        _cc(p, HG[st_], pk[:, :], X[HOFF[st_]:HOFF[st_ + 1], :], ["pk" + gkey], [f"{gkey}{st_}"])
        have.append(f"{gkey}{st_}")
    return have


def _hyper_idx(k):
    snd = lambda d: d * 128
    A_ = lambda sender, blk: HOFF[0] + ((sender >> 2) & 1) * 512 + blk * 128
    B_ = lambda sender, blk: HOFF[1] + (sender & 1) * 512 + blk * 128
    C_ = lambda sender, blk: HOFF[2] + (sender & 3) * 512 + blk * 128
    fin = {k: snd(k), k ^ 4: A_(k ^ 4, 0), k ^ 1: B_(k ^ 1, 0), k ^ 5: B_(k ^ 1, 2),
           k ^ 2: C_(k ^ 2, 0), k ^ 6: C_(k ^ 2, 1), k ^ 3: C_(k ^ 2, 2), k ^ 7: C_(k ^ 2, 3)}
    packs = [snd(k ^ 4), snd(k ^ 5), snd(k ^ 6), snd(k ^ 7),
             snd(k ^ 1), snd(k ^ 3), A_(k ^ 4, 1), A_(k ^ 4, 3),
             snd(k ^ 2), A_(k ^ 4, 2), B_(k ^ 1, 1), B_(k ^ 1, 3)]
    idx = np.empty((128, 21), np.int32)
    pp = np.arange(128)
    for s_ in range(8):
        idx[:, s_] = fin[s_] + pp
    idx[:, 8] = k * 128 + pp
    for j, base in enumerate(packs):
        idx[:, 9 + j] = base + pp
    return idx


def _gather(p, out_ap, src2d, idx_ap, rkeys, wkeys):
    p._ng = getattr(p, "_ng", 0) + 1
    p.custom("pool", lambda e: e.indirect_dma_start(out=out_ap, out_offset=None, in_=src2d,
                                                    in_offset=bass.IndirectOffsetOnAxis(ap=idx_ap, axis=0)),
             f"ig{p._ng % 4}", 16, r=rkeys, w=wkeys)


def build_fused():
    nc = _new_nc()
    dt_in = lambda name, shape, dt=F32: nc.dram_tensor(name, list(shape), dt, kind="ExternalInput").ap()
    xin0 = dt_in("xin", [TT, D])
    cc = dt_in("cc", [128, 16, 2])
    wada = dt_in("wada", [2, 16, 128, ADA_N])
    bada = dt_in("bada", [128, 2, 6, 2])
    win = dt_in("win", [2, 16, 128, DIN])
    ident = dt_in("ident", [128, 128])
    pv = dt_in("pv", [2, 128, 8, 5])
    wglu = dt_in("wglu", [2, 8, 128, DS5])
    wout = dt_in("wout", [2, 16, 128, D])
    lnb = dt_in("lnb", [2, 128, 2, D])
    par = dt_in("par", [2, 128, GP, 3])
    bri = dt_in("bri", [2, 128, GP, 16, 2])
    cri = dt_in("cri", [2, 128, GP, 16, 2])
    dcol = dt_in("dcol", [2, 128, GP])
    cst = dt_in("cst", [128, 3, 128])
    idx = dt_in("idx", [128, 21], I32)
    selm = dt_in("selm", [128, 64, 128], BF16)
    xout = nc.dram_tensor("xo", [TL, D], F32, kind="ExternalOutput").ap()
    x1 = nc.dram_tensor("x1", [TT, D], F32).ap()
    ib = lambda name, shape: nc.dram_tensor(name, list(shape), BF16).ap()
    Xu = [ib(f"Xu{l}", [X_ROWS, TL]) for l in range(2)]
    Xy = [ib(f"Xy{l}", [X_ROWS, TL]) for l in range(2)]
    usend = [Xu[l][0:1024, :] for l in range(2)]
    uctx = [ib(f"uctx{l}", [1024, CTX]) for l in range(2)]
    G2u = Xu
    pku = [ib(f"pku{l}", [512, TL]) for l in range(2)]
    ysend = [Xy[l][0:1024, :] for l in range(2)]
    G2y = Xy
    pky = [ib(f"pky{l}", [512, TL]) for l in range(2)]
    yctx = ib("yctx", [128, CTX])
    msend = nc.dram_tensor("msend", [128, 24], F32).ap()
    Mg1 = nc.dram_tensor("Mg1", [512, 24], F32).ap()
    Mg2 = nc.dram_tensor("Mg2", [1024, 24], F32).ap()
    Gc1 = ib("Gc1", [512, CTX])
    Gc2 = ib("Gc2", [1024, CTX])

    with contextlib.ExitStack() as st:
        p = Prog(nc, st)
        idf = p.sb("idf", [128, 128], F32)
        idb = p.sb("idb", [128, 128], BF16)
        ones = p.sb("ones", [128, 128], F32)
        Mall = p.sb("Mall", [128, 8, 2, 6, 2], F32)
        mall = lambda l, m, t0, t1: Mall[:, m // 6, l, m % 6, t0:t1]
        idxt = p.sb("idxt", [128, 21], I32)
        acc = [p.ps(f"acc{i}", [128, 512]) for i in range(4)]
        tps = [p.ps(f"tp{i}", [128, 512], BF16) for i in range(2)]
        qq = [p.ps(f"qq{i}", [128, 512]) for i in range(2)]
        p.dma("sp", idf[:], ident[:, :], w=["idf"])
        p.dma("sp", idxt[:], idx[:, :], w=["idxt"])
        p.op("dve", lambda e: e.tensor_copy(out=idb[:], in_=idf[:]), r=["idf"], w=["idb"])
        p.op("dve", lambda e: e.memset(ones[:], 1.0), w=["ones"])

        w0 = contextlib.ExitStack()
        wstacks = [w0, st]
        FSs = [dict(p=p, standalone=False, stage=9, par=par[l], bri=bri[l], cri=cri[l], dcol=dcol[l], cst=cst,
                    pt=qq, psS=acc[0:4], psY=acc[2:4], ptk=["qq0", "qq1"], psSk=["acc0", "acc1", "acc2", "acc3"], psYk=["acc2", "acc3"],
                    wstack=wstacks[l]) for l in range(2)]

        for l_ in (1, 0):
            p.pfx = f"L{l_}W_"
            FSs[l_].update(st=st, phase="alloc")
            _s5_body(FSs[l_])
            p.stack = st

        def derive(l, cur, no_barrier=True):
            p.pfx = f"L{l}D_"
            FSs[l].update(st=cur, phase="derive", no_barrier=no_barrier)
            _s5_body(FSs[l])
            p.stack = cur

        with contextlib.ExitStack() as sa:
            p.stack = sa
            p.pfx = "ada_"
            cct = p.sb("cct", [128, 16, 2], F32)
            sg = p.sb("sg", [128, 16, 2], F32)
            sc = p.sb("sc", [128, 16, 2], BF16)
            bat = p.sb("bat", [128, 2, 6, 2], F32)
            Mloc = p.sb("Mloc", [128, 2, 6, 2], F32)
            wbA = [p.sb(f"wbA{i}", [128, 16, 256], BF16) for i in range(3)]
            p.dma("sp", cct[:], cc[:, :, :], w=["cct"])
            p.dma("sp", bat[:], bada[:, :, :, :], w=["bat"])
            p.op("act", lambda e: e.activation(out=sg[:], in_=cct[:], func=AF.Sigmoid), r=["cct"], w=["sg"])
            p.op("dve", lambda e: e.tensor_tensor(out=sc[:], in0=cct[:], in1=sg[:], op=ALU.mult), r=["cct", "sg"], w=["sc"])
            na = 0
            for l in range(2):
                for pair in range(3):
                    i = (l * 3 + pair) % 3
                    p.dma("pool", wbA[i][:], wada[l, :, :, pair * 256:(pair + 1) * 256].rearrange("k p n -> p k n"), w=[f"wbA{i}"], sem=f"dwa{i}")
                    for cl in range(2):
                        m = pair * 2 + cl
                        ps = acc[na % 4]
                        pk = f"acc{na % 4}"
                        na += 1
                        for kf in range(16):
                            p.op("pe", lambda e, ps=ps, i=i, kf=kf, cl=cl: e.matmul(ps[:, 0:2], lhsT=wbA[i][:, kf, cl * 128:(cl + 1) * 128], rhs=sc[:, kf, :],
                                                                                   start=(kf == 0), stop=(kf == 15)),
                                 r=[f"wbA{i}", "sc"], w=[pk])
                        p.op("dve", lambda e, ps=ps, l=l, m=m: e.tensor_tensor(out=Mloc[:, l, m, :], in0=ps[:, 0:2], in1=bat[:, l, m, :], op=ALU.add),
                             r=[pk, "bat"], w=["Mloc"])
            p.dma("pool", msend[:, :], Mloc[:].rearrange("p l m t -> p (l m t)"), r=["Mloc"], w=["msend"])
            _cc(p, G4, msend[:, :], Mg1[:, :], ["msend"], ["Mg1"])
            _cc(p, G2P, Mg1[:, :], Mg2[:, :], ["Mg1"], ["Mg2"])
            derive(0, sa)
            p.pfx = "ada_"
            p.dma("pool", Mall[:].rearrange("p c l m t -> p c (l m t)"), Mg2.rearrange("(c p) n -> p c n", p=128), r=["Mg2"], w=["Mall"])
            _barrier(p)
        p.stack = st

        for l in range(2):
            with contextlib.ExitStack() as stl:
                p.stack = stl
                p.pfx = f"L{l}_"
                mst = p.sb("mst", [128, 16, 4], F32)
                for (j, m0, w_, add1) in ((0, 0, 0, False), (1, 16, 0, True), (2, 0, 1, False), (3, 16, 1, True)):
                    m = m0
                    while m < m0 + 16:
                        c_ = m // 6
                        m1 = min(m0 + 16, 6 * (c_ + 1))
                        src = Mall[:, c_, l, m - 6 * c_:m1 - 6 * c_, w_]
                        dst = mst[:, m - m0:m1 - m0, j]
                        p.op("dve", lambda e, src=src, dst=dst, add1=add1: e.tensor_scalar(out=dst, in0=src, scalar1=(1.0 if add1 else 0.0), scalar2=None, op0=ALU.add),
                             r=["Mall"], w=["mst"])
                        m = m1
                xsrc = xin0 if l == 0 else x1
                o = p.sb("o", [128, 16, TT], BF16)
                pvt = p.sb("pvt", [128, 8, 5], F32)
                p.dma("sp", pvt[:], pv[l][:, :, :], w=["pvt"])
                FS = FSs[l]
                with contextlib.ExitStack() as sA:
                    p.stack = sA
                    p.pfx = f"L{l}A_"
                    hT = p.sb("hT", [128, 16, TT], BF16)
                    def cc_piece(a, l=l):
                        pass

                    ukeys = {}

                    FA = dict(p=p, st=sA, xin=xsrc, win=win[l], idb=idb, mst=mst, hT=hT, acc=acc, tps=tps, do_ln=True, standalone=False,
                              usend=usend[l], uctx=uctx[l], cc_piece=cc_piece)
                    _tok_body("A", True, FA)
                    p.pfx = f"L{l}C1_"
                    FC1 = dict(p=p, st=sA, xin=xsrc, win=win[l], idb=idb, mst=mst, hT=hT, acc=acc, tps=tps, do_ln=False, standalone=False,
                               part=1, o=o, pvt=pvt, odbg=None, hwcast=True,
                               mid_hook=(lambda l=l: ukeys.update(k=_hyper(p, Xu[l], pku[l], idxt, [f"usend{c_}" for c_ in range(8)], "G2u"))))
                    _tok_body("C", l == 0, FC1)
                    _barrier(p)
                p.stack = stl

                def load_U(U, l=l):
                    Ufm = p.sb("Ufm", [128, 8, TL], BF16)
                    Ucx = p.sb("Ucx", [128, CTX], BF16)
                    for r in range(8):
                        _gather(p, Ufm[:, r, :], G2u[l][:, :], idxt[:, r:r + 1], ukeys["k"] + ["idxt"], [f"Ufm{r}"])
                    _gather(p, Ucx[:, :], uctx[l][:, :], idxt[:, 8:9], ["uctx", "idxt"], ["Ucx"])
                    rk = [f"Ufm{r}" for r in range(8)] + ["Ucx"]
                    n = 0
                    for g in range(GP):
                        for j in range(8):
                            q = "sp" if n % 2 == 0 else "act"
                            n += 1
                            p.dma(q, U[16 * j:16 * j + 16, g, 32:NCH].rearrange("p (r c) -> p r c", c=128), Ufm[16 * g:16 * g + 16, :, j * 128:(j + 1) * 128],
                                  r=rk, w=[f"Urp{g}_{j}"], sem=f"rpk_{q}")
                            p.dma(q, U[16 * j:16 * j + 16, g, 0:32], Ucx[16 * g:16 * g + 16, j * 32:(j + 1) * 32],
                                  r=rk, w=[f"Urc{g}_{j}"], sem=f"rpk_{q}")
                    tok = {s_: p.cnt[s_] for s_ in ("rpk_sp", "rpk_act")}
                    for g in range(GP):
                        p.lastw[f"U{g}"] = dict(tok)
                        p.readers[f"U{g}"] = {}

                yn = {"n": 0}

                def store_y(g, c0, n, ybt, yk, l=l):
                    yn["n"] += 1
                    sem = f"dys{yn['n'] % 2}"
                    if c0 == 0:
                        if l == 0:
                            p.dma("sp", yctx[:, g * 32:(g + 1) * 32], ybt[:, 0:32], r=[yk], w=["yctx"], sem=sem)
                        return
                    r0 = (c0 - 32) // 128
                    dst = ysend[l].rearrange("(r p) (g c) -> p r g c", p=128, g=GP)[:, r0:r0 + 4, g, :]
                    p.dma("sp", dst, ybt[:, 0:512].rearrange("p (r c) -> p r c", c=128), r=[yk], w=["ysend"], sem=sem)

                p.pfx = f"L{l}S_"
                with contextlib.ExitStack() as sS:
                    p.stack = sS
                    FS.update(st=sS, phase="main", load_U=load_U, store_y=store_y)
                    _s5_body(FS)
                    _barrier(p)
                p.stack = stl
                ykeys = {}
                p.pfx = f"L{l}X_"
                with contextlib.ExitStack() as sX:
                    p.stack = sX
                    ykeys["k"] = _hyper(p, Xy[l], pky[l], idxt, ["ysend"], "G2y")
                    if l == 0:
                        _cc(p, G4, yctx[:, :], Gc1[:, :], ["yctx"], ["Gc1"])
                        _cc(p, G2P, Gc1[:, :], Gc2[:, :], ["Gc1"], ["Gc2"])
                        derive(1, sX, no_barrier=True)
                    _barrier(p)
                p.stack = stl

                def load_gel(gel, l=l):
                    prev = p.stack
                    with contextlib.ExitStack() as ssel:
                        p.stack = ssel
                        Sel = p.sb("Sel", [128, 64, 128], BF16)
                        p.dma("sp", Sel[:], selm[:, :, :], w=["Sel"])
                        nq = 0
                        for hh in range(4):
                            with contextlib.ExitStack() as sg_:
                                p.stack = sg_
                                k0 = 2 * hh
                                Yr = p.sb(f"Yr{hh}", [128, 2, TL], BF16)
                                for k_ in range(2):
                                    _gather(p, Yr[:, k_, :], G2y[l][:, :], idxt[:, k0 + k_:k0 + k_ + 1], ykeys["k"] + ["idxt"], [f"Yr{k_}"])
                                if l == 0:
                                    Yc = p.sb(f"Yc{hh}", [128, 2, CTX], BF16)
                                    p.dma("sp", Yc[:], Gc2.rearrange("(k p) n -> p k n", p=128)[:, k0:k0 + 2, :], r=["Gc2"], w=["Yc"])
                                for k_ in range(2):
                                    ch = k0 + k_
                                    for i4 in range(2):
                                        ps = acc[nq % 2]
                                        pk = f"acc{nq % 2}"
                                        psc = acc[2 + nq % 2]
                                        pkc = f"acc{2 + nq % 2}"
                                        eng = "act" if nq % 2 == 0 else "dve"
                                        nq += 1
                                        for ii in range(4):
                                            i = i4 * 4 + ii
                                            for g in range(8):
                                                p.op("pe", lambda e, ps=ps, i=i, ii=ii, g=g, k_=k_, Yr=Yr: e.matmul(
                                                    ps[:, ii * 128:(ii + 1) * 128], lhsT=Sel[:, i * 8 + g, :], rhs=Yr[:, k_, g * 128:(g + 1) * 128],
                                                    start=(g == 0), stop=(g == 7)),
                                                    r=["Sel", f"Yr{k_}"], w=[pk])
                                            if l == 0:
                                                for g in range(8):
                                                    p.op("pe", lambda e, psc=psc, i=i, ii=ii, g=g, k_=k_, Yc=Yc: e.matmul(
                                                        psc[:, ii * 32:(ii + 1) * 32], lhsT=Sel[:, i * 8 + g, :], rhs=Yc[:, k_, g * 32:(g + 1) * 32],
                                                        start=(g == 0), stop=(g == 7)),
                                                        r=["Sel", "Yc"], w=[pkc])
                                        dl = gel[:, ch, 0:TL].rearrange("p (c i) -> p i c", i=8)[:, i4 * 4:i4 * 4 + 4, :]
                                        dc_ = gel[:, ch, TL:TT].rearrange("p (c i) -> p i c", i=8)[:, i4 * 4:i4 * 4 + 4, :]
                                        sl = ps[:, 0:512].rearrange("p (i c) -> p i c", c=128)
                                        sc_ = psc[:, 0:128].rearrange("p (i c) -> p i c", c=32)
                                        if eng == "act":
                                            p.op("act", lambda e, sl=sl, dl=dl: e.activation(out=dl, in_=sl, func=AF.Identity), r=[pk], w=[f"gel{ch}"])
                                            if l == 0:
                                                p.op("dve", lambda e, sc_=sc_, dc_=dc_: e.tensor_copy(out=dc_, in_=sc_), r=[pkc], w=[f"gel{ch}"])
                                        else:
                                            p.op("dve", lambda e, sl=sl, dl=dl: e.tensor_copy(out=dl, in_=sl), r=[pk], w=[f"gel{ch}"])
                                            if l == 0:
                                                p.op("act", lambda e, sc_=sc_, dc_=dc_: e.activation(out=dc_, in_=sc_, func=AF.Identity), r=[pkc], w=[f"gel{ch}"])
                                _barrier(p)
                            p.stack = ssel
                    p.stack = prev

                def make_gb(gb, l=l):
                    with contextlib.ExitStack() as sg_:
                        prev = p.stack
                        p.stack = sg_
                        dg = [p.sb(f"dg{i}", [128, 128], F32) for i in range(2)]
                        nd = 0
                        for w_ in range(2):
                            for nb in range(4):
                                ps = qq[nb % 2]
                                pk = f"qq{nb % 2}"
                                for j in range(4):
                                    kf = nb * 4 + j
                                    d = dg[nd % 2]
                                    dk = f"dg{nd % 2}"
                                    nd += 1
                                    p.op("dve", lambda e, d=d, kf=kf, w_=w_: e.tensor_scalar(out=d[:], in0=idf[:], scalar1=mall(l, 32 + kf, w_, w_ + 1), scalar2=None, op0=ALU.mult),
                                         r=["idf", "Mall"], w=[dk])
                                    p.op("pe", lambda e, ps=ps, d=d, j=j: e.matmul(ps[:, j * 128:(j + 1) * 128], lhsT=ones[:], rhs=d[:], start=True, stop=True),
                                         r=["ones", dk], w=[pk])
                                p.op("act", lambda e, ps=ps, w_=w_, nb=nb: e.activation(out=gb[:, w_, nb * 512:(nb + 1) * 512], in_=ps[:, :], func=AF.Identity),
                                     r=[pk], w=["gb"])
                        _barrier(p)
                    p.stack = prev

                p.pfx = f"L{l}C_"
                with contextlib.ExitStack() as sC:
                    p.stack = sC
                    FC = dict(p=p, st=sC, xin=xsrc, win=win[l], idb=idb, mst=mst, hT=None, acc=acc, tps=tps, do_ln=False, standalone=False,
                              part=2, o=o, pvt=pvt,
                              pv=pv[l], wglu=wglu[l], wout=wout[l], lnb=lnb[l], xo=(x1 if l == 0 else xout), load_gel=load_gel, make_gb=make_gb, odbg=None)
                    _tok_body("C", l == 0, FC)
                    _barrier(p)
                p.stack = stl
            p.stack = st
            if l == 0:
                w0.close()
        p.emit()
    return nc


def _prep_fused_inputs(x, c, ctx, c_ctx, w_ada, b_ada, w_in, s5_a_re, s5_a_im, s5_log_dt, s5_b_re, s5_b_im,
                       s5_c_re, s5_c_im, s5_d, w_glu, b_glu, conv_w, conv_b, w_out, ln_g, ln_b):
    cc = np.stack([c.reshape(D), c_ctx.reshape(D)], axis=-1)
    cc = np.ascontiguousarray(cc.reshape(16, 128, 2).transpose(1, 0, 2))
    wada_all = w_ada.reshape(2, 16, 128, NC, ADA_N)
    bada_all = np.broadcast_to(b_ada.reshape(2, NC, 6, 128).transpose(3, 1, 0, 2)[:, :, :, :, None], (128, NC, 2, 6, 2))
    win = w_in.reshape(2, 16, 128, DIN)
    pv = np.empty((2, 128, 8, 5), np.float32)
    for l in range(2):
        pv[l, :, :, 0] = b_glu[l].reshape(8, 128).T
        for j in range(3):
            pv[l, :, :, 1 + j] = conv_w[l, j].reshape(8, 128).T
        pv[l, :, :, 4] = conv_b[l].reshape(8, 128).T
    wglu = w_glu.reshape(2, 8, 128, DS5)
    wout = w_out.reshape(2, 16, 128, D)
    lnb = np.ascontiguousarray(np.broadcast_to(np.stack([ln_g, ln_b], axis=1)[:, None, :, :], (2, 128, 2, D)))
    cst = _s5_consts()
    selm = np.zeros((128, 64, 128), np.float32)
    for i_ in range(8):
        for g_ in range(8):
            for h_ in range(16):
                selm[16 * i_ + h_, i_ * 8 + g_, 16 * g_ + h_] = 1.0
    selm = selm.astype(ml_dtypes.bfloat16)
    in_maps = []
    for k in range(NC):
        gs = slice(GP * k, GP * k + GP)
        par = np.empty((2, 128, GP, 3), np.float32)
        bri = np.empty((2, 128, GP, 16, 2), np.float32)
        cri = np.empty((2, 128, GP, 16, 2), np.float32)
        dcol = np.empty((2, 128, GP), np.float32)
        for l in range(2):
            par[l, :, :, 0] = s5_a_re[l][:, gs, :].transpose(0, 2, 1).reshape(128, GP)
            par[l, :, :, 1] = s5_a_im[l][:, gs, :].transpose(0, 2, 1).reshape(128, GP)
            par[l, :, :, 2] = np.broadcast_to(s5_log_dt[l][:, None, gs], (2, 64, GP)).reshape(128, GP)
            bri[l] = np.stack([s5_b_re[l][:, gs], s5_b_im[l][:, gs]], axis=-1).transpose(0, 2, 1, 3, 4).reshape(128, GP, 16, 2)
            cri[l] = np.stack([s5_c_re[l][:, gs], s5_c_im[l][:, gs]], axis=-1).transpose(0, 3, 1, 2, 4).reshape(128, GP, 16, 2)
            dcol[l] = np.broadcast_to(s5_d[l].reshape(NG, 16)[gs].T[None, :, :], (8, 16, GP)).reshape(128, GP)
        idx = _hyper_idx(k)
        xin = np.ascontiguousarray(np.concatenate([x[0, k * TL:(k + 1) * TL], ctx[0]], axis=0))
        wada = np.ascontiguousarray(wada_all[:, :, :, k, :])
        bada = np.ascontiguousarray(bada_all[:, k])
        in_maps.append({"xin": xin, "cc": cc, "wada": wada, "bada": bada, "win": win, "ident": _IDENT, "pv": pv, "wglu": wglu,
                        "wout": wout, "lnb": lnb, "par": par, "bri": bri, "cri": cri, "dcol": dcol, "cst": cst, "idx": idx, "selm": selm})
    return in_maps


def kernel_unfused(x, c, ctx, c_ctx, w_ada, b_ada, w_in, s5_a_re, s5_a_im, s5_log_dt, s5_b_re, s5_b_im,
                   s5_c_re, s5_c_im, s5_d, w_glu, b_glu, conv_w, conv_b, w_out, ln_g, ln_b):
    return _kernel_unfused_impl(x, c, ctx, c_ctx, w_ada, b_ada, w_in, s5_a_re, s5_a_im, s5_log_dt, s5_b_re, s5_b_im,
                                s5_c_re, s5_c_im, s5_d, w_glu, b_glu, conv_w, conv_b, w_out, ln_g, ln_b)


def kernel(x, c, ctx, c_ctx, w_ada, b_ada, w_in, s5_a_re, s5_a_im, s5_log_dt, s5_b_re, s5_b_im,
           s5_c_re, s5_c_im, s5_d, w_glu, b_glu, conv_w, conv_b, w_out, ln_g, ln_b):
    f = lambda a: np.ascontiguousarray(np.asarray(a, dtype=np.float32))
    args = [f(a) for a in (x, c, ctx, c_ctx, w_ada, b_ada, w_in, s5_a_re, s5_a_im, s5_log_dt, s5_b_re, s5_b_im,
                           s5_c_re, s5_c_im, s5_d, w_glu, b_glu, conv_w, conv_b, w_out, ln_g, ln_b)]
    in_maps = _prep_fused_inputs(*args)
    nc = build_fused()
    res = _run(nc, in_maps)
    out = np.concatenate([r["xo"] for r in res], axis=0)
    return np.ascontiguousarray(out[None].astype(np.float32))


def _kernel_unfused_impl(x, c, ctx, c_ctx, w_ada, b_ada, w_in, s5_a_re, s5_a_im, s5_log_dt, s5_b_re, s5_b_im,
           s5_c_re, s5_c_im, s5_d, w_glu, b_glu, conv_w, conv_b, w_out, ln_g, ln_b):
    f = lambda a: np.ascontiguousarray(np.asarray(a, dtype=np.float32))
    x, c, ctx, c_ctx, w_ada, b_ada, w_in = map(f, (x, c, ctx, c_ctx, w_ada, b_ada, w_in))
    s5_a_re, s5_a_im, s5_log_dt, s5_b_re, s5_b_im, s5_c_re, s5_c_im, s5_d = map(
        f, (s5_a_re, s5_a_im, s5_log_dt, s5_b_re, s5_b_im, s5_c_re, s5_c_im, s5_d))
    w_glu, b_glu, conv_w, conv_b, w_out, ln_g, ln_b = map(f, (w_glu, b_glu, conv_w, conv_b, w_out, ln_g, ln_b))
    mod = run_ada(c, c_ctx, w_ada, b_ada)
    xl = x[0]
    cx = ctx[0]
    for l in range(2):
        xs = [np.ascontiguousarray(np.concatenate([xl[k * TL:(k + 1) * TL], cx], axis=0)) for k in range(NC)]
        us = run_tok_A(xs, mod[l], w_in[l])
        u_lat = np.concatenate([u.reshape(DS5, TT)[:, :TL].T for u in us], axis=0)
        u_ctx = us[0].reshape(DS5, TT)[:, TL:].T
        u_full = np.ascontiguousarray(np.concatenate([u_ctx, u_lat], axis=0))
        y = run_s5(u_full, s5_a_re[l], s5_a_im[l], s5_log_dt[l], s5_b_re[l], s5_b_im[l], s5_c_re[l], s5_c_im[l], s5_d[l])
        ys = []
        for k in range(NC):
            yk = np.concatenate([y[CTX + k * TL:CTX + (k + 1) * TL], y[:CTX]], axis=0)
            ys.append(np.ascontiguousarray(yk.T.reshape(8, 128, TT)))
        outs = run_tok_C(xs, ys, mod[l], w_in[l], w_glu[l], b_glu[l], conv_w[l], conv_b[l], w_out[l], ln_g[l], ln_b[l],
                         with_ctx=(l == 0))
        xl = np.concatenate([o[:TL] for o in outs], axis=0)
        if l == 0:
            cx = np.ascontiguousarray(outs[0][TL:])
    return np.ascontiguousarray(xl[None].astype(np.float32))
```
